# Optimizing a Trainium2 kernel written in Bass

```python
import jax, jax.numpy as jnp
from jax import lax
import numpy as np

D_MODEL = 1024
BATCH = 4
SEQ = 4096
DEPTH = 2
DEC_BATCH = 16
DEC_SEQ = 2048
PAST_LEN = 128

GRID_W = 64
D_FF = 2816
N_MOD = 9
LRU_W = 384
LRU_BLOCKS = 6
LRU_BS = LRU_W // LRU_BLOCKS
LRU_C = 8.0
CONV_W = 4
RWKV_HEADS = 4
RWKV_HD = 64
RWKV_W = RWKV_HEADS * RWKV_HD
W_LORA = 64
A_LORA = 64
G_LORA = 128
RWKV_IN = 3 * RWKV_W + W_LORA + A_LORA + G_LORA
ATT_HEADS = 6
ATT_KV = 2
ATT_G = ATT_HEADS // ATT_KV
ATT_HD = 64
ATT_Q = ATT_HEADS * ATT_HD
ATT_KVW = ATT_KV * ATT_HD
Q_BLOCK = 128
ROPE_THETA = 10000.0
ROPE_PAIRS = ATT_HD // 4
D_MIX = LRU_W + RWKV_W + ATT_Q
D_IN = 2 * LRU_W + RWKV_IN + ATT_Q + 2 * ATT_KVW
NORM_EPS = 1e-6
GN_EPS = 64e-5

kernel_name = "hybrid_bidir_hymba_encoder"


def rmsnorm(x, g):
    xf = x.astype(jnp.float32)
    y = xf * lax.rsqrt(jnp.mean(xf * xf, axis=-1, keepdims=True) + NORM_EPS)
    return (y * g.astype(jnp.float32)).astype(x.dtype)


def head_rms(x, g):
    xf = x.astype(jnp.float32)
    return xf * lax.rsqrt(jnp.mean(xf * xf, axis=-1, keepdims=True) + NORM_EPS) * g.astype(jnp.float32)


def swiglu(h, w_in, w_out):
    gate, up = jnp.split(h @ w_in, 2, axis=-1)
    return (jax.nn.silu(gate) * up) @ w_out


def rope_tables(seq):
    n_rows = seq // GRID_W
    row = jnp.repeat(jnp.arange(n_rows, dtype=jnp.float32), GRID_W)
    col = jnp.tile(jnp.arange(GRID_W, dtype=jnp.float32), n_rows)
    inv = ROPE_THETA ** (-jnp.arange(ROPE_PAIRS, dtype=jnp.float32) / ROPE_PAIRS)
    ang = jnp.stack([row[:, None] * inv, col[:, None] * inv], axis=1)
    return jnp.cos(ang), jnp.sin(ang)


def apply_rope2d(x, cos, sin):
    b, s, h, _ = x.shape
    xr = x.reshape(b, s, h, 2, 2, ROPE_PAIRS)
    x1, x2 = xr[..., 0, :], xr[..., 1, :]
    c = cos[None, :, None]
    sn = sin[None, :, None]
    out = jnp.stack([x1 * c - x2 * sn, x2 * c + x1 * sn], axis=-2)
    return out.reshape(b, s, h, ATT_HD)


def lru_combine(e1, e2):
    a1, b1 = e1
    a2, b2 = e2
    return a1 * a2, a2 * b1 + b2


def rglru_mixer(xb, yb, lp):
    b, s, _ = xb.shape
    xf = xb.astype(jnp.float32)
    left = CONV_W // 2
    xp = jnp.pad(xf, ((0, 0), (left, CONV_W - 1 - left), (0, 0)))
    w = lp["lru_conv_w"].astype(jnp.float32)
    xc = sum(xp[:, j:j + s] * w[j] for j in range(CONV_W)) + lp["lru_conv_b"].astype(jnp.float32)
    xblk = xc.reshape(b, s, LRU_BLOCKS, LRU_BS)
    h = jnp.zeros_like(xc)
    for d, rev in ((0, False), (1, True)):
        r = jax.nn.sigmoid(jnp.einsum("bsni,nij->bsnj", xblk, lp["lru_w_gate_a"][d].astype(jnp.float32)).reshape(b, s, LRU_W) + lp["lru_b_gate_a"][d])
        i = jax.nn.sigmoid(jnp.einsum("bsni,nij->bsnj", xblk, lp["lru_w_gate_x"][d].astype(jnp.float32)).reshape(b, s, LRU_W) + lp["lru_b_gate_x"][d])
        log_a = -LRU_C * r * jax.nn.softplus(-lp["lru_lambda"][d].astype(jnp.float32))
        a = jnp.exp(log_a)
        u = jnp.sqrt(-jnp.expm1(2.0 * log_a)) * (i * xc)
        _, hd = lax.associative_scan(lru_combine, (a, u), reverse=rev, axis=1)
        h = h + hd
    return h * jax.nn.gelu(yb.astype(jnp.float32))


def rwkv_scan(r, w, k, v, kk, bb, reverse):
    b, _, h, n = r.shape
    xs = tuple(jnp.moveaxis(t, 1, 0) for t in (r, w, k, v, kk, bb))

    def step(st, inp):
        r_t, w_t, k_t, v_t, kk_t, b_t = inp
        sa = jnp.einsum("bhvk,bhk->bhv", st, -kk_t)
        st = st * w_t[:, :, None, :] + sa[..., None] * b_t[:, :, None, :] + v_t[..., None] * k_t[:, :, None, :]
        y = jnp.einsum("bhvk,bhk->bhv", st, r_t)
        return st, y

    st0 = jnp.zeros((b, h, n, n), jnp.float32)
    _, y = lax.scan(step, st0, xs, reverse=reverse)
    return jnp.moveaxis(y, 0, 1)


def rwkv_mixer(zr, lp):
    b, s, _ = zr.shape
    f = zr.astype(jnp.float32)
    prev = jnp.pad(f[:, :-1], ((0, 0), (1, 0), (0, 0)))
    nxt = jnp.pad(f[:, 1:], ((0, 0), (0, 1), (0, 0)))
    f = f + lp["rwkv_mu"].astype(jnp.float32) * (0.5 * (prev + nxt) - f)
    r, k, v, xw, xa, xg = jnp.split(f, [RWKV_W, 2 * RWKV_W, 3 * RWKV_W, 3 * RWKV_W + W_LORA, 3 * RWKV_W + W_LORA + A_LORA], axis=-1)

    def hd(t):
        return t.reshape(b, s, RWKV_HEADS, RWKV_HD)

    g = jax.nn.sigmoid(xg) @ lp["rwkv_g_up"].astype(jnp.float32)
    kk = hd(k * lp["rwkv_k_k"].astype(jnp.float32))
    kk = kk / jnp.maximum(jnp.sqrt(jnp.sum(kk * kk, axis=-1, keepdims=True)), 1e-12)
    a_lora = xa @ lp["rwkv_a_up"].astype(jnp.float32)
    w_lora = jnp.tanh(xw)
    k_a = lp["rwkv_k_a"].astype(jnp.float32)
    y = jnp.zeros((b, s, RWKV_HEADS, RWKV_HD), jnp.float32)
    for d, rev in ((0, False), (1, True)):
        u = lp["rwkv_w0"][d].astype(jnp.float32) + w_lora @ lp["rwkv_w_up"][d].astype(jnp.float32)
        w = jnp.exp(-jnp.exp(-jax.nn.softplus(-u) - 0.5))
        a = jax.nn.sigmoid(lp["rwkv_a0"][d].astype(jnp.float32) + a_lora)
        kd = k * (1.0 + (a - 1.0) * k_a)
        y = y + rwkv_scan(hd(r), hd(w), hd(kd), hd(v), kk, kk * hd(a), rev)
    mu = jnp.mean(y, axis=-1, keepdims=True)
    var = jnp.mean(jnp.square(y - mu), axis=-1, keepdims=True)
    yn = ((y - mu) * lax.rsqrt(var + GN_EPS)).reshape(b, s, RWKV_W)
    yn = yn * lp["rwkv_ln_g"].astype(jnp.float32) + lp["rwkv_ln_b"].astype(jnp.float32)
    bonus = (jnp.sum(hd(r) * hd(k) * lp["rwkv_r_k"].astype(jnp.float32), axis=-1, keepdims=True) * hd(v)).reshape(b, s, RWKV_W)
    return (yn + bonus) * g


def attention(q, k, v, q_g, k_g, rope):
    b, s, _ = q.shape
    cos, sin = rope
    q = apply_rope2d(head_rms(q.reshape(b, s, ATT_HEADS, ATT_HD), q_g), cos, sin)
    k = apply_rope2d(head_rms(k.reshape(b, s, ATT_KV, ATT_HD), k_g), cos, sin)
    v = v.astype(jnp.float32).reshape(b, s, ATT_KV, ATT_HD)
    nblk = s // Q_BLOCK
    qb = jnp.moveaxis(q.reshape(b, nblk, Q_BLOCK, ATT_KV, ATT_G, ATT_HD), 1, 0)
    scale = ATT_HD ** -0.5

    def block(qi):
        sc = jnp.einsum("bqkgd,bskd->bkgqs", qi, k) * scale
        p = jax.nn.softmax(sc, axis=-1)
        return jnp.einsum("bkgqs,bskd->bqkgd", p, v)

    o = lax.map(block, qb)
    return jnp.moveaxis(o, 0, 1).reshape(b, s, ATT_Q)


def mixer(h, lp, rope):
    z = h @ lp["w_mix_in"]
    o1 = 2 * LRU_W + RWKV_IN
    xb, yb, zr, q, k, v = jnp.split(z, [LRU_W, 2 * LRU_W, o1, o1 + ATT_Q, o1 + ATT_Q + ATT_KVW], axis=-1)
    o_lru = rglru_mixer(xb, yb, lp)
    o_rwkv = rwkv_mixer(zr, lp)
    o_att = attention(q, k, v, lp["attn_q_norm"], lp["attn_k_norm"], rope)
    o = jnp.concatenate([o_lru, o_rwkv, o_att], axis=-1).astype(h.dtype)
    return o @ lp["w_mix_out"]


def layer(x, c, lp, rope):
    mod = (jax.nn.silu(c) @ lp["w_ada"] + lp["b_ada"])[:, None, :]
    sh1, sc1, g1, sh2, sc2, g2, sh3, sc3, g3 = jnp.split(mod, N_MOD, axis=-1)
    h = rmsnorm(x, lp["norm_g"][0]) * (1 + sc1) + sh1
    x = x + 0.5 * g1 * swiglu(h, lp["ffn_w_in"][0], lp["ffn_w_out"][0])
    h = rmsnorm(x, lp["norm_g"][1]) * (1 + sc2) + sh2
    x = x + g2 * mixer(h, lp, rope)
    h = rmsnorm(x, lp["norm_g"][2]) * (1 + sc3) + sh3
    x = x + 0.5 * g3 * swiglu(h, lp["ffn_w_in"][1], lp["ffn_w_out"][1])
    return x


def trunk(x, c, params):
    rope = rope_tables(x.shape[1])
    for l in range(DEPTH):
        lp = {name: arr[l] for name, arr in params.items()}
        x = layer(x, c, lp, rope)
    return x


def setup_inputs(seed: int = 0) -> dict:
    key = jax.random.key(seed)
    ks = jax.random.split(key, 32)
    f32 = jnp.float32

    def nrm(k, shape, s):
        return jax.random.normal(k, shape, f32) * s

    u = jax.random.uniform(ks[13], (DEPTH, 2, LRU_W), f32, minval=0.9, maxval=0.999)
    a_lru = u ** (1.0 / LRU_C)
    return {
        "x_prompt": nrm(ks[0], (BATCH, SEQ, D_MODEL), 1.0),
        "x_sample": nrm(ks[1], (DEC_BATCH, DEC_SEQ, D_MODEL), 1.0),
        "c_prompt": nrm(ks[2], (BATCH, D_MODEL), 1.0),
        "c_sample": nrm(ks[3], (DEC_BATCH, D_MODEL), 1.0),
        "w_ada": nrm(ks[4], (DEPTH, D_MODEL, N_MOD * D_MODEL), 0.5 * D_MODEL ** -0.5),
        "b_ada": nrm(ks[5], (DEPTH, N_MOD * D_MODEL), 0.02),
        "norm_g": 1.0 + nrm(ks[6], (DEPTH, 3, D_MODEL), 0.02),
        "ffn_w_in": nrm(ks[7], (DEPTH, 2, D_MODEL, 2 * D_FF), D_MODEL ** -0.5),
        "ffn_w_out": nrm(ks[8], (DEPTH, 2, D_FF, D_MODEL), D_FF ** -0.5),
        "w_mix_in": nrm(ks[9], (DEPTH, D_MODEL, D_IN), D_MODEL ** -0.5),
        "w_mix_out": nrm(ks[10], (DEPTH, D_MIX, D_MODEL), D_MIX ** -0.5),
        "lru_conv_w": nrm(ks[11], (DEPTH, CONV_W, LRU_W), CONV_W ** -0.5),
        "lru_conv_b": nrm(ks[12], (DEPTH, LRU_W), 0.02),
        "lru_w_gate_a": nrm(ks[14], (DEPTH, 2, LRU_BLOCKS, LRU_BS, LRU_BS), LRU_BS ** -0.5),
        "lru_b_gate_a": nrm(ks[15], (DEPTH, 2, LRU_W), 0.02),
        "lru_w_gate_x": nrm(ks[16], (DEPTH, 2, LRU_BLOCKS, LRU_BS, LRU_BS), LRU_BS ** -0.5),
        "lru_b_gate_x": nrm(ks[17], (DEPTH, 2, LRU_W), 0.02),
        "lru_lambda": jnp.log(a_lru) - jnp.log1p(-a_lru),
        "rwkv_mu": jax.random.uniform(ks[18], (DEPTH, RWKV_IN), f32),
        "rwkv_w_up": nrm(ks[19], (DEPTH, 2, W_LORA, RWKV_W), 0.1),
        "rwkv_w0": jax.random.uniform(ks[20], (DEPTH, 2, RWKV_W), f32, minval=-6.0, maxval=1.0),
        "rwkv_a_up": nrm(ks[21], (DEPTH, A_LORA, RWKV_W), 0.1),
        "rwkv_a0": nrm(ks[22], (DEPTH, 2, RWKV_W), 0.5),
        "rwkv_g_up": nrm(ks[23], (DEPTH, G_LORA, RWKV_W), G_LORA ** -0.5),
        "rwkv_k_k": 0.85 + nrm(ks[24], (DEPTH, RWKV_W), 0.05),
        "rwkv_k_a": 1.0 + nrm(ks[25], (DEPTH, RWKV_W), 0.05),
        "rwkv_r_k": nrm(ks[26], (DEPTH, RWKV_HEADS, RWKV_HD), 0.1),
        "rwkv_ln_g": 1.0 + nrm(ks[27], (DEPTH, RWKV_W), 0.02),
        "rwkv_ln_b": nrm(ks[28], (DEPTH, RWKV_W), 0.02),
        "attn_q_norm": 1.0 + nrm(ks[29], (DEPTH, ATT_HD), 0.02),
        "attn_k_norm": 1.0 + nrm(ks[30], (DEPTH, ATT_HD), 0.02),
    }


def reference(x_prompt, x_sample, c_prompt, c_sample, w_ada, b_ada, norm_g, ffn_w_in, ffn_w_out, w_mix_in, w_mix_out, lru_conv_w, lru_conv_b, lru_w_gate_a, lru_b_gate_a, lru_w_gate_x, lru_b_gate_x, lru_lambda, rwkv_mu, rwkv_w_up, rwkv_w0, rwkv_a_up, rwkv_a0, rwkv_g_up, rwkv_k_k, rwkv_k_a, rwkv_r_k, rwkv_ln_g, rwkv_ln_b, attn_q_norm, attn_k_norm):
    params = {
        "w_ada": w_ada, "b_ada": b_ada, "norm_g": norm_g,
        "ffn_w_in": ffn_w_in, "ffn_w_out": ffn_w_out,
        "w_mix_in": w_mix_in, "w_mix_out": w_mix_out,
        "lru_conv_w": lru_conv_w, "lru_conv_b": lru_conv_b,
        "lru_w_gate_a": lru_w_gate_a, "lru_b_gate_a": lru_b_gate_a,
        "lru_w_gate_x": lru_w_gate_x, "lru_b_gate_x": lru_b_gate_x, "lru_lambda": lru_lambda,
        "rwkv_mu": rwkv_mu, "rwkv_w_up": rwkv_w_up, "rwkv_w0": rwkv_w0,
        "rwkv_a_up": rwkv_a_up, "rwkv_a0": rwkv_a0, "rwkv_g_up": rwkv_g_up,
        "rwkv_k_k": rwkv_k_k, "rwkv_k_a": rwkv_k_a, "rwkv_r_k": rwkv_r_k,
        "rwkv_ln_g": rwkv_ln_g, "rwkv_ln_b": rwkv_ln_b,
        "attn_q_norm": attn_q_norm, "attn_k_norm": attn_k_norm,
    }
    y_prompt = trunk(x_prompt, c_prompt, params)
    y_sample = trunk(x_sample, c_sample, params)
    return (y_prompt, y_sample)
```

```python
import contextlib
import numpy as np
import concourse.bass as bass
import concourse.mybir as mybir
from concourse.bass_utils import run_bass_kernel_spmd

F32 = mybir.dt.float32
BF16 = mybir.dt.bfloat16
AF = mybir.ActivationFunctionType
ALU = mybir.AluOpType

NCORES = 8
D = 1024
DFF = 2816
DIN = 2432
SEG = 2048
NSEG = 3
NT = NSEG * SEG
TT = 1024
NTILE = NT // TT
KC = D // 128
FC = DFF // 128
NZC = 20
DEPTH = 2
EPS = 1e-6


class Dep:
    __slots__ = ("w", "r")

    def __init__(self):
        self.w = None
        self.r = {}


class Eng:
    def __init__(self, fw, name, eng, is_pe=False):
        self.name = name
        self.e = eng
        self.is_pe = is_pe
        self.sem = fw.new_sem("s_" + name)
        self.cnt = 0
        self.known = {}


class FW:
    DMA_RING = 8

    def __init__(self, nc, stack):
        self.nc = nc
        self.stack = stack
        self.nsem = 0
        self.semobj = {}
        self.pe = Eng(self, "pe", nc.tensor, True)
        self.dve = Eng(self, "dve", nc.vector)
        self.act = Eng(self, "act", nc.scalar)
        self.pool = Eng(self, "pool", nc.gpsimd)
        self.sp = Eng(self, "sp", nc.sync)
        self.engs = [self.pe, self.dve, self.act, self.pool, self.sp]
        self.dmaq = {}
        self.ntens = 0
        self.ninst = 0

    def new_sem(self, name):
        s = self.stack.enter_context(self.nc.semaphore(name))
        self.nsem += 1
        self.semobj[id(s)] = s
        return s

    def sb(self, shape, dt=F32, stack=None):
        self.ntens += 1
        return (stack or self.stack).enter_context(self.nc.sbuf_tensor(f"t{self.ntens}", list(shape), dt))

    def ps(self, shape, dt=F32, stack=None):
        self.ntens += 1
        return (stack or self.stack).enter_context(self.nc.psum_tensor(f"p{self.ntens}", list(shape), dt))

    def _need(self, E, ev):
        if ev is None:
            return
        key, val = ev
        if key is E.sem and E.is_pe:
            return
        if E.known.get(id(key), 0) >= val:
            return
        E.e.wait_ge(key, val)
        self.ninst += 1
        E.known[id(key)] = val

    def _pre(self, E, reads, writes):
        for d in reads:
            self._need(E, d.w)
        for d in writes:
            self._need(E, d.w)
            for k, v in list(d.r.items()):
                self._need(E, (self.semobj[k], v))

    def _post(self, ev, reads, writes):
        key, val = ev
        for d in reads:
            if d.r.get(id(key), 0) < val:
                d.r[id(key)] = val
        for d in writes:
            d.w = ev
            d.r = {}

    def op(self, E, fn, reads=(), writes=()):
        self._pre(E, reads, writes)
        ins = fn()
        E.cnt += 1
        self.ninst += 1
        ins.then_inc(E.sem, 1)
        ev = (E.sem, E.cnt)
        self._post(ev, reads, writes)
        return ev

    def dma(self, E, out, in_, reads=(), writes=(), q="q", **kw):
        key = (E.name, q)
        if key not in self.dmaq:
            self.dmaq[key] = {"sems": [self.new_sem(f"d_{E.name}_{q}_{i}") for i in range(self.DMA_RING)], "n": 0}
        Q = self.dmaq[key]
        i = Q["n"]
        Q["n"] += 1
        sem = Q["sems"][i % self.DMA_RING]
        gen = i // self.DMA_RING
        if gen > 0:
            self._need(E, (sem, 16 * gen))
        self._pre(E, reads, writes)
        ins = E.e.dma_start(out=out, in_=in_, **kw)
        ins.then_inc(sem, 16)
        self.ninst += 1
        ev = (sem, 16 * (gen + 1))
        self._post(ev, reads, writes)
        return ev

    def all_events(self):
        evs = []
        for Q in self.dmaq.values():
            n = Q["n"]
            for s_i, sem in enumerate(Q["sems"]):
                cnt = (n - s_i + self.DMA_RING - 1) // self.DMA_RING if n > s_i else 0
                if cnt > 0:
                    evs.append((sem, 16 * cnt))
        for X in self.engs:
            if X.cnt > 0:
                evs.append((X.sem, X.cnt))
        return evs

    def barrier(self):
        evs = self.all_events()
        for E in self.engs:
            for ev in evs:
                self._need(E, ev)

    def finish(self):
        for ev in self.all_events():
            self._need(self.sp, ev)


class Ring:
    def __init__(self, items):
        self.items = [(t, Dep()) for t in items]
        self.i = 0

    def nxt(self):
        r = self.items[self.i % len(self.items)]
        self.i += 1
        return r


def zc_cols():
    ch = []
    for c in range(14):
        ch.append([(c * 128, 128)])
    for c in range(3):
        ch.append([(1792 + c * 128, 128)])
    ch.append([(2176, 64), (2176, 64)])
    ch.append([(2240, 64), (2240, 64)])
    ch.append([(2304, 128)])
    return ch


class K:
    def __init__(self, debug=False):
        self.debug = debug
        nc = self.nc = bass.Bass("TRN2", target_bir_lowering=False)
        dt = nc.dram_tensor
        self.x_in = dt("x_in", [NT, D], F32, kind="ExternalInput").ap()
        self.c_in = dt("c_in", [NSEG, D], F32, kind="ExternalInput").ap()
        self.link_in = dt("link_in", [128, 1], F32, kind="ExternalInput").ap()
        self.cos_in = dt("cos_in", [128, NT], F32, kind="ExternalInput").ap()
        self.sin_in = dt("sin_in", [128, NT], F32, kind="ExternalInput").ap()
        self.cst_in = dt("cst_in", [14, 128, 128], F32, kind="ExternalInput").ap()
        self.W = {}
        for name, shape in WSHAPES.items():
            self.W[name] = dt(name, list(shape), F32, kind="ExternalInput").ap()
        self.y_out = dt("y_out", [NT, D], F32, kind="ExternalOutput").ap()
        sk = "ExternalOutput" if debug else "Internal"
        self.xs = dt("xs", [D, NT], F32, kind=sk).ap()
        self.zs = dt("zs", [NZC * 128, NT], F32, kind="ExternalInput" if (debug and debug.get("mixer_test")) else sk).ap()
        self.os = dt("os", [D, NT], BF16, kind=sk).ap()
        self.ys = dt("ys", [256, NT], F32, kind="Internal").ap()
        self.ws = {nm: dt("ws_" + nm, [256, NT], F32, kind="Internal").ap() for nm in ("r", "kap", "v", "bon", "g", "sg0", "sg1", "kd0", "kd1", "b0", "b1")}

    def mm(self, out, lhsT, rhs, start, stop, reads, writes):
        nc = self.nc
        return self.fw.op(self.fw.pe, lambda: nc.tensor.matmul(out, lhsT, rhs, start=start, stop=stop), reads, writes)

    def act(self, out, in_, func, reads, writes, bias=None, scale=None):
        nc = self.nc
        kw = {}
        if bias is not None:
            kw["bias"] = bias
        if scale is not None:
            kw["scale"] = scale
        return self.fw.op(self.fw.act, lambda: nc.scalar.activation(out=out, in_=in_, func=func, **kw), reads, writes)

    def tt(self, out, in0, in1, op, reads, writes, E=None):
        E = E or self.fw.dve
        return self.fw.op(E, lambda: E.e.tensor_tensor(out=out, in0=in0, in1=in1, op=op), reads, writes)

    def ts(self, out, in0, s1, s2, op0, op1, reads, writes, E=None):
        E = E or self.fw.dve
        if op1 is None:
            return self.fw.op(E, lambda: E.e.tensor_scalar(out=out, in0=in0, scalar1=s1, scalar2=None, op0=op0), reads, writes)
        return self.fw.op(E, lambda: E.e.tensor_scalar(out=out, in0=in0, scalar1=s1, scalar2=s2, op0=op0, op1=op1), reads, writes)

    def stt(self, out, in0, scalar, in1, op0, op1, reads, writes):
        nc = self.nc
        return self.fw.op(self.fw.dve, lambda: nc.vector.scalar_tensor_tensor(out=out, in0=in0, scalar=scalar, in1=in1, op0=op0, op1=op1), reads, writes)

    def aff(self, out, in_, scale, bias, reads, writes):
        kw = {}
        if bias is not None:
            kw["bias"] = bias
        return self.fw.op(self.fw.act, lambda: self.nc.scalar.activation(out=out, in_=in_, func=AF.Identity, scale=scale, **kw), reads, writes)

    def cp(self, E, out, in_, reads, writes):
        if E is self.fw.act:
            return self.fw.op(E, lambda: self.nc.scalar.copy(out=out, in_=in_), reads, writes)
        return self.fw.op(E, lambda: E.e.tensor_copy(out=out, in_=in_), reads, writes)

    def memset(self, E, ap, val, writes):
        return self.fw.op(E, lambda: E.e.memset(ap, val), (), writes)

    def vec_load(self, dst, src_ap, dep):
        return self.fw.dma(self.fw.sp, dst, src_ap, writes=[dep], q="v", allow_slow_non_contiguous=True)

    def build(self):
        nc = self.nc
        with contextlib.ExitStack() as st:
            fw = self.fw = FW(nc, st)
            ps_all = fw.ps([128, 4096])
            self.psr = Ring([ps_all[:, i * 512:(i + 1) * 512] for i in range(6)])
            self.pso = Ring([ps_all[:, i * 512:(i + 1) * 512] for i in range(6, 8)])
            self.psw_views = [ps_all[:, j * 1024:(j + 1) * 1024] for j in range(3)]
            self.psq_views = [ps_all[:, j * 128:(j + 1) * 128] for j in range(16)]
            self.psh_views = [ps_all[:, 2048 + j * 256:2048 + (j + 1) * 256] for j in range(8)]
            self.setup_consts()
            if self.debug and self.debug.get("mixer_test"):
                with contextlib.ExitStack() as pst:
                    self.mixer_phase(pst, 0)
                fw.finish()
                return nc
            with nc.named_scope("phase0"):
                self.phase0()
            fw.barrier()
            for l in range(DEPTH):
                with contextlib.ExitStack() as pst, nc.named_scope(f"tok{l}"):
                    self.token_phase(pst, l)
                fw.barrier()
                if self.debug and self.debug.get("stop_after_A"):
                    break
                with contextlib.ExitStack() as pst:
                    self.mixer_phase(pst, l)
                fw.barrier()
            if not (self.debug and self.debug.get("stop_after_A")):
                with contextlib.ExitStack() as pst, nc.named_scope(f"tok{DEPTH}"):
                    self.token_phase(pst, DEPTH)
            fw.finish()
        return nc

    def setup_consts(self):
        fw = self.fw
        nc = self.nc
        self.cst = fw.sb([128, 14, 128])
        self.dcst = Dep()
        fw.dma(fw.sp, self.cst[:], self.cst_in.rearrange("c p n -> p c n"), writes=[self.dcst], q="v")
        self.ident = self.cst[:, 0, :]
        self.onesD = fw.sb([128, 128], BF16)
        self.blk64 = fw.sb([128, 128], BF16)
        self.dconst2 = Dep()
        self.ts(self.onesD[:], self.cst[:, 3, :], 1.0 / D, None, ALU.mult, None, [self.dcst], [self.dconst2])
        self.cp(fw.dve, self.blk64[:], self.cst[:, 2, :], [self.dcst], [self.dconst2])
        self.link = fw.sb([128, 1])
        self.dlink = Dep()
        fw.dma(fw.sp, self.link[:], self.link_in, writes=[self.dlink], q="v")

    def phase0(self):
        fw = self.fw
        nc = self.nc
        W = self.W
        self.mod = [fw.sb([128, 72, NSEG]) for _ in range(DEPTH)]
        self.dmod = [Dep() for _ in range(DEPTH)]
        self.gsc = [fw.sb([128, 3, KC, NSEG]) for _ in range(DEPTH)]
        self.gate = [fw.sb([128, 3, KC, NSEG]) for _ in range(DEPTH)]
        with contextlib.ExitStack() as pst:
            cT = fw.sb([128, KC, NSEG], stack=pst)
            dc = Dep()
            for s_ in range(NSEG):
                self.vec_load(cT[:, :, s_], self.c_in[s_].rearrange("(kc p) -> p kc", p=128), dc)
            sc = fw.sb([128, KC, NSEG], stack=pst)
            self.act(sc[:], cT[:], AF.Silu, [dc], [dc])
            wr = Ring([fw.sb([128, KC, 512], stack=pst) for _ in range(2)])
            for l in range(DEPTH):
                bT = fw.sb([128, 72], stack=pst)
                db = Dep()
                self.vec_load(bT[:], W["b_ada"][l].rearrange("(c p) -> p c", p=128), db)
                gT = fw.sb([128, 3, KC], stack=pst)
                dg = Dep()
                for i_ in range(3):
                    self.vec_load(gT[:, i_, :], W["norm_g"][l, i_].rearrange("(c p) -> p c", p=128), dg)
                wv = W["w_ada"][l].rearrange("(kc p) n -> p kc n", p=128)
                for blk in range(18):
                    wt, dw = wr.nxt()
                    fw.dma(fw.sp, wt[:], wv[:, :, blk * 512:(blk + 1) * 512], writes=[dw], q="w")
                    for j in range(4):
                        fcn = blk * 4 + j
                        pt, dp = self.psr.nxt()
                        for k in range(KC):
                            self.mm(pt[:, 0:NSEG], wt[:, k, j * 128:(j + 1) * 128], sc[:, k, :], k == 0, k == KC - 1, [dw, dc], [dp])
                        self.ts(self.mod[l][:, fcn, :], pt[:, 0:NSEG], bT[:, fcn:fcn + 1], None, ALU.add, None, [dp, db], [self.dmod[l]])
                m = self.mod[l]
                for i in range(3):
                    for s in range(NSEG):
                        self.stt(self.gsc[l][:, i, :, s], m[:, (3 * i + 1) * 8:(3 * i + 2) * 8, s], 1.0, gT[:, i, :], ALU.add, ALU.mult, [self.dmod[l], dg], [self.dmod[l]])
                    self.ts(self.gate[l][:, i, :, :], m[:, (3 * i + 2) * 8:(3 * i + 3) * 8, :], 1.0 if i == 1 else 0.5, None, ALU.mult, None, [self.dmod[l]], [self.dmod[l]])
            fw.barrier()

    def shift_ap(self, l, i, c, seg):
        return self.mod[l][:, 3 * i * 8 + c, seg:seg + 1]

    def token_phase(self, pst, l):
        fw = self.fw
        nc = self.nc
        self.xT = fw.sb([128, KC, TT], stack=pst)
        self.dx = Dep()
        self.hT = fw.sb([128, KC, TT], BF16, stack=pst)
        self.dh = Dep()
        self.aT = fw.sb([128, FC, TT], BF16, stack=pst)
        self.da = Dep()
        self.sqr = Ring([fw.sb([128, 512], BF16, stack=pst) for _ in range(3)])
        self.f32r = Ring([fw.sb([128, 512], F32, stack=pst) for _ in range(4)])
        self.rstd = fw.sb([128, 512], stack=pst)
        self.drstd = Dep()
        self.wr = Ring([fw.sb([128, KC, 512], BF16, stack=pst) for _ in range(3)])
        self.wor = Ring([fw.sb([128, FC, 256], BF16, stack=pst) for _ in range(2)])
        self.iot = Ring([fw.sb([128, 512], F32, stack=pst) for _ in range(3)])
        for ti in range(NTILE):
            seg = ti // (SEG // TT)
            t0 = ti * TT
            if l == 0:
                self.load_x_input(t0)
            else:
                fw.dma(fw.sp, self.xT[:], self.xs.rearrange("(c p) t -> p c t", p=128)[:, :, t0:t0 + TT], writes=[self.dx], q="x")
                fw.dma(fw.sp, self.hT[:], self.os.rearrange("(c p) t -> p c t", p=128)[:, :, t0:t0 + TT], writes=[self.dh], q="x")
                self.out_proj(self.hT, self.dh, KC, self.W["w_mix_out"][l - 1].rearrange("(c p) n -> p c n", p=128), l - 1, 1, seg)
                self.norm_mod(l - 1, 2, seg)
                self.ffn(l - 1, 1, 2, seg)
            if l < DEPTH:
                self.norm_mod(l, 0, seg)
                self.ffn(l, 0, 0, seg)
                fw.dma(fw.sp, self.xs.rearrange("(c p) t -> p c t", p=128)[:, :, t0:t0 + TT], self.xT[:], reads=[self.dx], q="xo")
                self.norm_mod(l, 1, seg)
                self.mix_in(l, t0)
            else:
                self.store_y(t0)

    def load_x_input(self, t0):
        fw = self.fw
        nc = self.nc
        for b in range(TT // 128):
            it, di = self.iot.nxt()
            it2, di2 = self.iot.nxt()
            for hf, (tt_, dd) in enumerate(((it, di), (it2, di2))):
                fw.dma(fw.sp, tt_[:], self.x_in[t0 + b * 128:t0 + (b + 1) * 128, hf * 512:(hf + 1) * 512], writes=[dd], q="x")
            for hf, (tt_, dd) in enumerate(((it, di), (it2, di2))):
                pt, dp = self.psr.nxt()
                for j in range(4):
                    self.fw.op(fw.pe, lambda j=j: nc.tensor.transpose(pt[:, j * 128:(j + 1) * 128], tt_[:, j * 128:(j + 1) * 128], self.ident), [dd, self.dcst], [dp])
                E = fw.act if hf == 0 else fw.dve
                self.cp(E, self.xT[:, hf * 4:(hf + 1) * 4, b * 128:(b + 1) * 128], pt[:].rearrange("p (j t) -> p j t", j=4), [dp], [self.dx])

    def store_y(self, t0):
        fw = self.fw
        nc = self.nc
        for b in range(TT // 128):
            for hf in range(2):
                pt, dp = self.psr.nxt()
                for j in range(4):
                    c = hf * 4 + j
                    self.fw.op(fw.pe, lambda j=j, c=c: nc.tensor.transpose(pt[:, j * 128:(j + 1) * 128], self.xT[:, c, b * 128:(b + 1) * 128], self.ident), [self.dx, self.dcst], [dp])
                ot, do = self.iot.nxt()
                E = fw.act if hf == 0 else fw.dve
                self.cp(E, ot[:], pt[:], [dp], [do])
                fw.dma(fw.sp, self.y_out[t0 + b * 128:t0 + (b + 1) * 128, hf * 512:(hf + 1) * 512], ot[:], reads=[do], q="xo")

    def norm_mod(self, l, i, seg):
        fw = self.fw
        for hf in range(TT // 512):
            sl = slice(hf * 512, (hf + 1) * 512)
            pt, dp = self.psr.nxt()
            for c in range(KC):
                sq, dsq = self.sqr.nxt()
                self.act(sq[:], self.xT[:, c, sl], AF.Square, [self.dx], [dsq])
                self.mm(pt[:], self.onesD[:], sq[:], c == 0, c == KC - 1, [dsq, self.dconst2], [dp])
            t1, d1 = self.f32r.nxt()
            self.act(t1[:], pt[:], AF.Ln, [dp], [d1], bias=EPS)
            self.act(self.rstd[:], t1[:], AF.Exp, [d1], [self.drstd], scale=-0.5)
            for c in range(KC):
                t2, d2 = self.f32r.nxt()
                self.stt(t2[:], self.xT[:, c, sl], self.gsc[l][:, i, c, seg:seg + 1], self.rstd[:], ALU.mult, ALU.mult, [self.dx, self.drstd, self.dmod[l]], [d2])
                self.act(self.hT[:, c, sl], t2[:], AF.Identity, [d2, self.dmod[l]], [self.dh], bias=self.shift_ap(l, i, c, seg))

    def ffn(self, l, which, i, seg):
        fw = self.fw
        nc = self.nc
        wv = self.W["ffn_w_in"][l, which].rearrange("(kc p) n -> p kc n", p=128)
        for j in range(FC // 2):
            wt, dw = self.wr.nxt()
            fw.dma(fw.pool, wt[:, :, 0:256], wv[:, :, j * 256:(j + 1) * 256], writes=[dw], q="w")
            fw.dma(fw.pool, wt[:, :, 256:512], wv[:, :, DFF + j * 256:DFF + (j + 1) * 256], writes=[dw], q="w")
            for fc in range(2):
                for hf in range(TT // 512):
                    sl = slice(hf * 512, (hf + 1) * 512)
                    pg, dpg = self.psr.nxt()
                    pu, dpu = self.psr.nxt()
                    for k in range(KC):
                        self.mm(pg[:], wt[:, k, fc * 128:(fc + 1) * 128], self.hT[:, k, sl], k == 0, k == KC - 1, [dw, self.dh], [dpg])
                    for k in range(KC):
                        self.mm(pu[:], wt[:, k, 256 + fc * 128:256 + (fc + 1) * 128], self.hT[:, k, sl], k == 0, k == KC - 1, [dw, self.dh], [dpu])
                    sg, dsg = self.f32r.nxt()
                    self.act(sg[:], pg[:], AF.Silu, [dpg], [dsg])
                    self.tt(self.aT[:, 2 * j + fc, sl], sg[:], pu[:], ALU.mult, [dsg, dpu], [self.da])
        self.out_proj(self.aT, self.da, FC, self.W["ffn_w_out"][l, which].rearrange("(c p) n -> p c n", p=128), l, i, seg)

    def out_proj(self, src, dsrc, nk, wv, l, i, seg):
        fw = self.fw
        for dp2 in range(KC // 2):
            wt, dw = self.wor.nxt()
            fw.dma(fw.pool, wt[:, 0:nk, :], wv[:, :, dp2 * 256:(dp2 + 1) * 256], writes=[dw], q="w")
            for dc in range(2):
                c = dp2 * 2 + dc
                for hf in range(TT // 512):
                    sl = slice(hf * 512, (hf + 1) * 512)
                    pt, dp = self.psr.nxt()
                    for k in range(nk):
                        self.mm(pt[:], wt[:, k, dc * 128:(dc + 1) * 128], src[:, k, sl], k == 0, k == nk - 1, [dw, dsrc], [dp])
                    self.stt(self.xT[:, c, sl], pt[:], self.gate[l][:, i, c, seg:seg + 1], self.xT[:, c, sl], ALU.mult, ALU.add, [dp, self.dx, self.dmod[l]], [self.dx])

    def mix_in(self, l, t0):
        fw = self.fw
        wv = self.W["w_mix_in"][l].rearrange("(kc p) n -> p kc n", p=128)
        for zc, cols in enumerate(zc_cols()):
            wt, dw = self.wr.nxt()
            o = 0
            for (c0, n) in cols:
                fw.dma(fw.pool, wt[:, :, o:o + n], wv[:, :, c0:c0 + n], writes=[dw], q="w")
                o += n
            for hf in range(TT // 512):
                sl = slice(hf * 512, (hf + 1) * 512)
                pt, dp = self.psr.nxt()
                for k in range(KC):
                    self.mm(pt[:], wt[:, k, 0:128], self.hT[:, k, sl], k == 0, k == KC - 1, [dw, self.dh], [dp])
                ot, do = self.iot.nxt()
                self.cp(fw.act if hf == 0 else fw.dve, ot[:], pt[:], [dp], [do])
                fw.dma(fw.sp, self.zs[zc * 128:(zc + 1) * 128, t0 + hf * 512:t0 + (hf + 1) * 512], ot[:], reads=[do], q="xo")

    def mixer_phase(self, pst, l):
        fw = self.fw
        which = (self.debug or {}).get("mixers", ("lru", "att", "rwkv"))
        if "lru" in which:
            with contextlib.ExitStack() as st2, self.nc.named_scope(f"lru{l}"):
                self.lru(st2, l)
            fw.barrier()
        if "att" in which:
            with contextlib.ExitStack() as st2, self.nc.named_scope(f"att{l}"):
                self.attention(st2, l)
            fw.barrier()
        if "rwkv" in which:
            with contextlib.ExitStack() as st2, self.nc.named_scope(f"rwkv{l}"):
                self.rwkv(st2, l)
            fw.barrier()

    UNITS = ((0, 2), (2 * SEG, 1))

    def lru(self, st, l):
        fw = self.fw
        nc = self.nc
        W = self.W
        SA = 2 * SEG
        pv = fw.sb([128, 3, 16], stack=st)
        dpv = Dep()
        col1 = lambda ap: ap.rearrange("(p o) -> p o", o=1)
        for c in range(3):
            cs = slice(c * 128, (c + 1) * 128)
            self.vec_load(pv[:, c, 0:4], W["lru_conv_w"][l][:, cs].rearrange("j p -> p j"), dpv)
            self.vec_load(pv[:, c, 4:5], col1(W["lru_conv_b"][l, cs]), dpv)
            for d in range(2):
                self.vec_load(pv[:, c, 5 + d:6 + d], col1(W["lru_b_gate_a"][l, d, cs]), dpv)
                self.vec_load(pv[:, c, 7 + d:8 + d], col1(W["lru_b_gate_x"][l, d, cs]), dpv)
                self.vec_load(pv[:, c, 9 + d:10 + d], col1(W["lru_lambda"][l, d, cs]), dpv)
        self.act(pv[:, :, 11:13], pv[:, :, 9:11], AF.Exp, [dpv], [dpv], scale=-1.0)
        self.act(pv[:, :, 11:13], pv[:, :, 11:13], AF.Ln, [dpv], [dpv], bias=1.0)
        self.ts(pv[:, :, 13:15], pv[:, :, 11:13], -16.0, None, ALU.mult, None, [dpv], [dpv])
        self.ts(pv[:, :, 11:13], pv[:, :, 11:13], -8.0, None, ALU.mult, None, [dpv], [dpv])
        w32 = fw.sb([128, 4, 128], stack=st)
        dw32 = Dep()
        wbd = fw.sb([128, 4, 128], BF16, stack=st)
        dwbd = Dep()
        xpad = fw.sb([128, 2, SEG + 3], stack=st)
        dxp = Dep()
        xc = fw.sb([128, SA], stack=st)
        dxc = Dep()
        xcb = fw.sb([128, SA], BF16, stack=st)
        dxcb = Dep()
        bufs = [fw.sb([128, SA], stack=st) for _ in range(5)]
        dbs = [Dep() for _ in range(5)]
        h0, h1 = bufs[3], bufs[4]
        dh0, dh1 = dbs[3], dbs[4]
        for c in range(3):
            cs = slice(c * 128, (c + 1) * 128)
            self.memset(fw.pool, w32[:], 0.0, [dw32])
            for d in range(2):
                for gi, nm in enumerate(("lru_w_gate_a", "lru_w_gate_x")):
                    for n in range(2):
                        fw.dma(fw.sp, w32[n * 64:(n + 1) * 64, d * 2 + gi, n * 64:(n + 1) * 64], W[nm][l, d, 2 * c + n], writes=[dw32], q="v")
            self.cp(fw.pool, wbd[:], w32[:], [dw32], [dwbd])
            for (tok0, nseg) in self.UNITS:
                S = nseg * SEG
                xp = xpad[:, 0:nseg, :]
                self.memset(fw.pool, xp[:, :, 0:2], 0.0, [dxp])
                self.memset(fw.pool, xp[:, :, SEG + 2:SEG + 3], 0.0, [dxp])
                fw.dma(fw.sp, xp[:, :, 2:SEG + 2], self.zs[cs, tok0:tok0 + S].rearrange("p (s t) -> p s t", s=nseg), writes=[dxp], q="x")
                if nseg == 2:
                    self.ts(xpad[:, 1, 0:2], xpad[:, 0, SEG:SEG + 2], self.link[:, 0:1], None, ALU.mult, None, [dxp, self.dlink], [dxp])
                    self.ts(xpad[:, 0, SEG + 2:SEG + 3], xpad[:, 1, 2:3], self.link[:, 0:1], None, ALU.mult, None, [dxp, self.dlink], [dxp])
                xc3 = xc[:, 0:S].rearrange("p (s t) -> p s t", s=nseg)
                self.ts(xc3, xp[:, :, 0:SEG], pv[:, c, 0:1], pv[:, c, 4:5], ALU.mult, ALU.add, [dxp, dpv], [dxc])
                for j in range(1, 4):
                    self.stt(xc3, xp[:, :, j:j + SEG], pv[:, c, j:j + 1], xc3, ALU.mult, ALU.add, [dxp, dpv, dxc], [dxc])
                self.cp(fw.act, xcb[:, 0:S], xc[:, 0:S], [dxc], [dxcb])
                for d in range(2):
                    hb, dhb = (h0, dh0) if d == 0 else (h1, dh1)
                    b1, b2, b3 = bufs[0:3]
                    d1, d2, d3 = dbs[0:3]
                    for gi, (bt, dbt) in enumerate(((b1, d1), (b2, d2))):
                        for blk in range(S // 512):
                            sl = slice(blk * 512, (blk + 1) * 512)
                            pt, dp = self.psr.nxt()
                            self.mm(pt[:], wbd[:, d * 2 + gi, :], xcb[:, sl], True, True, [dwbd, dxcb], [dp])
                            self.act(bt[:, sl], pt[:], AF.Sigmoid, [dp, dpv], [dbt], bias=pv[:, c, 5 + 2 * gi + d:6 + 2 * gi + d])
                    self.act(b3[:, 0:S], b1[:, 0:S], AF.Exp, [d1, dpv], [d3], scale=pv[:, c, 11 + d:12 + d])
                    self.act(b1[:, 0:S], b1[:, 0:S], AF.Exp, [d1, dpv], [d1], scale=pv[:, c, 13 + d:14 + d])
                    self.act(b1[:, 0:S], b1[:, 0:S], AF.Sqrt, [d1], [d1], scale=-1.0, bias=1.0)
                    self.tt(b2[:, 0:S], b2[:, 0:S], xc[:, 0:S], ALU.mult, [d2, dxc], [d2])
                    self.tt(b2[:, 0:S], b2[:, 0:S], b1[:, 0:S], ALU.mult, [d2, d1], [d2])
                    if nseg == 2:
                        cp_ = SEG if d == 0 else SEG - 1
                        self.ts(b3[:, cp_:cp_ + 1], b3[:, cp_:cp_ + 1], self.link[:, 0:1], None, ALU.mult, None, [d3, self.dlink], [d3])
                    if d == 0:
                        self.fw.op(fw.dve, lambda: nc.vector.tensor_tensor_scan(out=hb[:, 0:S], data0=b3[:, 0:S], data1=b2[:, 0:S], initial=0.0, op0=ALU.mult, op1=ALU.add), [d3, d2], [dhb])
                    else:
                        self.fw.op(fw.dve, lambda: nc.vector.tensor_tensor_scan(out=hb[:, S - 1::-1] if False else hb[:, 0:S][:, ::-1], data0=b3[:, 0:S][:, ::-1], data1=b2[:, 0:S][:, ::-1], initial=0.0, op0=ALU.mult, op1=ALU.add), [d3, d2], [dhb])
                b1, b2 = bufs[0], bufs[1]
                d1, d2 = dbs[0], dbs[1]
                fw.dma(fw.sp, b1[:, 0:S], self.zs[384 + c * 128:384 + (c + 1) * 128, tok0:tok0 + S], writes=[d1], q="x")
                self.act(b2[:, 0:S], b1[:, 0:S], AF.Square, [d1], [d2])
                self.ts(b2[:, 0:S], b2[:, 0:S], 0.044715, 1.0, ALU.mult, ALU.add, [d2], [d2])
                self.tt(b2[:, 0:S], b2[:, 0:S], b1[:, 0:S], ALU.mult, [d2, d1], [d2])
                self.act(b2[:, 0:S], b2[:, 0:S], AF.Sigmoid, [d2], [d2], scale=1.5957691216057308)
                self.tt(b1[:, 0:S], b1[:, 0:S], b2[:, 0:S], ALU.mult, [d1, d2], [d1])
                self.tt(h0[:, 0:S], h0[:, 0:S], h1[:, 0:S], ALU.add, [dh0, dh1], [dh0])
                self.tt(xcb[:, 0:S], h0[:, 0:S], b1[:, 0:S], ALU.mult, [dh0, d1], [dxcb])
                fw.dma(fw.sp, self.os[cs, tok0:tok0 + S], xcb[:, 0:S], reads=[dxcb], q="xo")

    def attention(self, st, l):
        fw = self.fw
        nc = self.nc
        W = self.W
        SA = 2 * SEG
        onesf = self.cst[:, 3, :]
        psw = Ring(self.psw_views)
        rotT = self.cst[:, 1, :]
        gv = fw.sb([128, 8], stack=st)
        dgv = Dep()
        col1 = lambda ap: ap.rearrange("(p o) -> p o", o=1)
        for hh in range(2):
            self.vec_load(gv[hh * 64:(hh + 1) * 64, 0:1], col1(W["attn_q_norm"][l]), dgv)
            self.vec_load(gv[hh * 64:(hh + 1) * 64, 1:2], col1(W["attn_k_norm"][l]), dgv)
        self.ts(gv[:, 0:1], gv[:, 0:1], 0.125, None, ALU.mult, None, [dgv], [dgv])
        rows = fw.sb([1, 132], stack=st)
        drw = Dep()
        fw.dma(fw.sp, rows[0:1, 0:64], W["attn_q_norm"][l].rearrange("(o d) -> o d", o=1), writes=[drw], q="v")
        fw.dma(fw.sp, rows[0:1, 64:128], W["attn_k_norm"][l].rearrange("(o d) -> o d", o=1), writes=[drw], q="v")
        self.fw.op(fw.dve, lambda: nc.vector.tensor_reduce(out=rows[0:1, 128:130], in_=rows[0:1, 0:128].rearrange("o (a d) -> o a d", a=2), axis=mybir.AxisListType.X, op=ALU.max, apply_absolute_value=True), [drw], [drw])
        self.tt(rows[0:1, 130:131], rows[0:1, 128:129], rows[0:1, 129:130], ALU.mult, [drw], [drw])
        self.cp(fw.dve, rows[0:1, 131:132], rows[0:1, 130:131], [drw], [drw])
        pt, dp = psw.nxt()
        self.mm(pt[:, 0:2], onesf[0:1, :], rows[0:1, 130:132], True, True, [drw, self.dcst], [dp])
        self.ts(gv[:, 2:3], pt[:, 0:1], -8.0, None, ALU.mult, None, [dp], [dgv])
        self.ts(gv[:, 3:4], self.link[:, 0:1], 30000.0, -30000.0, ALU.mult, ALU.add, [self.dlink], [dgv])
        self.tt(gv[:, 3:4], gv[:, 3:4], gv[:, 2:3], ALU.add, [dgv], [dgv])
        cosT = fw.sb([128, SA], stack=st)
        sinT = fw.sb([128, SA], stack=st)
        dcs = Dep()
        srcr = Ring([fw.sb([128, SA], stack=st) for _ in range(2)])
        kT = [[fw.sb([128, SA], BF16, stack=st) for _ in range(2)] for _ in range(2)]
        dkT = [Dep(), Dep()]
        for kv_ in range(2):
            self.memset(fw.pool, kT[kv_][0][64:128, :], 0.0, [dkT[kv_]])
            self.memset(fw.pool, kT[kv_][1][0:64, :], 0.0, [dkT[kv_]])
        vtok = fw.sb([128, SA // 128, 2, 192], BF16, stack=st)
        dvt = Dep()
        qT = fw.sb([128, SA], BF16, stack=st)
        dqT = Dep()
        och = fw.sb([128, SA], BF16, stack=st)
        doc = Dep()
        pTr = Ring([fw.sb([128, 1024], BF16, stack=st) for _ in range(4)])
        osbr = Ring([fw.sb([128, 512], stack=st) for _ in range(2)])
        lnrr = Ring([fw.sb([128, 512], stack=st) for _ in range(2)])
        tail = [None]
        sqr = Ring([fw.sb([128, 512], BF16, stack=st) for _ in range(2)])
        f32r = Ring([fw.sb([128, 512], stack=st) for _ in range(6)])
        osb = fw.sb([128, 512], stack=st)
        dosb = Dep()
        lnr = fw.sb([128, 512], stack=st)
        dlnr = Dep()
        self.memset(fw.pool, vtok[:], 0.0, [dvt])
        self.memset(fw.pool, vtok[:, :, :, 64:65], 1.0, [dvt])

        def norm_rope(src, dsrc, dst, ddst, gcol, S, tok0):
            for blk in range(S // 512):
                sl = slice(blk * 512, (blk + 1) * 512)
                sq, dsq = sqr.nxt()
                self.act(sq[:], src[:, sl], AF.Square, [dsrc], [dsq])
                p1, dp1 = psw.nxt()
                self.mm(p1[:, 0:512], self.blk64[:], sq[:], True, True, [dsq, self.dconst2], [dp1])
                t, dt_ = f32r.nxt()
                self.act(t[:], p1[:, 0:512], AF.Ln, [dp1], [dt_], bias=EPS)
                self.act(t[:], t[:], AF.Exp, [dt_], [dt_], scale=-0.5)
                qn, dqn = f32r.nxt()
                self.stt(qn[:], src[:, sl], gv[:, gcol:gcol + 1], t[:], ALU.mult, ALU.mult, [dsrc, dgv, dt_], [dqn])
                p2, dp2 = psw.nxt()
                self.mm(p2[:, 0:512], rotT, qn[:], True, True, [dqn, self.dcst], [dp2])
                t1, dt1 = f32r.nxt()
                self.tt(t1[:], qn[:], cosT[:, sl], ALU.mult, [dqn, dcs], [dt1], )
                self.tt(t[:], p2[:, 0:512], sinT[:, sl], ALU.mult, [dp2, dcs, dt_], [dt_])
                if isinstance(dst, list):
                    self.tt(dst[0][0:64, sl], t1[0:64, :], t[0:64, :], ALU.add, [dt1, dt_], [ddst])
                    self.tt(dst[1][64:128, sl], t1[64:128, :], t[64:128, :], ALU.add, [dt1, dt_], [ddst])
                else:
                    self.tt(dst[:, sl], t1[:], t[:], ALU.add, [dt1, dt_], [ddst])

        for (tok0, nseg) in self.UNITS:
            S = nseg * SEG
            fw.dma(fw.sp, cosT[:, 0:S], self.cos_in[:, tok0:tok0 + S], writes=[dcs], q="x")
            fw.dma(fw.sp, sinT[:, 0:S], self.sin_in[:, tok0:tok0 + S], writes=[dcs], q="x")
            for kv in range(2):
                src, dsrc = srcr.nxt()
                fw.dma(fw.sp, src[:, 0:S], self.zs[(17 + kv) * 128:(18 + kv) * 128, tok0:tok0 + S], writes=[dsrc], q="x")
                norm_rope(src, dsrc, kT[kv], dkT[kv], 1, S, tok0)
            src, dsrc = srcr.nxt()
            fw.dma(fw.sp, src[:, 0:S], self.zs[19 * 128:20 * 128, tok0:tok0 + S], writes=[dsrc], q="x")
            for b0 in range(0, S // 128, 4):
                pt, dp = psw.nxt()
                for j in range(4):
                    self.fw.op(fw.pe, lambda j=j: nc.tensor.transpose(pt[:, j * 128:(j + 1) * 128], src[:, (b0 + j) * 128:(b0 + j + 1) * 128], self.ident), [dsrc, self.dcst], [dp])
                pv4 = pt[:, 0:512].rearrange("p (j k d) -> p j k d", j=4, k=2)
                self.cp(fw.act, vtok[:, b0:b0 + 4, :, 0:64], pv4, [dp], [dvt])
                self.cp(fw.dve, vtok[:, b0:b0 + 4, :, 128:192], pv4, [dp], [dvt])
            for qc in range(3):
                src, dsrc = srcr.nxt()
                fw.dma(fw.sp, src[:, 0:S], self.zs[(14 + qc) * 128:(15 + qc) * 128, tok0:tok0 + S], writes=[dsrc], q="x")
                norm_rope(src, dsrc, qT, dqT, 0, S, tok0)
                for qb in range(S // 512):
                    qs = slice(qb * 512, (qb + 1) * 512)
                    for e in range(2):
                        h = 2 * qc + e
                        kv = h // 3
                        es = slice(e * 64, (e + 1) * 64)
                        po, dpo = self.pso.nxt()
                        nk = S // 128
                        LOOK = 2
                        pend = []
                        nk2 = nk // 2
                        for k2i in range(nk2 + LOOK):
                            if k2i < nk2:
                                ps_, dps = psw.nxt()
                                for u in range(2):
                                    kc = 2 * k2i + u
                                    self.mm(ps_[:, u * 512:(u + 1) * 512], kT[kv][e][:, kc * 128:(kc + 1) * 128], qT[:, qs], True, True, [dkT[kv], dqT], [dps])
                                pT, dpT = pTr.nxt()
                                same = (nseg == 1) or ((qb // 4) == ((2 * k2i) // 16))
                                bc = 2 if same else 3
                                self.act(pT[:], ps_[:], AF.Exp, [dps, dgv], [dpT], bias=gv[:, bc:bc + 1])
                                pend.append((pT, dpT, k2i))
                            if k2i == LOOK - 1 and tail[0] is not None:
                                tail[0]()
                                tail[0] = None
                            if k2i >= LOOK:
                                pT, dpT, kk2 = pend.pop(0)
                                for u in range(2):
                                    k2 = 2 * kk2 + u
                                    if e == 0:
                                        self.mm(po[0:65, :], vtok[:, k2, kv, 0:65], pT[:, u * 512:(u + 1) * 512], k2 == 0, k2 == nk - 1, [dvt, dpT], [dpo])
                                    else:
                                        self.mm(po[:, :], vtok[:, k2, kv, 64:192], pT[:, u * 512:(u + 1) * 512], k2 == 0, k2 == nk - 1, [dvt, dpT], [dpo])

                        def mk_tail(po=po, dpo=dpo, e=e, es=es, qs=qs):
                            def f():
                                r0 = 64 if e == 0 else 0
                                lnr, dlnr = lnrr.nxt()
                                self.act(lnr[r0:r0 + 1, :], po[r0:r0 + 1, :], AF.Ln, [dpo], [dlnr])
                                self.act(lnr[r0:r0 + 1, :], lnr[r0:r0 + 1, :], AF.Exp, [dlnr], [dlnr], scale=-1.0)
                                pb, dpb = psw.nxt()
                                self.mm(pb[:, 0:512], onesf[r0:r0 + 1, :], lnr[r0:r0 + 1, :], True, True, [dlnr, self.dcst], [dpb])
                                osb, dosb = osbr.nxt()
                                self.cp(fw.act, osb[es, :], po[es, :], [dpo], [dosb])
                                self.tt(och[es, qs], osb[es, :], pb[es, 0:512], ALU.mult, [dosb, dpb], [doc])
                            return f
                        tail[0] = mk_tail()
                if tail[0] is not None:
                    tail[0]()
                    tail[0] = None
                fw.dma(fw.sp, self.os[640 + qc * 128:640 + (qc + 1) * 128, tok0:tok0 + S], och[:, 0:S], reads=[doc], q="xo")

    def rwkv_pre(self, st, l):
        fw = self.fw
        nc = self.nc
        W = self.W
        ws = self.ws
        BL = 512
        col1 = lambda ap: ap.rearrange("(p o) -> p o", o=1)
        cp2 = lambda ap: ap.rearrange("(c p) -> p c", p=128)
        blk64f = self.cst[:, 2, :]
        pp = fw.sb([128, 64], stack=st)
        dpp = Dep()
        mu = W["rwkv_mu"][l]
        self.vec_load(pp[:, 0:8], cp2(mu), dpp)
        self.ts(pp[:, 8:16], pp[:, 0:8], 0.5, None, ALU.mult, None, [dpp], [dpp])
        self.ts(pp[:, 16:24], pp[:, 0:8], -1.0, 1.0, ALU.mult, ALU.add, [dpp], [dpp])
        for d in range(2):
            self.vec_load(pp[:, 24 + 2 * d:26 + 2 * d], cp2(W["rwkv_w0"][l, d]), dpp)
            self.vec_load(pp[:, 28 + 2 * d:30 + 2 * d], cp2(W["rwkv_a0"][l, d]), dpp)
        self.vec_load(pp[:, 32:34], cp2(W["rwkv_k_k"][l]), dpp)
        self.vec_load(pp[:, 34:36], cp2(W["rwkv_k_a"][l]), dpp)
        self.ts(pp[:, 36:38], pp[:, 34:36], -1.0, 1.0, ALU.mult, ALU.add, [dpp], [dpp])
        self.vec_load(pp[:, 38:40], cp2(W["rwkv_r_k"][l].rearrange("h k -> (h k)")), dpp)
        wup = fw.sb([64, 2, 256], stack=st)
        aup = fw.sb([128, 256], stack=st)
        gup = fw.sb([128, 256], stack=st)
        dwl = Dep()
        for d in range(2):
            fw.dma(fw.sp, wup[:, d, :], W["rwkv_w_up"][l, d], writes=[dwl], q="v")
        fw.dma(fw.sp, aup[64:128, :], W["rwkv_a_up"][l], writes=[dwl], q="v")
        fw.dma(fw.sp, gup[:], W["rwkv_g_up"][l], writes=[dwl], q="v")
        padr = Ring([fw.sb([128, 8, BL + 2], stack=st) for _ in range(2)])
        sR = Ring([fw.sb([128, 8, BL], stack=st) for _ in range(2)])
        fR = Ring([fw.sb([128, 8, BL], stack=st) for _ in range(2)])
        tR = Ring([fw.sb([128, BL], stack=st) for _ in range(12)])
        oR = Ring([fw.sb([128, BL], stack=st) for _ in range(10)])
        zv = self.zs[768:1792, :].rearrange("(c p) t -> p c t", p=128)
        for (tok0, nseg) in self.UNITS:
            S = nseg * SEG
            for bi in range(S // BL):
                t0 = bi * BL
                g0 = tok0 + t0
                sl = slice(g0, g0 + BL)
                lo = 1 if t0 == 0 else 0
                hi = 1 if t0 + BL == S else 0
                n = BL + 2 - lo - hi
                pad, dpad = padr.nxt()
                if lo:
                    self.memset(fw.pool, pad[:, :, 0:1], 0.0, [dpad])
                if hi:
                    self.memset(fw.pool, pad[:, :, BL + 1:BL + 2], 0.0, [dpad])
                fw.dma(fw.sp, pad[:, :, lo:lo + n], zv[:, :, g0 - 1 + lo:g0 - 1 + lo + n], writes=[dpad], q="x")
                if S == 2 * SEG and (t0 == SEG or t0 + BL == SEG):
                    cix = 0 if t0 == SEG else BL + 1
                    self.ts(pad[:, :, cix:cix + 1], pad[:, :, cix:cix + 1], self.link[:, 0:1], None, ALU.mult, None, [dpad, self.dlink], [dpad])
                s_, ds_ = sR.nxt()
                f, df = fR.nxt()
                for c in range(8):
                    self.tt(s_[:, c, :], pad[:, c, 0:BL], pad[:, c, 2:BL + 2], ALU.add, [dpad], [ds_], E=fw.pool if c % 2 else fw.dve)
                    self.aff(s_[:, c, :], s_[:, c, :], pp[:, 8 + c:9 + c], None, [ds_, dpp], [ds_])
                    self.stt(f[:, c, :], pad[:, c, 1:BL + 1], pp[:, 16 + c:17 + c], s_[:, c, :], ALU.mult, ALU.add, [dpad, ds_, dpp], [df])
                for fc in range(2):
                    fw.dma(fw.sp, ws["r"][fc * 128:(fc + 1) * 128, sl], f[:, fc, :], reads=[df], q="xo")
                    fw.dma(fw.sp, ws["v"][fc * 128:(fc + 1) * 128, sl], f[:, 4 + fc, :], reads=[df], q="xo")
                tw, dtw = tR.nxt()
                self.act(tw[0:64, :], f[0:64, 6, :], AF.Tanh, [df], [dtw])
                for d in range(2):
                    for fc in range(2):
                        pt, dp = self.psr.nxt()
                        self.mm(pt[:], wup[:, d, fc * 128:(fc + 1) * 128], tw[0:64, :], True, True, [dwl, dtw], [dp])
                        o, do = oR.nxt()
                        self.act(o[:], pt[:], AF.Sigmoid, [dp, dpp], [do], bias=pp[:, 24 + 2 * d + fc:25 + 2 * d + fc])
                        fw.dma(fw.sp, ws["sg%d" % d][fc * 128:(fc + 1) * 128, sl], o[:], reads=[do], q="xo")
                kaps = []
                for fc in range(2):
                    kk, dkk = tR.nxt()
                    self.aff(kk[:], f[:, 2 + fc, :], pp[:, 32 + fc:33 + fc], None, [df, dpp], [dkk])
                    sq, dsq = tR.nxt()
                    self.act(sq[:], kk[:], AF.Square, [dkk], [dsq])
                    pt, dp = self.psr.nxt()
                    self.mm(pt[:], blk64f, sq[:], True, True, [dsq, self.dcst], [dp])
                    self.act(sq[:], pt[:], AF.Ln, [dp], [dsq], scale=64.0, bias=1e-24)
                    self.act(sq[:], sq[:], AF.Exp, [dsq], [dsq], scale=-0.5)
                    kap, dkap = oR.nxt()
                    self.tt(kap[:], kk[:], sq[:], ALU.mult, [dkk, dsq], [dkap])
                    fw.dma(fw.sp, ws["kap"][fc * 128:(fc + 1) * 128, sl], kap[:], reads=[dkap], q="xo")
                    kaps.append((kap, dkap))
                for fc in range(2):
                    pt, dp = self.psr.nxt()
                    self.mm(pt[:], aup[64:128, fc * 128:(fc + 1) * 128], f[64:128, 6, :], True, True, [dwl, df], [dp])
                    for d in range(2):
                        a_, da_ = tR.nxt()
                        self.act(a_[:], pt[:], AF.Sigmoid, [dp, dpp], [da_], bias=pp[:, 28 + 2 * d + fc:29 + 2 * d + fc])
                        t_, dt_ = tR.nxt()
                        self.aff(t_[:], a_[:], pp[:, 34 + fc:35 + fc], pp[:, 36 + fc:37 + fc], [da_, dpp], [dt_])
                        kd, dkd = oR.nxt()
                        self.tt(kd[:], t_[:], f[:, 2 + fc, :], ALU.mult, [dt_, df], [dkd])
                        fw.dma(fw.sp, ws["kd%d" % d][fc * 128:(fc + 1) * 128, sl], kd[:], reads=[dkd], q="xo")
                        b_, db_ = oR.nxt()
                        self.tt(b_[:], a_[:], kaps[fc][0][:], ALU.mult, [da_, kaps[fc][1]], [db_], E=fw.pool)
                        fw.dma(fw.sp, ws["b%d" % d][fc * 128:(fc + 1) * 128, sl], b_[:], reads=[db_], q="xo")
                for fc in range(2):
                    rk, drk = tR.nxt()
                    self.stt(rk[:], f[:, fc, :], pp[:, 38 + fc:39 + fc], f[:, 2 + fc, :], ALU.mult, ALU.mult, [df, dpp], [drk])
                    pt, dp = self.psr.nxt()
                    self.mm(pt[:], blk64f, rk[:], True, True, [drk, self.dcst], [dp])
                    bo, dbo = oR.nxt()
                    self.stt(bo[:], pt[:], 64.0, f[:, 4 + fc, :], ALU.mult, ALU.mult, [dp, df], [dbo])
                    fw.dma(fw.sp, ws["bon"][fc * 128:(fc + 1) * 128, sl], bo[:], reads=[dbo], q="xo")
                sgx, dsgx = tR.nxt()
                self.act(sgx[:], f[:, 7, :], AF.Sigmoid, [df], [dsgx])
                for fc in range(2):
                    pt, dp = self.psr.nxt()
                    self.mm(pt[:], gup[:, fc * 128:(fc + 1) * 128], sgx[:], True, True, [dwl, dsgx], [dp])
                    go, dgo = oR.nxt()
                    self.cp(fw.act, go[:], pt[:], [dp], [dgo])
                    fw.dma(fw.sp, ws["g"][fc * 128:(fc + 1) * 128, sl], go[:], reads=[dgo], q="xo")

    def rwkv(self, st0, l):
        fw = self.fw
        with contextlib.ExitStack() as st1, self.nc.named_scope(f"rwpre{l}"):
            self.rwkv_pre(st1, l)
        fw.barrier()
        with contextlib.ExitStack() as st, self.nc.named_scope(f"rwscan{l}"):
            self.rwkv_scan(st, l)

    def rwkv_scan(self, st, l):
        fw = self.fw
        nc = self.nc
        W = self.W
        ws = self.ws
        SB = 256
        NP = 2
        NCH = 4
        C0 = -0.6065306597126334
        GN_EPS = 64e-5
        ident = self.ident
        id64 = self.cst[0:64, 0, 0:64]
        c64 = self.cst[0:64, 2, 0:64]
        hk = lambda ap: ap.rearrange("(h k) -> k h", k=64)
        hkt = lambda ap: ap.rearrange("(h k) t -> k h t", k=64)
        flat2 = lambda ap: ap.rearrange("p a b -> p (a b)")
        mX = [self.cst[:, 4, :], self.cst[:, 8, :]]
        mXt = [self.cst[:, 8, :], self.cst[:, 4, :]]
        mD = [flat2(self.cst[0:64, 9:11, :])[:, 0:192], flat2(self.cst[0:64, 12:14, :])[:, 0:192]]
        pr = fw.sb([64, 8], stack=st)
        dpr = Dep()
        self.vec_load(pr[:, 0:4], hk(W["rwkv_ln_g"][l]), dpr)
        self.vec_load(pr[:, 4:8], hk(W["rwkv_ln_b"][l]), dpr)
        smask = [fw.sb([64, 4, SB], stack=st) for _ in range(2)]
        dsm = Dep()
        for d in range(2):
            self.memset(fw.pool, smask[d][:], 1.0, [dsm])
            z0 = 0 if d == 0 else 63
            self.memset(fw.pool, smask[d][:].rearrange("k h (c t) -> k (h c) t", t=64)[:, :, z0:z0 + 1], 0.0, [dsm])
        tsm = Ring([fw.sb([64, SB], stack=st) for _ in range(6)])

        class Set:
            pass
        sets = []
        for _ in range(2):
            S_ = Set()
            S_.B = [fw.sb([64, 4, SB], stack=st) for _ in range(9)]
            S_.dB = [Dep() for _ in range(9)]
            S_.KRf = fw.sb([64, 4, NP, 2, 128], stack=st)
            S_.dKR = Dep()
            sets.append(S_)
        Tst = fw.sb([64, 4, 64], stack=st)
        dT = [Dep() for _ in range(4)]
        NPI = 4 * NP
        NCI = 4 * NCH
        tok3 = [fw.sb([64, 192], stack=st) for _ in range(NCI)]
        Ach = [fw.sb([64, 192], stack=st) for _ in range(NCI)]
        dch = [Dep() for _ in range(NCI)]
        MT = [fw.sb([128, 128], stack=st) for _ in range(NPI)]
        MT1 = [fw.sb([64, 64], stack=st) for _ in range(NPI)]
        dsl = [Dep() for _ in range(NPI)]
        Xb = [[fw.sb([128, 128], BF16, stack=st) for _ in range(2)] for _ in range(NPI)]
        Xtb = [[fw.sb([128, 128], BF16, stack=st) for _ in range(2)] for _ in range(NPI)]
        Accb = [[fw.sb([128, 128], BF16, stack=st) for _ in range(2)] for _ in range(NPI)]
        dAcc = [[Dep(), Dep()] for _ in range(NPI)]
        identb = fw.sb([128, 128], BF16, stack=st)
        self.cp(fw.dve, identb[:], ident, [self.dcst], [dsm])
        dXb = [[Dep(), Dep()] for _ in range(NPI)]
        dXtb = [[Dep(), Dep()] for _ in range(NPI)]
        gsr = Ring([fw.sb([64, 64], stack=st) for _ in range(16)])
        usr = Ring([fw.sb([64, 64], stack=st) for _ in range(8)])
        obuf = fw.sb([64, 4, SB], BF16, stack=st)
        dob = Dep()
        dys = {}

        def elementwise(job, Z):
            tok0, S, d, sbi = job
            B, dB = Z.B, Z.dB
            KRf, dKR = Z.KRf, Z.dKR
            g0 = tok0 + sbi * SB
            sl = slice(g0, g0 + SB)
            for bi, nm in ((0, "r"), (1, "kap"), (2, "v"), (3, "kd%d" % d), (4, "sg%d" % d), (7, "b%d" % d)):
                fw.dma(fw.sp, B[bi][:], hkt(ws[nm][:, sl]), writes=[dB[bi]], q="x")
            yield
            sg, dsg = B[4], dB[4]
            Ls, dLs = B[5], dB[5]
            flat = lambda t: t[:].rearrange("k h t -> k (h t)")
            rvf = (lambda ap: ap) if d == 0 else (lambda ap: ap[:, ::-1])
            self.fw.op(fw.dve, lambda: nc.vector.tensor_tensor_scan(out=rvf(flat(Ls)), data0=rvf(flat(smask[d])), data1=rvf(flat(sg)), initial=0.0, op0=ALU.mult, op1=ALU.add), [dsm, dsg], [dLs])
            yield
            self.tt(sg[:], Ls[:], sg[:], ALU.subtract, [dLs, dsg], [dsg], E=fw.pool)
            self.act(sg[:], sg[:], AF.Exp, [dsg], [dsg], scale=C0)
            yield
            Ep, dEp = B[6], dB[6]
            self.act(Ep[:], Ls[:], AF.Exp, [dLs], [dEp], scale=C0)
            self.act(Ls[:], Ls[:], AF.Exp, [dLs], [dLs], scale=-C0)
            yield
            v4 = lambda t: t[:].rearrange("k h (p t) -> k h p t", p=NP)
            self.tt(KRf[:, :, :, 0, :], v4(B[1]), v4(sg), ALU.mult, [dB[1], dsg], [dKR], E=fw.pool)
            yield
            self.tt(KRf[:, :, :, 1, :], v4(B[0]), v4(Ep), ALU.mult, [dB[0], dEp], [dKR], E=fw.pool)
            yield
            self.tt(B[7][:], B[7][:], Ls[:], ALU.mult, [dB[7], dLs], [dB[7]], E=fw.pool)
            yield
            self.tt(B[3][:], B[3][:], Ls[:], ALU.mult, [dB[3], dLs], [dB[3]], E=fw.pool)
            yield

        def matrices(job, Z, pull, first_of_dir, diag=False):
            tok0, S, d, sbi = job
            nsb = S // SB
            B, dB = Z.B, Z.dB
            KRf, dKR = Z.KRf, Z.dKR
            Bf, dBf = B[7], dB[7]
            Kf, dKf = B[3], dB[3]
            Vf, dVf = B[2], dB[2]
            Ep, dEp = B[6], dB[6]
            yb, dyb = B[8], dB[8]
            t0 = sbi * SB
            gt0 = tok0 + t0
            if first_of_dir:
                for h in range(4):
                    self.memset(fw.pool, Tst[:, h, :], 0.0, [dT[h]])
            if d == 1:
                fw.dma(fw.sp, B[5][:], hkt(self.ys[:, gt0:gt0 + SB]), reads=[dys[gt0]], writes=[dB[5]], q="x")
            scope = (lambda nm: nc.named_scope(nm)) if diag else (lambda nm: contextlib.nullcontext())
            sc_ = scope("rwA_chunk"); sc_.__enter__()
            for h in range(4):
                for c in range(NCH):
                    ci = h * NCH + c
                    p, e = c // 2, c % 2
                    cs_ = slice(c * 64, (c + 1) * 64)
                    pt, dp = self.psr.nxt()
                    for j, (src, dsrc) in enumerate(((Bf, dBf), (Kf, dKf), (Vf, dVf))):
                        self.fw.op(fw.pe, lambda j=j, src=src: nc.tensor.transpose(pt[0:64, j * 64:(j + 1) * 64], src[:, h, cs_], id64), [dsrc, self.dcst], [dp])
                    self.cp(fw.act, tok3[ci][:], pt[0:64, 0:192], [dp], [dch[ci]])
                    khat = KRf[:, h, p, 0, e * 64:(e + 1) * 64]
                    rhat = KRf[:, h, p, 1, e * 64:(e + 1) * 64]
                    pc, dpc = self.psr.nxt()
                    self.mm(pc[0:64, 0:64], Bf[:, h, cs_], rhat, True, True, [dBf, dKR], [dpc])
                    self.mm(pc[0:64, 64:192].rearrange("p (a t) -> p a t", a=2), Kf[:, h, cs_], KRf[:, h, p, :, e * 64:(e + 1) * 64], True, True, [dKf, dKR], [dpc])
                    self.tt(Ach[ci][:], pc[0:64, 0:192], mD[d], ALU.mult, [dpc, self.dcst], [dch[ci]])
                    pull(1)
            sc_.__exit__(None, None, None); sc_ = scope("rwB_dbl"); sc_.__enter__()
            cur = [0] * NPI
            for h in range(4):
                for p in range(NP):
                    pi = h * NP + p
                    ps_ = slice(p * 128, (p + 1) * 128)
                    p1, dp1 = self.psr.nxt()
                    self.mm(p1[:, 0:128], Bf[:, h, ps_], KRf[:, h, p, 0, :], True, True, [dBf, dKR], [dp1])
                    self.tt(Xb[pi][0][:], p1[:, 0:128], mX[d], ALU.mult, [dp1, self.dcst], [dXb[pi][0]])
                    p3, dp3 = self.psr.nxt()
                    self.mm(p3[:, 0:128], KRf[:, h, p, 0, :], Bf[:, h, ps_], True, True, [dBf, dKR], [dp3])
                    self.tt(Xtb[pi][0][:], p3[:, 0:128], mXt[d], ALU.mult, [dp3, self.dcst], [dXtb[pi][0]])
                    self.tt(Accb[pi][0][:], Xb[pi][0][:], ident, ALU.add, [dXb[pi][0], self.dcst], [dAcc[pi][0]])
                    pull(1)
            acur = [0] * NPI
            for lev in range(1, 6):
                for pi in range(NPI):
                    k = cur[pi]
                    X, dX, Xt, dXt = Xb[pi][k], dXb[pi][k], Xtb[pi][k], dXtb[pi][k]
                    pxt, dpxt = self.psr.nxt()
                    self.mm(pxt[:, 0:128], X[:], Xt[:], True, True, [dX, dXt], [dpxt])
                    self.cp(fw.act if pi % 2 else fw.dve, Xtb[pi][1 - k][:], pxt[:, 0:128], [dpxt], [dXtb[pi][1 - k]])
                    if lev < 5:
                        px, dpx = self.psr.nxt()
                        self.mm(px[:, 0:128], Xt[:], X[:], True, True, [dX, dXt], [dpx])
                        self.cp(fw.dve if pi % 2 else fw.act, Xb[pi][1 - k][:], px[:, 0:128], [dpx], [dXb[pi][1 - k]])
                    cur[pi] = 1 - k
                    pull(1)
                for pi in range(NPI):
                    k = cur[pi]
                    ak = acur[pi]
                    pa, dpa = self.psr.nxt()
                    self.mm(pa[:, 0:128], identb[:], Accb[pi][ak][:], True, False, [dsm, dAcc[pi][ak]], [dpa])
                    self.mm(pa[:, 0:128], Xtb[pi][k][:], Accb[pi][ak][:], False, True, [dXtb[pi][k], dAcc[pi][ak]], [dpa])
                    if lev < 5:
                        self.cp(fw.act if (pi + lev) % 2 else fw.dve, Accb[pi][1 - ak][:], pa[:, 0:128], [dpa], [dAcc[pi][1 - ak]])
                        acur[pi] = 1 - ak
                    else:
                        self.cp(fw.act if (pi + lev) % 2 else fw.dve, MT[pi][:], pa[:, 0:128], [dpa], [dsl[pi]])
                    pull(1)
            for pi in range(NPI):
                psh, dpsh = self.psr.nxt()
                self.mm(psh[0:64, 0:64], self.cst[:, 11, 0:64], MT[pi][:, 64:128], True, True, [self.dcst, dsl[pi]], [dpsh])
                self.cp(fw.act, MT1[pi][:], psh[0:64, 0:64], [dpsh], [dsl[pi]])
            pull(2)
            sc_.__exit__(None, None, None); sc_ = scope("rwC_chain"); sc_.__enter__()
            for c in (range(NCH) if d == 0 else range(NCH - 1, -1, -1)):
                p, e = c // 2, c % 2
                cs = slice(c * 64, (c + 1) * 64)
                wcol = c * 64 + 63 if d == 0 else c * 64
                Gs, Us = [], []
                for h in range(4):
                    ci = h * NCH + c
                    khat = KRf[:, h, p, 0, e * 64:(e + 1) * 64]
                    pgc, dpg = self.psr.nxt()
                    self.mm(pgc[0:64, 0:64], khat, Tst[:, h, :], True, False, [dKR, dT[h]], [dpg])
                    self.mm(pgc[0:64, 0:64], Ach[ci][:, 64:128], tok3[ci][:, 128:192], False, True, [dch[ci]], [dpg])
                    G, dG = gsr.nxt()
                    self.aff(G[:], pgc[0:64, 0:64], -1.0, None, [dpg], [dG])
                    Gs.append((G, dG))
                pull(2)
                for h in range(4):
                    pi = h * NP + p
                    G, dG = Gs[h]
                    MTc = MT[pi][0:64, 0:64] if e == 0 else MT1[pi][:]
                    pu, dpu = self.psr.nxt()
                    self.mm(pu[0:64, 0:64], MTc, G[:], True, True, [dsl[pi], dG], [dpu])
                    U, dU = usr.nxt()
                    self.cp(fw.act, U[:], pu[0:64, 0:64], [dpu], [dU])
                    Us.append((U, dU))
                pull(2)
                for h in range(4):
                    ci = h * NCH + c
                    U, dU = Us[h]
                    rhat = KRf[:, h, p, 1, e * 64:(e + 1) * 64]
                    Btok, Ktok, Vtok = tok3[ci][:, 0:64], tok3[ci][:, 64:128], tok3[ci][:, 128:192]
                    ArbT, ArkT = Ach[ci][:, 0:64], Ach[ci][:, 128:192]
                    py, dpy = self.psr.nxt()
                    self.mm(py[0:64, 0:64], Tst[:, h, :], rhat, True, False, [dT[h], dKR], [dpy])
                    self.mm(py[0:64, 0:64], U[:], ArbT, False, False, [dU, dch[ci]], [dpy])
                    self.mm(py[0:64, 0:64], Vtok, ArkT, False, True, [dch[ci]], [dpy])
                    self.cp(fw.dve, yb[:, h, cs], py[0:64, 0:64], [dpy], [dyb])
                    ptn, dptn = self.psr.nxt()
                    self.mm(ptn[0:64, 0:64], Btok, U[:], True, False, [dch[ci], dU], [dptn])
                    self.mm(ptn[0:64, 0:64], Ktok, Vtok, False, True, [dch[ci]], [dptn])
                    TW, dTW = gsr.nxt()
                    self.aff(TW[:], Tst[:, h, :], Ep[:, h, wcol:wcol + 1], None, [dT[h], dEp], [dTW])
                    self.stt(Tst[:, h, :], ptn[0:64, 0:64], Ep[:, h, wcol:wcol + 1], TW[:], ALU.mult, ALU.add, [dptn, dEp, dTW], [dT[h]])
                pull(2)
            sc_.__exit__(None, None, None)
            if S == 2 * SEG and ((d == 0 and sbi == nsb // 2 - 1) or (d == 1 and sbi == nsb // 2)):
                for h in range(4):
                    self.ts(Tst[:, h, :], Tst[:, h, :], self.link[0:64, 0:1], None, ALU.mult, None, [dT[h], self.dlink], [dT[h]])
            if d == 0:
                dys[gt0] = Dep()
                fw.dma(fw.sp, hkt(self.ys[:, gt0:gt0 + SB]), yb[:], reads=[dyb], writes=[dys[gt0]], q="xo")
                return
            yf, dyf = B[5], dB[5]
            bon, dbon = B[0], dB[0]
            gg, dgg = B[1], dB[1]
            fw.dma(fw.sp, bon[:], hkt(ws["bon"][:, gt0:gt0 + SB]), writes=[dbon], q="x")
            fw.dma(fw.sp, gg[:], hkt(ws["g"][:, gt0:gt0 + SB]), writes=[dgg], q="x")
            self.tt(yf[:], yf[:], yb[:], ALU.add, [dyf, dyb], [dyf])
            for h in range(4):
                pm, dpm = self.psr.nxt()
                self.mm(pm[0:64, 0:SB], c64, yf[:, h, :], True, True, [dyf, self.dcst], [dpm])
                yc, dyc = tsm.nxt()
                self.tt(yc[:], yf[:, h, :], pm[0:64, 0:SB], ALU.subtract, [dyf, dpm], [dyc])
                sq, dsq = tsm.nxt()
                self.act(sq[:], yc[:], AF.Square, [dyc], [dsq])
                pv_, dpv_ = self.psr.nxt()
                self.mm(pv_[0:64, 0:SB], c64, sq[:], True, True, [dsq, self.dcst], [dpv_])
                self.act(sq[:], pv_[0:64, 0:SB], AF.Ln, [dpv_], [dsq], bias=GN_EPS)
                self.act(sq[:], sq[:], AF.Exp, [dsq], [dsq], scale=-0.5)
                self.stt(yc[:], yc[:], pr[:, h:h + 1], sq[:], ALU.mult, ALU.mult, [dyc, dsq, dpr], [dyc])
                self.stt(yc[:], yc[:], pr[:, 4 + h:5 + h], bon[:, h, :], ALU.add, ALU.add, [dyc, dbon, dpr], [dyc])
                self.tt(obuf[:, h, :], yc[:], gg[:, h, :], ALU.mult, [dyc, dgg], [dob])
                pull(1)
            fw.dma(fw.sp, hkt(self.os[384:640, gt0:gt0 + SB]), obuf[:], reads=[dob], q="xo")

        jobs = []
        for (tok0, nseg) in self.UNITS:
            S = nseg * SEG
            nsb = S // SB
            for d in range(2):
                order = range(nsb) if d == 0 else range(nsb - 1, -1, -1)
                for i, sbi in enumerate(order):
                    jobs.append(((tok0, S, d, sbi), i == 0))
        g0_ = elementwise(jobs[0][0], sets[0])
        for _ in g0_:
            pass
        for n, (job, first) in enumerate(jobs):
            nxt = elementwise(jobs[n + 1][0], sets[(n + 1) % 2]) if n + 1 < len(jobs) else iter(())

            def pull(k, g=nxt):
                for _ in range(k):
                    try:
                        next(g)
                    except StopIteration:
                        return
            matrices(job, sets[n % 2], pull, first, diag=(n == 5 and (self.debug or {}).get("diag")))
            for _ in nxt:
                pass


WSHAPES = {
    "w_ada": (2, 1024, 9216), "b_ada": (2, 9216), "norm_g": (2, 3, 1024),
    "ffn_w_in": (2, 2, 1024, 5632), "ffn_w_out": (2, 2, 2816, 1024),
    "w_mix_in": (2, 1024, 2432), "w_mix_out": (2, 1024, 1024),
    "lru_conv_w": (2, 4, 384), "lru_conv_b": (2, 384),
    "lru_w_gate_a": (2, 2, 6, 64, 64), "lru_b_gate_a": (2, 2, 384),
    "lru_w_gate_x": (2, 2, 6, 64, 64), "lru_b_gate_x": (2, 2, 384), "lru_lambda": (2, 2, 384),
    "rwkv_mu": (2, 1024), "rwkv_w_up": (2, 2, 64, 256), "rwkv_w0": (2, 2, 256),
    "rwkv_a_up": (2, 64, 256), "rwkv_a0": (2, 2, 256), "rwkv_g_up": (2, 128, 256),
    "rwkv_k_k": (2, 256), "rwkv_k_a": (2, 256), "rwkv_r_k": (2, 4, 64),
    "rwkv_ln_g": (2, 256), "rwkv_ln_b": (2, 256), "attn_q_norm": (2, 64), "attn_k_norm": (2, 64),
}


def rope_tables(positions):
    inv = (np.float32(10000.0) ** (-(np.arange(16, dtype=np.float32)) / np.float32(16))).astype(np.float32)
    row = (positions // 64).astype(np.float32)
    col = (positions % 64).astype(np.float32)
    cos = np.zeros((128, positions.shape[0]), np.float32)
    sin = np.zeros((128, positions.shape[0]), np.float32)
    for p in range(128):
        d = p % 64
        base = row if d < 32 else col
        ang = (base * inv[d % 16]).astype(np.float32)
        cos[p] = np.cos(ang)
        sin[p] = np.sin(ang)
    return cos, sin


def make_consts():
    c = np.zeros((14, 128, 128), np.float32)
    ii = np.arange(128)
    same = (ii[:, None] // 64) == (ii[None, :] // 64)
    su = (same & (ii[:, None] < ii[None, :])).astype(np.float32)
    iu = (same & (ii[:, None] <= ii[None, :])).astype(np.float32)
    c[4] = -su
    c[5] = iu
    c[6] = su
    c[7] = iu
    c[8] = -su.T
    c[9, :64, 0:64] = iu[:64, :64]
    c[9, :64, 64:128] = su[:64, :64]
    c[10, :64, 0:64] = iu[:64, :64]
    for i in range(64):
        c[11, 64 + i, i] = 1.0
    c[12, :64, 0:64] = iu[:64, :64].T
    c[12, :64, 64:128] = su[:64, :64].T
    c[13, :64, 0:64] = iu[:64, :64].T
    c[0] = np.eye(128, dtype=np.float32)
    R = np.zeros((128, 128), np.float32)
    for d in range(128):
        if d % 32 < 16:
            R[d, d + 16] = -1.0
        else:
            R[d, d - 16] = 1.0
    c[1] = R.T
    c[2, :64, :64] = 1.0 / 64
    c[2, 64:, 64:] = 1.0 / 64
    c[3] = 1.0
    return c


def core_layout(x_prompt, x_sample, c_prompt, c_sample):
    maps = []
    for core in range(NCORES):
        if core < 4:
            xs = [x_prompt[core, :SEG], x_prompt[core, SEG:], x_sample[core]]
            cs = [c_prompt[core], c_prompt[core], c_sample[core]]
            pos = np.concatenate([np.arange(2 * SEG), np.arange(SEG)])
            link = 1.0
        else:
            b = 4 + 3 * (core - 4)
            xs = [x_sample[b], x_sample[b + 1], x_sample[b + 2]]
            cs = [c_sample[b], c_sample[b + 1], c_sample[b + 2]]
            pos = np.concatenate([np.arange(SEG)] * 3)
            link = 0.0
        cos, sin = rope_tables(pos)
        maps.append({
            "x_in": np.ascontiguousarray(np.concatenate(xs, 0)),
            "c_in": np.ascontiguousarray(np.stack(cs, 0)),
            "link_in": np.full((128, 1), link, np.float32),
            "cos_in": cos, "sin_in": sin, "cst_in": make_consts(),
        })
    return maps


_NC_CACHE = {}


def kernel(**inputs):
    inputs = {k: np.asarray(v) for k, v in inputs.items()}
    if "nc" not in _NC_CACHE:
        _NC_CACHE["nc"] = K().build()
    nc = _NC_CACHE["nc"]
    maps = core_layout(inputs["x_prompt"], inputs["x_sample"], inputs["c_prompt"], inputs["c_sample"])
    for m in maps:
        for name in WSHAPES:
            m[name] = np.ascontiguousarray(inputs[name], dtype=np.float32)
    res = run_bass_kernel_spmd(nc, maps, core_ids=list(range(NCORES)))
    y_prompt = np.zeros((4, 2 * SEG, D), np.float32)
    y_sample = np.zeros((16, SEG, D), np.float32)
    for core in range(NCORES):
        y = res.results[core]["y_out"]
        if core < 4:
            y_prompt[core, :SEG] = y[:SEG]
            y_prompt[core, SEG:] = y[SEG:2 * SEG]
            y_sample[core] = y[2 * SEG:]
        else:
            b = 4 + 3 * (core - 4)
            for j in range(3):
                y_sample[b + j] = y[j * SEG:(j + 1) * SEG]
    return (y_prompt, y_sample)
```

```python
import contextlib
import numpy as np
import concourse.bass as bass
import concourse.mybir as mybir
from concourse.bass_utils import run_bass_kernel_spmd

F32 = mybir.dt.float32
BF16 = mybir.dt.bfloat16
AF = mybir.ActivationFunctionType
ALU = mybir.AluOpType

NCORES = 8
D = 1024
DFF = 2816
DIN = 2432
SEG = 2048
NSEG = 3
NT = NSEG * SEG
TT = 1024
NTILE = NT // TT
KC = D // 128
FC = DFF // 128
NZC = 20
DEPTH = 2
EPS = 1e-6


class Dep:
    __slots__ = ("w", "r")

    def __init__(self):
        self.w = None
        self.r = {}


class Eng:
    def __init__(self, fw, name, eng, is_pe=False):
        self.name = name
        self.e = eng
        self.is_pe = is_pe
        self.sem = fw.new_sem("s_" + name)
        self.cnt = 0
        self.known = {}


class FW:
    DMA_RING = 8

    def __init__(self, nc, stack):
        self.nc = nc
        self.stack = stack
        self.nsem = 0
        self.semobj = {}
        self.pe = Eng(self, "pe", nc.tensor, True)
        self.dve = Eng(self, "dve", nc.vector)
        self.act = Eng(self, "act", nc.scalar)
        self.pool = Eng(self, "pool", nc.gpsimd)
        self.sp = Eng(self, "sp", nc.sync)
        self.engs = [self.pe, self.dve, self.act, self.pool, self.sp]
        self.dmaq = {}
        self.ntens = 0
        self.ninst = 0

    def new_sem(self, name):
        s = self.stack.enter_context(self.nc.semaphore(name))
        self.nsem += 1
        self.semobj[id(s)] = s
        return s

    def sb(self, shape, dt=F32, stack=None):
        self.ntens += 1
        return (stack or self.stack).enter_context(self.nc.sbuf_tensor(f"t{self.ntens}", list(shape), dt))

    def ps(self, shape, dt=F32, stack=None):
        self.ntens += 1
        return (stack or self.stack).enter_context(self.nc.psum_tensor(f"p{self.ntens}", list(shape), dt))

    def _need(self, E, ev):
        if ev is None:
            return
        key, val = ev
        if key is E.sem and E.is_pe:
            return
        if E.known.get(id(key), 0) >= val:
            return
        E.e.wait_ge(key, val)
        self.ninst += 1
        E.known[id(key)] = val

    def _pre(self, E, reads, writes):
        for d in reads:
            self._need(E, d.w)
        for d in writes:
            self._need(E, d.w)
            for k, v in list(d.r.items()):
                self._need(E, (self.semobj[k], v))

    def _post(self, ev, reads, writes):
        key, val = ev
        for d in reads:
            if d.r.get(id(key), 0) < val:
                d.r[id(key)] = val
        for d in writes:
            d.w = ev
            d.r = {}

    def op(self, E, fn, reads=(), writes=()):
        self._pre(E, reads, writes)
        ins = fn()
        E.cnt += 1
        self.ninst += 1
        ins.then_inc(E.sem, 1)
        ev = (E.sem, E.cnt)
        self._post(ev, reads, writes)
        return ev

    def dma(self, E, out, in_, reads=(), writes=(), q="q", **kw):
        key = (E.name, q)
        if key not in self.dmaq:
            self.dmaq[key] = {"sems": [self.new_sem(f"d_{E.name}_{q}_{i}") for i in range(self.DMA_RING)], "n": 0}
        Q = self.dmaq[key]
        i = Q["n"]
        Q["n"] += 1
        sem = Q["sems"][i % self.DMA_RING]
        gen = i // self.DMA_RING
        if gen > 0:
            self._need(E, (sem, 16 * gen))
        self._pre(E, reads, writes)
        ins = E.e.dma_start(out=out, in_=in_, **kw)
        ins.then_inc(sem, 16)
        self.ninst += 1
        ev = (sem, 16 * (gen + 1))
        self._post(ev, reads, writes)
        return ev

    def all_events(self):
        evs = []
        for Q in self.dmaq.values():
            n = Q["n"]
            for s_i, sem in enumerate(Q["sems"]):
                cnt = (n - s_i + self.DMA_RING - 1) // self.DMA_RING if n > s_i else 0
                if cnt > 0:
                    evs.append((sem, 16 * cnt))
        for X in self.engs:
            if X.cnt > 0:
                evs.append((X.sem, X.cnt))
        return evs

    def barrier(self):
        evs = self.all_events()
        for E in self.engs:
            for ev in evs:
                self._need(E, ev)

    def finish(self):
        for ev in self.all_events():
            self._need(self.sp, ev)


class Ring:
    def __init__(self, items):
        self.items = [(t, Dep()) for t in items]
        self.i = 0

    def nxt(self):
        r = self.items[self.i % len(self.items)]
        self.i += 1
        return r


def zc_cols():
    ch = []
    for c in range(14):
        ch.append([(c * 128, 128)])
    for c in range(3):
        ch.append([(1792 + c * 128, 128)])
    ch.append([(2176, 64), (2176, 64)])
    ch.append([(2240, 64), (2240, 64)])
    ch.append([(2304, 128)])
    return ch


class K:
    def __init__(self, debug=False):
        self.debug = debug
        nc = self.nc = bass.Bass("TRN2", target_bir_lowering=False)
        dt = nc.dram_tensor
        self.x_in = dt("x_in", [NT, D], F32, kind="ExternalInput").ap()
        self.c_in = dt("c_in", [NSEG, D], F32, kind="ExternalInput").ap()
        self.link_in = dt("link_in", [128, 1], F32, kind="ExternalInput").ap()
        self.cos_in = dt("cos_in", [128, NT], F32, kind="ExternalInput").ap()
        self.sin_in = dt("sin_in", [128, NT], F32, kind="ExternalInput").ap()
        self.cst_in = dt("cst_in", [14, 128, 128], F32, kind="ExternalInput").ap()
        self.W = {}
        for name, shape in WSHAPES.items():
            self.W[name] = dt(name, list(shape), F32, kind="ExternalInput").ap()
        self.y_out = dt("y_out", [NT, D], F32, kind="ExternalOutput").ap()
        sk = "ExternalOutput" if debug else "Internal"
        self.xs = dt("xs", [D, NT], F32, kind=sk).ap()
        self.zs = dt("zs", [NZC * 128, NT], F32, kind="ExternalInput" if (debug and debug.get("mixer_test")) else sk).ap()
        self.os = dt("os", [D, NT], BF16, kind=sk).ap()
        self.ys = dt("ys", [256, NT], F32, kind="Internal").ap()
        self.ws = {nm: dt("ws_" + nm, [256, NT], F32, kind="Internal").ap() for nm in ("r", "kap", "v", "bon", "g", "sg0", "sg1", "kd0", "kd1", "b0", "b1")}

    def mm(self, out, lhsT, rhs, start, stop, reads, writes):
        nc = self.nc
        return self.fw.op(self.fw.pe, lambda: nc.tensor.matmul(out, lhsT, rhs, start=start, stop=stop), reads, writes)

    def act(self, out, in_, func, reads, writes, bias=None, scale=None):
        nc = self.nc
        kw = {}
        if bias is not None:
            kw["bias"] = bias
        if scale is not None:
            kw["scale"] = scale
        return self.fw.op(self.fw.act, lambda: nc.scalar.activation(out=out, in_=in_, func=func, **kw), reads, writes)

    def tt(self, out, in0, in1, op, reads, writes, E=None):
        E = E or self.fw.dve
        return self.fw.op(E, lambda: E.e.tensor_tensor(out=out, in0=in0, in1=in1, op=op), reads, writes)

    def ts(self, out, in0, s1, s2, op0, op1, reads, writes, E=None):
        E = E or self.fw.dve
        if op1 is None:
            return self.fw.op(E, lambda: E.e.tensor_scalar(out=out, in0=in0, scalar1=s1, scalar2=None, op0=op0), reads, writes)
        return self.fw.op(E, lambda: E.e.tensor_scalar(out=out, in0=in0, scalar1=s1, scalar2=s2, op0=op0, op1=op1), reads, writes)

    def stt(self, out, in0, scalar, in1, op0, op1, reads, writes):
        nc = self.nc
        return self.fw.op(self.fw.dve, lambda: nc.vector.scalar_tensor_tensor(out=out, in0=in0, scalar=scalar, in1=in1, op0=op0, op1=op1), reads, writes)

    def aff(self, out, in_, scale, bias, reads, writes):
        kw = {}
        if bias is not None:
            kw["bias"] = bias
        return self.fw.op(self.fw.act, lambda: self.nc.scalar.activation(out=out, in_=in_, func=AF.Identity, scale=scale, **kw), reads, writes)

    def cp(self, E, out, in_, reads, writes):
        if E is self.fw.act:
            return self.fw.op(E, lambda: self.nc.scalar.copy(out=out, in_=in_), reads, writes)
        return self.fw.op(E, lambda: E.e.tensor_copy(out=out, in_=in_), reads, writes)

    def memset(self, E, ap, val, writes):
        return self.fw.op(E, lambda: E.e.memset(ap, val), (), writes)

    def vec_load(self, dst, src_ap, dep):
        return self.fw.dma(self.fw.sp, dst, src_ap, writes=[dep], q="v", allow_slow_non_contiguous=True)

    def build(self):
        nc = self.nc
        with contextlib.ExitStack() as st:
            fw = self.fw = FW(nc, st)
            ps_all = fw.ps([128, 4096])
            self.psr = Ring([ps_all[:, i * 512:(i + 1) * 512] for i in range(6)])
            self.pso = Ring([ps_all[:, i * 512:(i + 1) * 512] for i in range(6, 8)])
            self.psw_views = [ps_all[:, j * 1024:(j + 1) * 1024] for j in range(3)]
            self.psq_views = [ps_all[:, j * 128:(j + 1) * 128] for j in range(16)]
            self.psh_views = [ps_all[:, 2048 + j * 256:2048 + (j + 1) * 256] for j in range(8)]
            self.setup_consts()
            if self.debug and self.debug.get("mixer_test"):
                with contextlib.ExitStack() as pst:
                    self.mixer_phase(pst, 0)
                fw.finish()
                return nc
            with nc.named_scope("phase0"):
                self.phase0()
            fw.barrier()
            for l in range(DEPTH):
                with contextlib.ExitStack() as pst, nc.named_scope(f"tok{l}"):
                    self.token_phase(pst, l)
                fw.barrier()
                if self.debug and self.debug.get("stop_after_A"):
                    break
                with contextlib.ExitStack() as pst:
                    self.mixer_phase(pst, l)
                fw.barrier()
            if not (self.debug and self.debug.get("stop_after_A")):
                with contextlib.ExitStack() as pst, nc.named_scope(f"tok{DEPTH}"):
                    self.token_phase(pst, DEPTH)
            fw.finish()
        return nc

    def setup_consts(self):
        fw = self.fw
        nc = self.nc
        self.cst = fw.sb([128, 14, 128])
        self.dcst = Dep()
        fw.dma(fw.sp, self.cst[:], self.cst_in.rearrange("c p n -> p c n"), writes=[self.dcst], q="v")
        self.ident = self.cst[:, 0, :]
        self.onesD = fw.sb([128, 128], BF16)
        self.blk64 = fw.sb([128, 128], BF16)
        self.dconst2 = Dep()
        self.ts(self.onesD[:], self.cst[:, 3, :], 1.0 / D, None, ALU.mult, None, [self.dcst], [self.dconst2])
        self.cp(fw.dve, self.blk64[:], self.cst[:, 2, :], [self.dcst], [self.dconst2])
        self.link = fw.sb([128, 1])
        self.dlink = Dep()
        fw.dma(fw.sp, self.link[:], self.link_in, writes=[self.dlink], q="v")

    def phase0(self):
        fw = self.fw
        nc = self.nc
        W = self.W
        self.mod = [fw.sb([128, 72, NSEG]) for _ in range(DEPTH)]
        self.dmod = [Dep() for _ in range(DEPTH)]
        self.gsc = [fw.sb([128, 3, KC, NSEG]) for _ in range(DEPTH)]
        self.gate = [fw.sb([128, 3, KC, NSEG]) for _ in range(DEPTH)]
        with contextlib.ExitStack() as pst:
            cT = fw.sb([128, KC, NSEG], stack=pst)
            dc = Dep()
            for s_ in range(NSEG):
                self.vec_load(cT[:, :, s_], self.c_in[s_].rearrange("(kc p) -> p kc", p=128), dc)
            sc = fw.sb([128, KC, NSEG], stack=pst)
            self.act(sc[:], cT[:], AF.Silu, [dc], [dc])
            wr = Ring([fw.sb([128, KC, 512], stack=pst) for _ in range(2)])
            for l in range(DEPTH):
                bT = fw.sb([128, 72], stack=pst)
                db = Dep()
                self.vec_load(bT[:], W["b_ada"][l].rearrange("(c p) -> p c", p=128), db)
                gT = fw.sb([128, 3, KC], stack=pst)
                dg = Dep()
                for i_ in range(3):
                    self.vec_load(gT[:, i_, :], W["norm_g"][l, i_].rearrange("(c p) -> p c", p=128), dg)
                wv = W["w_ada"][l].rearrange("(kc p) n -> p kc n", p=128)
                for blk in range(18):
                    wt, dw = wr.nxt()
                    fw.dma(fw.sp, wt[:], wv[:, :, blk * 512:(blk + 1) * 512], writes=[dw], q="w")
                    for j in range(4):
                        fcn = blk * 4 + j
                        pt, dp = self.psr.nxt()
                        for k in range(KC):
                            self.mm(pt[:, 0:NSEG], wt[:, k, j * 128:(j + 1) * 128], sc[:, k, :], k == 0, k == KC - 1, [dw, dc], [dp])
                        self.ts(self.mod[l][:, fcn, :], pt[:, 0:NSEG], bT[:, fcn:fcn + 1], None, ALU.add, None, [dp, db], [self.dmod[l]])
                m = self.mod[l]
                for i in range(3):
                    for s in range(NSEG):
                        self.stt(self.gsc[l][:, i, :, s], m[:, (3 * i + 1) * 8:(3 * i + 2) * 8, s], 1.0, gT[:, i, :], ALU.add, ALU.mult, [self.dmod[l], dg], [self.dmod[l]])
                    self.ts(self.gate[l][:, i, :, :], m[:, (3 * i + 2) * 8:(3 * i + 3) * 8, :], 1.0 if i == 1 else 0.5, None, ALU.mult, None, [self.dmod[l]], [self.dmod[l]])
            fw.barrier()

    def shift_ap(self, l, i, c, seg):
        return self.mod[l][:, 3 * i * 8 + c, seg:seg + 1]

    def token_phase(self, pst, l):
        fw = self.fw
        nc = self.nc
        self.xT = fw.sb([128, KC, TT], stack=pst)
        self.dx = Dep()
        self.hT = fw.sb([128, KC, TT], BF16, stack=pst)
        self.dh = Dep()
        self.aT = fw.sb([128, FC, TT], BF16, stack=pst)
        self.da = Dep()
        self.sqr = Ring([fw.sb([128, 512], BF16, stack=pst) for _ in range(3)])
        self.f32r = Ring([fw.sb([128, 512], F32, stack=pst) for _ in range(4)])
        self.rstd = fw.sb([128, 512], stack=pst)
        self.drstd = Dep()
        self.wr = Ring([fw.sb([128, KC, 512], BF16, stack=pst) for _ in range(3)])
        self.wor = Ring([fw.sb([128, FC, 256], BF16, stack=pst) for _ in range(2)])
        self.iot = Ring([fw.sb([128, 512], F32, stack=pst) for _ in range(3)])
        for ti in range(NTILE):
            seg = ti // (SEG // TT)
            t0 = ti * TT
            if l == 0:
                self.load_x_input(t0)
            else:
                fw.dma(fw.sp, self.xT[:], self.xs.rearrange("(c p) t -> p c t", p=128)[:, :, t0:t0 + TT], writes=[self.dx], q="x")
                fw.dma(fw.sp, self.hT[:], self.os.rearrange("(c p) t -> p c t", p=128)[:, :, t0:t0 + TT], writes=[self.dh], q="x")
                self.out_proj(self.hT, self.dh, KC, self.W["w_mix_out"][l - 1].rearrange("(c p) n -> p c n", p=128), l - 1, 1, seg)
                self.norm_mod(l - 1, 2, seg)
                self.ffn(l - 1, 1, 2, seg)
            if l < DEPTH:
                self.norm_mod(l, 0, seg)
                self.ffn(l, 0, 0, seg)
                fw.dma(fw.sp, self.xs.rearrange("(c p) t -> p c t", p=128)[:, :, t0:t0 + TT], self.xT[:], reads=[self.dx], q="xo")
                self.norm_mod(l, 1, seg)
                self.mix_in(l, t0)
            else:
                self.store_y(t0)

    def load_x_input(self, t0):
        fw = self.fw
        nc = self.nc
        for b in range(TT // 128):
            it, di = self.iot.nxt()
            it2, di2 = self.iot.nxt()
            for hf, (tt_, dd) in enumerate(((it, di), (it2, di2))):
                fw.dma(fw.sp, tt_[:], self.x_in[t0 + b * 128:t0 + (b + 1) * 128, hf * 512:(hf + 1) * 512], writes=[dd], q="x")
            for hf, (tt_, dd) in enumerate(((it, di), (it2, di2))):
                pt, dp = self.psr.nxt()
                for j in range(4):
                    self.fw.op(fw.pe, lambda j=j: nc.tensor.transpose(pt[:, j * 128:(j + 1) * 128], tt_[:, j * 128:(j + 1) * 128], self.ident), [dd, self.dcst], [dp])
                E = fw.act if hf == 0 else fw.dve
                self.cp(E, self.xT[:, hf * 4:(hf + 1) * 4, b * 128:(b + 1) * 128], pt[:].rearrange("p (j t) -> p j t", j=4), [dp], [self.dx])

    def store_y(self, t0):
        fw = self.fw
        nc = self.nc
        for b in range(TT // 128):
            for hf in range(2):
                pt, dp = self.psr.nxt()
                for j in range(4):
                    c = hf * 4 + j
                    self.fw.op(fw.pe, lambda j=j, c=c: nc.tensor.transpose(pt[:, j * 128:(j + 1) * 128], self.xT[:, c, b * 128:(b + 1) * 128], self.ident), [self.dx, self.dcst], [dp])
                ot, do = self.iot.nxt()
                E = fw.act if hf == 0 else fw.dve
                self.cp(E, ot[:], pt[:], [dp], [do])
                fw.dma(fw.sp, self.y_out[t0 + b * 128:t0 + (b + 1) * 128, hf * 512:(hf + 1) * 512], ot[:], reads=[do], q="xo")

    def norm_mod(self, l, i, seg):
        fw = self.fw
        for hf in range(TT // 512):
            sl = slice(hf * 512, (hf + 1) * 512)
            pt, dp = self.psr.nxt()
            for c in range(KC):
                sq, dsq = self.sqr.nxt()
                self.act(sq[:], self.xT[:, c, sl], AF.Square, [self.dx], [dsq])
                self.mm(pt[:], self.onesD[:], sq[:], c == 0, c == KC - 1, [dsq, self.dconst2], [dp])
            t1, d1 = self.f32r.nxt()
            self.act(t1[:], pt[:], AF.Ln, [dp], [d1], bias=EPS)
            self.act(self.rstd[:], t1[:], AF.Exp, [d1], [self.drstd], scale=-0.5)
            for c in range(KC):
                t2, d2 = self.f32r.nxt()
                self.stt(t2[:], self.xT[:, c, sl], self.gsc[l][:, i, c, seg:seg + 1], self.rstd[:], ALU.mult, ALU.mult, [self.dx, self.drstd, self.dmod[l]], [d2])
                self.act(self.hT[:, c, sl], t2[:], AF.Identity, [d2, self.dmod[l]], [self.dh], bias=self.shift_ap(l, i, c, seg))

    def ffn(self, l, which, i, seg):
        fw = self.fw
        nc = self.nc
        wv = self.W["ffn_w_in"][l, which].rearrange("(kc p) n -> p kc n", p=128)
        for j in range(FC // 2):
            wt, dw = self.wr.nxt()
            fw.dma(fw.pool, wt[:, :, 0:256], wv[:, :, j * 256:(j + 1) * 256], writes=[dw], q="w")
            fw.dma(fw.pool, wt[:, :, 256:512], wv[:, :, DFF + j * 256:DFF + (j + 1) * 256], writes=[dw], q="w")
            for fc in range(2):
                for hf in range(TT // 512):
                    sl = slice(hf * 512, (hf + 1) * 512)
                    pg, dpg = self.psr.nxt()
                    pu, dpu = self.psr.nxt()
                    for k in range(KC):
                        self.mm(pg[:], wt[:, k, fc * 128:(fc + 1) * 128], self.hT[:, k, sl], k == 0, k == KC - 1, [dw, self.dh], [dpg])
                    for k in range(KC):
                        self.mm(pu[:], wt[:, k, 256 + fc * 128:256 + (fc + 1) * 128], self.hT[:, k, sl], k == 0, k == KC - 1, [dw, self.dh], [dpu])
                    sg, dsg = self.f32r.nxt()
                    self.act(sg[:], pg[:], AF.Silu, [dpg], [dsg])
                    self.tt(self.aT[:, 2 * j + fc, sl], sg[:], pu[:], ALU.mult, [dsg, dpu], [self.da])
        self.out_proj(self.aT, self.da, FC, self.W["ffn_w_out"][l, which].rearrange("(c p) n -> p c n", p=128), l, i, seg)

    def out_proj(self, src, dsrc, nk, wv, l, i, seg):
        fw = self.fw
        for dp2 in range(KC // 2):
            wt, dw = self.wor.nxt()
            fw.dma(fw.pool, wt[:, 0:nk, :], wv[:, :, dp2 * 256:(dp2 + 1) * 256], writes=[dw], q="w")
            for dc in range(2):
                c = dp2 * 2 + dc
                for hf in range(TT // 512):
                    sl = slice(hf * 512, (hf + 1) * 512)
                    pt, dp = self.psr.nxt()
                    for k in range(nk):
                        self.mm(pt[:], wt[:, k, dc * 128:(dc + 1) * 128], src[:, k, sl], k == 0, k == nk - 1, [dw, dsrc], [dp])
                    self.stt(self.xT[:, c, sl], pt[:], self.gate[l][:, i, c, seg:seg + 1], self.xT[:, c, sl], ALU.mult, ALU.add, [dp, self.dx, self.dmod[l]], [self.dx])

    def mix_in(self, l, t0):
        fw = self.fw
        wv = self.W["w_mix_in"][l].rearrange("(kc p) n -> p kc n", p=128)
        for zc, cols in enumerate(zc_cols()):
            wt, dw = self.wr.nxt()
            o = 0
            for (c0, n) in cols:
                fw.dma(fw.pool, wt[:, :, o:o + n], wv[:, :, c0:c0 + n], writes=[dw], q="w")
                o += n
            for hf in range(TT // 512):
                sl = slice(hf * 512, (hf + 1) * 512)
                pt, dp = self.psr.nxt()
                for k in range(KC):
                    self.mm(pt[:], wt[:, k, 0:128], self.hT[:, k, sl], k == 0, k == KC - 1, [dw, self.dh], [dp])
                ot, do = self.iot.nxt()
                self.cp(fw.act if hf == 0 else fw.dve, ot[:], pt[:], [dp], [do])
                fw.dma(fw.sp, self.zs[zc * 128:(zc + 1) * 128, t0 + hf * 512:t0 + (hf + 1) * 512], ot[:], reads=[do], q="xo")

    def mixer_phase(self, pst, l):
        fw = self.fw
        which = (self.debug or {}).get("mixers", ("lru", "att", "rwkv"))
        if "lru" in which:
            with contextlib.ExitStack() as st2, self.nc.named_scope(f"lru{l}"):
                self.lru(st2, l)
            fw.barrier()
        if "att" in which:
            with contextlib.ExitStack() as st2, self.nc.named_scope(f"att{l}"):
                self.attention(st2, l)
            fw.barrier()
        if "rwkv" in which:
            with contextlib.ExitStack() as st2, self.nc.named_scope(f"rwkv{l}"):
                self.rwkv(st2, l)
            fw.barrier()

    UNITS = ((0, 2), (2 * SEG, 1))

    def lru(self, st, l):
        fw = self.fw
        nc = self.nc
        W = self.W
        SA = 2 * SEG
        pv = fw.sb([128, 3, 16], stack=st)
        dpv = Dep()
        col1 = lambda ap: ap.rearrange("(p o) -> p o", o=1)
        for c in range(3):
            cs = slice(c * 128, (c + 1) * 128)
            self.vec_load(pv[:, c, 0:4], W["lru_conv_w"][l][:, cs].rearrange("j p -> p j"), dpv)
            self.vec_load(pv[:, c, 4:5], col1(W["lru_conv_b"][l, cs]), dpv)
            for d in range(2):
                self.vec_load(pv[:, c, 5 + d:6 + d], col1(W["lru_b_gate_a"][l, d, cs]), dpv)
                self.vec_load(pv[:, c, 7 + d:8 + d], col1(W["lru_b_gate_x"][l, d, cs]), dpv)
                self.vec_load(pv[:, c, 9 + d:10 + d], col1(W["lru_lambda"][l, d, cs]), dpv)
        self.act(pv[:, :, 11:13], pv[:, :, 9:11], AF.Exp, [dpv], [dpv], scale=-1.0)
        self.act(pv[:, :, 11:13], pv[:, :, 11:13], AF.Ln, [dpv], [dpv], bias=1.0)
        self.ts(pv[:, :, 13:15], pv[:, :, 11:13], -16.0, None, ALU.mult, None, [dpv], [dpv])
        self.ts(pv[:, :, 11:13], pv[:, :, 11:13], -8.0, None, ALU.mult, None, [dpv], [dpv])
        w32 = fw.sb([128, 4, 128], stack=st)
        dw32 = Dep()
        wbd = fw.sb([128, 4, 128], BF16, stack=st)
        dwbd = Dep()
        xpad = fw.sb([128, 2, SEG + 3], stack=st)
        dxp = Dep()
        xc = fw.sb([128, SA], stack=st)
        dxc = Dep()
        xcb = fw.sb([128, SA], BF16, stack=st)
        dxcb = Dep()
        bufs = [fw.sb([128, SA], stack=st) for _ in range(5)]
        dbs = [Dep() for _ in range(5)]
        h0, h1 = bufs[3], bufs[4]
        dh0, dh1 = dbs[3], dbs[4]
        for c in range(3):
            cs = slice(c * 128, (c + 1) * 128)
            self.memset(fw.pool, w32[:], 0.0, [dw32])
            for d in range(2):
                for gi, nm in enumerate(("lru_w_gate_a", "lru_w_gate_x")):
                    for n in range(2):
                        fw.dma(fw.sp, w32[n * 64:(n + 1) * 64, d * 2 + gi, n * 64:(n + 1) * 64], W[nm][l, d, 2 * c + n], writes=[dw32], q="v")
            self.cp(fw.pool, wbd[:], w32[:], [dw32], [dwbd])
            for (tok0, nseg) in self.UNITS:
                S = nseg * SEG
                xp = xpad[:, 0:nseg, :]
                self.memset(fw.pool, xp[:, :, 0:2], 0.0, [dxp])
                self.memset(fw.pool, xp[:, :, SEG + 2:SEG + 3], 0.0, [dxp])
                fw.dma(fw.sp, xp[:, :, 2:SEG + 2], self.zs[cs, tok0:tok0 + S].rearrange("p (s t) -> p s t", s=nseg), writes=[dxp], q="x")
                if nseg == 2:
                    self.ts(xpad[:, 1, 0:2], xpad[:, 0, SEG:SEG + 2], self.link[:, 0:1], None, ALU.mult, None, [dxp, self.dlink], [dxp])
                    self.ts(xpad[:, 0, SEG + 2:SEG + 3], xpad[:, 1, 2:3], self.link[:, 0:1], None, ALU.mult, None, [dxp, self.dlink], [dxp])
                xc3 = xc[:, 0:S].rearrange("p (s t) -> p s t", s=nseg)
                self.ts(xc3, xp[:, :, 0:SEG], pv[:, c, 0:1], pv[:, c, 4:5], ALU.mult, ALU.add, [dxp, dpv], [dxc])
                for j in range(1, 4):
                    self.stt(xc3, xp[:, :, j:j + SEG], pv[:, c, j:j + 1], xc3, ALU.mult, ALU.add, [dxp, dpv, dxc], [dxc])
                self.cp(fw.act, xcb[:, 0:S], xc[:, 0:S], [dxc], [dxcb])
                for d in range(2):
                    hb, dhb = (h0, dh0) if d == 0 else (h1, dh1)
                    b1, b2, b3 = bufs[0:3]
                    d1, d2, d3 = dbs[0:3]
                    for gi, (bt, dbt) in enumerate(((b1, d1), (b2, d2))):
                        for blk in range(S // 512):
                            sl = slice(blk * 512, (blk + 1) * 512)
                            pt, dp = self.psr.nxt()
                            self.mm(pt[:], wbd[:, d * 2 + gi, :], xcb[:, sl], True, True, [dwbd, dxcb], [dp])
                            self.act(bt[:, sl], pt[:], AF.Sigmoid, [dp, dpv], [dbt], bias=pv[:, c, 5 + 2 * gi + d:6 + 2 * gi + d])
                    self.act(b3[:, 0:S], b1[:, 0:S], AF.Exp, [d1, dpv], [d3], scale=pv[:, c, 11 + d:12 + d])
                    self.act(b1[:, 0:S], b1[:, 0:S], AF.Exp, [d1, dpv], [d1], scale=pv[:, c, 13 + d:14 + d])
                    self.act(b1[:, 0:S], b1[:, 0:S], AF.Sqrt, [d1], [d1], scale=-1.0, bias=1.0)
                    self.tt(b2[:, 0:S], b2[:, 0:S], xc[:, 0:S], ALU.mult, [d2, dxc], [d2])
                    self.tt(b2[:, 0:S], b2[:, 0:S], b1[:, 0:S], ALU.mult, [d2, d1], [d2])
                    if nseg == 2:
                        cp_ = SEG if d == 0 else SEG - 1
                        self.ts(b3[:, cp_:cp_ + 1], b3[:, cp_:cp_ + 1], self.link[:, 0:1], None, ALU.mult, None, [d3, self.dlink], [d3])
                    if d == 0:
                        self.fw.op(fw.dve, lambda: nc.vector.tensor_tensor_scan(out=hb[:, 0:S], data0=b3[:, 0:S], data1=b2[:, 0:S], initial=0.0, op0=ALU.mult, op1=ALU.add), [d3, d2], [dhb])
                    else:
                        self.fw.op(fw.dve, lambda: nc.vector.tensor_tensor_scan(out=hb[:, S - 1::-1] if False else hb[:, 0:S][:, ::-1], data0=b3[:, 0:S][:, ::-1], data1=b2[:, 0:S][:, ::-1], initial=0.0, op0=ALU.mult, op1=ALU.add), [d3, d2], [dhb])
                b1, b2 = bufs[0], bufs[1]
                d1, d2 = dbs[0], dbs[1]
                fw.dma(fw.sp, b1[:, 0:S], self.zs[384 + c * 128:384 + (c + 1) * 128, tok0:tok0 + S], writes=[d1], q="x")
                self.act(b2[:, 0:S], b1[:, 0:S], AF.Square, [d1], [d2])
                self.ts(b2[:, 0:S], b2[:, 0:S], 0.044715, 1.0, ALU.mult, ALU.add, [d2], [d2])
                self.tt(b2[:, 0:S], b2[:, 0:S], b1[:, 0:S], ALU.mult, [d2, d1], [d2])
                self.act(b2[:, 0:S], b2[:, 0:S], AF.Sigmoid, [d2], [d2], scale=1.5957691216057308)
                self.tt(b1[:, 0:S], b1[:, 0:S], b2[:, 0:S], ALU.mult, [d1, d2], [d1])
                self.tt(h0[:, 0:S], h0[:, 0:S], h1[:, 0:S], ALU.add, [dh0, dh1], [dh0])
                self.tt(xcb[:, 0:S], h0[:, 0:S], b1[:, 0:S], ALU.mult, [dh0, d1], [dxcb])
                fw.dma(fw.sp, self.os[cs, tok0:tok0 + S], xcb[:, 0:S], reads=[dxcb], q="xo")

    def attention(self, st, l):
        fw = self.fw
        nc = self.nc
        W = self.W
        SA = 2 * SEG
        onesf = self.cst[:, 3, :]
        psw = Ring(self.psw_views)
        rotT = self.cst[:, 1, :]
        gv = fw.sb([128, 8], stack=st)
        dgv = Dep()
        col1 = lambda ap: ap.rearrange("(p o) -> p o", o=1)
        for hh in range(2):
            self.vec_load(gv[hh * 64:(hh + 1) * 64, 0:1], col1(W["attn_q_norm"][l]), dgv)
            self.vec_load(gv[hh * 64:(hh + 1) * 64, 1:2], col1(W["attn_k_norm"][l]), dgv)
        self.ts(gv[:, 0:1], gv[:, 0:1], 0.125, None, ALU.mult, None, [dgv], [dgv])
        rows = fw.sb([1, 132], stack=st)
        drw = Dep()
        fw.dma(fw.sp, rows[0:1, 0:64], W["attn_q_norm"][l].rearrange("(o d) -> o d", o=1), writes=[drw], q="v")
        fw.dma(fw.sp, rows[0:1, 64:128], W["attn_k_norm"][l].rearrange("(o d) -> o d", o=1), writes=[drw], q="v")
        self.fw.op(fw.dve, lambda: nc.vector.tensor_reduce(out=rows[0:1, 128:130], in_=rows[0:1, 0:128].rearrange("o (a d) -> o a d", a=2), axis=mybir.AxisListType.X, op=ALU.max, apply_absolute_value=True), [drw], [drw])
        self.tt(rows[0:1, 130:131], rows[0:1, 128:129], rows[0:1, 129:130], ALU.mult, [drw], [drw])
        self.cp(fw.dve, rows[0:1, 131:132], rows[0:1, 130:131], [drw], [drw])
        pt, dp = psw.nxt()
        self.mm(pt[:, 0:2], onesf[0:1, :], rows[0:1, 130:132], True, True, [drw, self.dcst], [dp])
        self.ts(gv[:, 2:3], pt[:, 0:1], -8.0, None, ALU.mult, None, [dp], [dgv])
        self.ts(gv[:, 3:4], self.link[:, 0:1], 30000.0, -30000.0, ALU.mult, ALU.add, [self.dlink], [dgv])
        self.tt(gv[:, 3:4], gv[:, 3:4], gv[:, 2:3], ALU.add, [dgv], [dgv])
        cosT = fw.sb([128, SA], stack=st)
        sinT = fw.sb([128, SA], stack=st)
        dcs = Dep()
        srcr = Ring([fw.sb([128, SA], stack=st) for _ in range(2)])
        kT = [[fw.sb([128, SA], BF16, stack=st) for _ in range(2)] for _ in range(2)]
        dkT = [Dep(), Dep()]
        for kv_ in range(2):
            self.memset(fw.pool, kT[kv_][0][64:128, :], 0.0, [dkT[kv_]])
            self.memset(fw.pool, kT[kv_][1][0:64, :], 0.0, [dkT[kv_]])
        vtok = fw.sb([128, SA // 128, 2, 192], BF16, stack=st)
        dvt = Dep()
        qT = fw.sb([128, SA], BF16, stack=st)
        dqT = Dep()
        och = fw.sb([128, SA], BF16, stack=st)
        doc = Dep()
        pTr = Ring([fw.sb([128, 1024], BF16, stack=st) for _ in range(4)])
        osbr = Ring([fw.sb([128, 512], stack=st) for _ in range(2)])
        lnrr = Ring([fw.sb([128, 512], stack=st) for _ in range(2)])
        tail = [None]
        sqr = Ring([fw.sb([128, 512], BF16, stack=st) for _ in range(2)])
        f32r = Ring([fw.sb([128, 512], stack=st) for _ in range(6)])
        osb = fw.sb([128, 512], stack=st)
        dosb = Dep()
        lnr = fw.sb([128, 512], stack=st)
        dlnr = Dep()
        self.memset(fw.pool, vtok[:], 0.0, [dvt])
        self.memset(fw.pool, vtok[:, :, :, 64:65], 1.0, [dvt])

        def norm_rope(src, dsrc, dst, ddst, gcol, S, tok0):
            for blk in range(S // 512):
                sl = slice(blk * 512, (blk + 1) * 512)
                sq, dsq = sqr.nxt()
                self.act(sq[:], src[:, sl], AF.Square, [dsrc], [dsq])
                p1, dp1 = psw.nxt()
                self.mm(p1[:, 0:512], self.blk64[:], sq[:], True, True, [dsq, self.dconst2], [dp1])
                t, dt_ = f32r.nxt()
                self.act(t[:], p1[:, 0:512], AF.Ln, [dp1], [dt_], bias=EPS)
                self.act(t[:], t[:], AF.Exp, [dt_], [dt_], scale=-0.5)
                qn, dqn = f32r.nxt()
                self.stt(qn[:], src[:, sl], gv[:, gcol:gcol + 1], t[:], ALU.mult, ALU.mult, [dsrc, dgv, dt_], [dqn])
                p2, dp2 = psw.nxt()
                self.mm(p2[:, 0:512], rotT, qn[:], True, True, [dqn, self.dcst], [dp2])
                t1, dt1 = f32r.nxt()
                self.tt(t1[:], qn[:], cosT[:, sl], ALU.mult, [dqn, dcs], [dt1], )
                self.tt(t[:], p2[:, 0:512], sinT[:, sl], ALU.mult, [dp2, dcs, dt_], [dt_])
                if isinstance(dst, list):
                    self.tt(dst[0][0:64, sl], t1[0:64, :], t[0:64, :], ALU.add, [dt1, dt_], [ddst])
                    self.tt(dst[1][64:128, sl], t1[64:128, :], t[64:128, :], ALU.add, [dt1, dt_], [ddst])
                else:
                    self.tt(dst[:, sl], t1[:], t[:], ALU.add, [dt1, dt_], [ddst])

        for (tok0, nseg) in self.UNITS:
            S = nseg * SEG
            fw.dma(fw.sp, cosT[:, 0:S], self.cos_in[:, tok0:tok0 + S], writes=[dcs], q="x")
            fw.dma(fw.sp, sinT[:, 0:S], self.sin_in[:, tok0:tok0 + S], writes=[dcs], q="x")
            for kv in range(2):
                src, dsrc = srcr.nxt()
                fw.dma(fw.sp, src[:, 0:S], self.zs[(17 + kv) * 128:(18 + kv) * 128, tok0:tok0 + S], writes=[dsrc], q="x")
                norm_rope(src, dsrc, kT[kv], dkT[kv], 1, S, tok0)
            src, dsrc = srcr.nxt()
            fw.dma(fw.sp, src[:, 0:S], self.zs[19 * 128:20 * 128, tok0:tok0 + S], writes=[dsrc], q="x")
            for b0 in range(0, S // 128, 4):
                pt, dp = psw.nxt()
                for j in range(4):
                    self.fw.op(fw.pe, lambda j=j: nc.tensor.transpose(pt[:, j * 128:(j + 1) * 128], src[:, (b0 + j) * 128:(b0 + j + 1) * 128], self.ident), [dsrc, self.dcst], [dp])
                pv4 = pt[:, 0:512].rearrange("p (j k d) -> p j k d", j=4, k=2)
                self.cp(fw.act, vtok[:, b0:b0 + 4, :, 0:64], pv4, [dp], [dvt])
                self.cp(fw.dve, vtok[:, b0:b0 + 4, :, 128:192], pv4, [dp], [dvt])
            for qc in range(3):
                src, dsrc = srcr.nxt()
                fw.dma(fw.sp, src[:, 0:S], self.zs[(14 + qc) * 128:(15 + qc) * 128, tok0:tok0 + S], writes=[dsrc], q="x")
                norm_rope(src, dsrc, qT, dqT, 0, S, tok0)
                for qb in range(S // 512):
                    qs = slice(qb * 512, (qb + 1) * 512)
                    for e in range(2):
                        h = 2 * qc + e
                        kv = h // 3
                        es = slice(e * 64, (e + 1) * 64)
                        po, dpo = self.pso.nxt()
                        nk = S // 128
                        LOOK = 2
                        pend = []
                        nk2 = nk // 2
                        for k2i in range(nk2 + LOOK):
                            if k2i < nk2:
                                ps_, dps = psw.nxt()
                                for u in range(2):
                                    kc = 2 * k2i + u
                                    self.mm(ps_[:, u * 512:(u + 1) * 512], kT[kv][e][:, kc * 128:(kc + 1) * 128], qT[:, qs], True, True, [dkT[kv], dqT], [dps])
                                pT, dpT = pTr.nxt()
                                same = (nseg == 1) or ((qb // 4) == ((2 * k2i) // 16))
                                bc = 2 if same else 3
                                self.act(pT[:], ps_[:], AF.Exp, [dps, dgv], [dpT], bias=gv[:, bc:bc + 1])
                                pend.append((pT, dpT, k2i))
                            if k2i == LOOK - 1 and tail[0] is not None:
                                tail[0]()
                                tail[0] = None
                            if k2i >= LOOK:
                                pT, dpT, kk2 = pend.pop(0)
                                for u in range(2):
                                    k2 = 2 * kk2 + u
                                    if e == 0:
                                        self.mm(po[0:65, :], vtok[:, k2, kv, 0:65], pT[:, u * 512:(u + 1) * 512], k2 == 0, k2 == nk - 1, [dvt, dpT], [dpo])
                                    else:
                                        self.mm(po[:, :], vtok[:, k2, kv, 64:192], pT[:, u * 512:(u + 1) * 512], k2 == 0, k2 == nk - 1, [dvt, dpT], [dpo])

                        def mk_tail(po=po, dpo=dpo, e=e, es=es, qs=qs):
                            def f():
                                r0 = 64 if e == 0 else 0
                                lnr, dlnr = lnrr.nxt()
                                self.act(lnr[r0:r0 + 1, :], po[r0:r0 + 1, :], AF.Ln, [dpo], [dlnr])
                                self.act(lnr[r0:r0 + 1, :], lnr[r0:r0 + 1, :], AF.Exp, [dlnr], [dlnr], scale=-1.0)
                                pb, dpb = psw.nxt()
                                self.mm(pb[:, 0:512], onesf[r0:r0 + 1, :], lnr[r0:r0 + 1, :], True, True, [dlnr, self.dcst], [dpb])
                                osb, dosb = osbr.nxt()
                                self.cp(fw.act, osb[es, :], po[es, :], [dpo], [dosb])
                                self.tt(och[es, qs], osb[es, :], pb[es, 0:512], ALU.mult, [dosb, dpb], [doc])
                            return f
                        tail[0] = mk_tail()
                if tail[0] is not None:
                    tail[0]()
                    tail[0] = None
                fw.dma(fw.sp, self.os[640 + qc * 128:640 + (qc + 1) * 128, tok0:tok0 + S], och[:, 0:S], reads=[doc], q="xo")

    def rwkv_pre(self, st, l):
        fw = self.fw
        nc = self.nc
        W = self.W
        ws = self.ws
        BL = 512
        col1 = lambda ap: ap.rearrange("(p o) -> p o", o=1)
        cp2 = lambda ap: ap.rearrange("(c p) -> p c", p=128)
        blk64f = self.cst[:, 2, :]
        pp = fw.sb([128, 64], stack=st)
        dpp = Dep()
        mu = W["rwkv_mu"][l]
        self.vec_load(pp[:, 0:8], cp2(mu), dpp)
        self.ts(pp[:, 8:16], pp[:, 0:8], 0.5, None, ALU.mult, None, [dpp], [dpp])
        self.ts(pp[:, 16:24], pp[:, 0:8], -1.0, 1.0, ALU.mult, ALU.add, [dpp], [dpp])
        for d in range(2):
            self.vec_load(pp[:, 24 + 2 * d:26 + 2 * d], cp2(W["rwkv_w0"][l, d]), dpp)
            self.vec_load(pp[:, 28 + 2 * d:30 + 2 * d], cp2(W["rwkv_a0"][l, d]), dpp)
        self.vec_load(pp[:, 32:34], cp2(W["rwkv_k_k"][l]), dpp)
        self.vec_load(pp[:, 34:36], cp2(W["rwkv_k_a"][l]), dpp)
        self.ts(pp[:, 36:38], pp[:, 34:36], -1.0, 1.0, ALU.mult, ALU.add, [dpp], [dpp])
        self.vec_load(pp[:, 38:40], cp2(W["rwkv_r_k"][l].rearrange("h k -> (h k)")), dpp)
        wup = fw.sb([64, 2, 256], stack=st)
        aup = fw.sb([128, 256], stack=st)
        gup = fw.sb([128, 256], stack=st)
        dwl = Dep()
        for d in range(2):
            fw.dma(fw.sp, wup[:, d, :], W["rwkv_w_up"][l, d], writes=[dwl], q="v")
        fw.dma(fw.sp, aup[64:128, :], W["rwkv_a_up"][l], writes=[dwl], q="v")
        fw.dma(fw.sp, gup[:], W["rwkv_g_up"][l], writes=[dwl], q="v")
        padr = Ring([fw.sb([128, 8, BL + 2], stack=st) for _ in range(2)])
        sR = Ring([fw.sb([128, 8, BL], stack=st) for _ in range(2)])
        fR = Ring([fw.sb([128, 8, BL], stack=st) for _ in range(2)])
        tR = Ring([fw.sb([128, BL], stack=st) for _ in range(12)])
        oR = Ring([fw.sb([128, BL], stack=st) for _ in range(10)])
        zv = self.zs[768:1792, :].rearrange("(c p) t -> p c t", p=128)
        for (tok0, nseg) in self.UNITS:
            S = nseg * SEG
            for bi in range(S // BL):
                t0 = bi * BL
                g0 = tok0 + t0
                sl = slice(g0, g0 + BL)
                lo = 1 if t0 == 0 else 0
                hi = 1 if t0 + BL == S else 0
                n = BL + 2 - lo - hi
                pad, dpad = padr.nxt()
                if lo:
                    self.memset(fw.pool, pad[:, :, 0:1], 0.0, [dpad])
                if hi:
                    self.memset(fw.pool, pad[:, :, BL + 1:BL + 2], 0.0, [dpad])
                fw.dma(fw.sp, pad[:, :, lo:lo + n], zv[:, :, g0 - 1 + lo:g0 - 1 + lo + n], writes=[dpad], q="x")
                if S == 2 * SEG and (t0 == SEG or t0 + BL == SEG):
                    cix = 0 if t0 == SEG else BL + 1
                    self.ts(pad[:, :, cix:cix + 1], pad[:, :, cix:cix + 1], self.link[:, 0:1], None, ALU.mult, None, [dpad, self.dlink], [dpad])
                s_, ds_ = sR.nxt()
                f, df = fR.nxt()
                for c in range(8):
                    self.tt(s_[:, c, :], pad[:, c, 0:BL], pad[:, c, 2:BL + 2], ALU.add, [dpad], [ds_], E=fw.pool if c % 2 else fw.dve)
                    self.aff(s_[:, c, :], s_[:, c, :], pp[:, 8 + c:9 + c], None, [ds_, dpp], [ds_])
                    self.stt(f[:, c, :], pad[:, c, 1:BL + 1], pp[:, 16 + c:17 + c], s_[:, c, :], ALU.mult, ALU.add, [dpad, ds_, dpp], [df])
                for fc in range(2):
                    fw.dma(fw.sp, ws["r"][fc * 128:(fc + 1) * 128, sl], f[:, fc, :], reads=[df], q="xo")
                    fw.dma(fw.sp, ws["v"][fc * 128:(fc + 1) * 128, sl], f[:, 4 + fc, :], reads=[df], q="xo")
                tw, dtw = tR.nxt()
                self.act(tw[0:64, :], f[0:64, 6, :], AF.Tanh, [df], [dtw])
                for d in range(2):
                    for fc in range(2):
                        pt, dp = self.psr.nxt()
                        self.mm(pt[:], wup[:, d, fc * 128:(fc + 1) * 128], tw[0:64, :], True, True, [dwl, dtw], [dp])
                        o, do = oR.nxt()
                        self.act(o[:], pt[:], AF.Sigmoid, [dp, dpp], [do], bias=pp[:, 24 + 2 * d + fc:25 + 2 * d + fc])
                        fw.dma(fw.sp, ws["sg%d" % d][fc * 128:(fc + 1) * 128, sl], o[:], reads=[do], q="xo")
                kaps = []
                for fc in range(2):
                    kk, dkk = tR.nxt()
                    self.aff(kk[:], f[:, 2 + fc, :], pp[:, 32 + fc:33 + fc], None, [df, dpp], [dkk])
                    sq, dsq = tR.nxt()
                    self.act(sq[:], kk[:], AF.Square, [dkk], [dsq])
                    pt, dp = self.psr.nxt()
                    self.mm(pt[:], blk64f, sq[:], True, True, [dsq, self.dcst], [dp])
                    self.act(sq[:], pt[:], AF.Ln, [dp], [dsq], scale=64.0, bias=1e-24)
                    self.act(sq[:], sq[:], AF.Exp, [dsq], [dsq], scale=-0.5)
                    kap, dkap = oR.nxt()
                    self.tt(kap[:], kk[:], sq[:], ALU.mult, [dkk, dsq], [dkap])
                    fw.dma(fw.sp, ws["kap"][fc * 128:(fc + 1) * 128, sl], kap[:], reads=[dkap], q="xo")
                    kaps.append((kap, dkap))
                for fc in range(2):
                    pt, dp = self.psr.nxt()
                    self.mm(pt[:], aup[64:128, fc * 128:(fc + 1) * 128], f[64:128, 6, :], True, True, [dwl, df], [dp])
                    for d in range(2):
                        a_, da_ = tR.nxt()
                        self.act(a_[:], pt[:], AF.Sigmoid, [dp, dpp], [da_], bias=pp[:, 28 + 2 * d + fc:29 + 2 * d + fc])
                        t_, dt_ = tR.nxt()
                        self.aff(t_[:], a_[:], pp[:, 34 + fc:35 + fc], pp[:, 36 + fc:37 + fc], [da_, dpp], [dt_])
                        kd, dkd = oR.nxt()
                        self.tt(kd[:], t_[:], f[:, 2 + fc, :], ALU.mult, [dt_, df], [dkd])
                        fw.dma(fw.sp, ws["kd%d" % d][fc * 128:(fc + 1) * 128, sl], kd[:], reads=[dkd], q="xo")
                        b_, db_ = oR.nxt()
                        self.tt(b_[:], a_[:], kaps[fc][0][:], ALU.mult, [da_, kaps[fc][1]], [db_], E=fw.pool)
                        fw.dma(fw.sp, ws["b%d" % d][fc * 128:(fc + 1) * 128, sl], b_[:], reads=[db_], q="xo")
                for fc in range(2):
                    rk, drk = tR.nxt()
                    self.stt(rk[:], f[:, fc, :], pp[:, 38 + fc:39 + fc], f[:, 2 + fc, :], ALU.mult, ALU.mult, [df, dpp], [drk])
                    pt, dp = self.psr.nxt()
                    self.mm(pt[:], blk64f, rk[:], True, True, [drk, self.dcst], [dp])
                    bo, dbo = oR.nxt()
                    self.stt(bo[:], pt[:], 64.0, f[:, 4 + fc, :], ALU.mult, ALU.mult, [dp, df], [dbo])
                    fw.dma(fw.sp, ws["bon"][fc * 128:(fc + 1) * 128, sl], bo[:], reads=[dbo], q="xo")
                sgx, dsgx = tR.nxt()
                self.act(sgx[:], f[:, 7, :], AF.Sigmoid, [df], [dsgx])
                for fc in range(2):
                    pt, dp = self.psr.nxt()
                    self.mm(pt[:], gup[:, fc * 128:(fc + 1) * 128], sgx[:], True, True, [dwl, dsgx], [dp])
                    go, dgo = oR.nxt()
                    self.cp(fw.act, go[:], pt[:], [dp], [dgo])
                    fw.dma(fw.sp, ws["g"][fc * 128:(fc + 1) * 128, sl], go[:], reads=[dgo], q="xo")

    def rwkv(self, st0, l):
        fw = self.fw
        with contextlib.ExitStack() as st1, self.nc.named_scope(f"rwpre{l}"):
            self.rwkv_pre(st1, l)
        fw.barrier()
        with contextlib.ExitStack() as st, self.nc.named_scope(f"rwscan{l}"):
            self.rwkv_scan(st, l)

    def rwkv_scan(self, st, l):
        fw = self.fw
        nc = self.nc
        W = self.W
        ws = self.ws
        SB = 256
        NP = 2
        NCH = 4
        C0 = -0.6065306597126334
        GN_EPS = 64e-5
        ident = self.ident
        id64 = self.cst[0:64, 0, 0:64]
        c64 = self.cst[0:64, 2, 0:64]
        hk = lambda ap: ap.rearrange("(h k) -> k h", k=64)
        hkt = lambda ap: ap.rearrange("(h k) t -> k h t", k=64)
        flat2 = lambda ap: ap.rearrange("p a b -> p (a b)")
        mX = [self.cst[:, 4, :], self.cst[:, 8, :]]
        mXt = [self.cst[:, 8, :], self.cst[:, 4, :]]
        mD = [flat2(self.cst[0:64, 9:11, :])[:, 0:192], flat2(self.cst[0:64, 12:14, :])[:, 0:192]]
        pr = fw.sb([64, 8], stack=st)
        dpr = Dep()
        self.vec_load(pr[:, 0:4], hk(W["rwkv_ln_g"][l]), dpr)
        self.vec_load(pr[:, 4:8], hk(W["rwkv_ln_b"][l]), dpr)
        smask = [fw.sb([64, 4, SB], stack=st) for _ in range(2)]
        dsm = Dep()
        for d in range(2):
            self.memset(fw.pool, smask[d][:], 1.0, [dsm])
            z0 = 0 if d == 0 else 63
            self.memset(fw.pool, smask[d][:].rearrange("k h (c t) -> k (h c) t", t=64)[:, :, z0:z0 + 1], 0.0, [dsm])
        tsm = Ring([fw.sb([64, SB], stack=st) for _ in range(6)])

        class Set:
            pass
        sets = []
        for _ in range(2):
            S_ = Set()
            S_.B = [fw.sb([64, 4, SB], stack=st) for _ in range(9)]
            S_.dB = [Dep() for _ in range(9)]
            S_.KRf = fw.sb([64, 4, NP, 2, 128], BF16, stack=st)
            S_.dKR = Dep()
            S_.Bfb, S_.Kfb, S_.Vb = [fw.sb([64, 4, SB], BF16, stack=st) for _ in range(3)]
            S_.dBfb, S_.dKfb, S_.dVb = Dep(), Dep(), Dep()
            sets.append(S_)
        Tst = fw.sb([64, 4, 64], stack=st)
        dT = [Dep() for _ in range(4)]
        Tb = fw.sb([64, 4, 64], BF16, stack=st)
        dTb = [Dep() for _ in range(4)]
        shb = fw.sb([128, 64], BF16, stack=st)
        twr = Ring([fw.sb([64, 64], stack=st) for _ in range(8)])
        NPI = 4 * NP
        NCI = 4 * NCH
        tok3 = [fw.sb([64, 192], BF16, stack=st) for _ in range(NCI)]
        Ach = [fw.sb([64, 192], BF16, stack=st) for _ in range(NCI)]
        dch = [Dep() for _ in range(NCI)]
        dAch = [Dep() for _ in range(NCI)]
        MT = [fw.sb([128, 128], BF16, stack=st) for _ in range(NPI)]
        MT1 = [fw.sb([64, 64], BF16, stack=st) for _ in range(NPI)]
        dsl = [Dep() for _ in range(NPI)]
        Xb = [[fw.sb([128, 128], BF16, stack=st) for _ in range(2)] for _ in range(NPI)]
        Xtb = [[fw.sb([128, 128], BF16, stack=st) for _ in range(2)] for _ in range(NPI)]
        Accb = [[fw.sb([128, 128], BF16, stack=st) for _ in range(2)] for _ in range(NPI)]
        dAcc = [[Dep(), Dep()] for _ in range(NPI)]
        identb = fw.sb([128, 128], BF16, stack=st)
        self.cp(fw.dve, identb[:], ident, [self.dcst], [dsm])
        self.cp(fw.dve, shb[:], self.cst[:, 11, 0:64], [self.dcst], [dsm])
        dXb = [[Dep(), Dep()] for _ in range(NPI)]
        dXtb = [[Dep(), Dep()] for _ in range(NPI)]
        gsr = Ring([fw.sb([64, 64], BF16, stack=st) for _ in range(8)])
        usr = Ring([fw.sb([64, 64], BF16, stack=st) for _ in range(8)])
        obuf = fw.sb([64, 4, SB], BF16, stack=st)
        dob = Dep()
        dys = {}

        def elementwise(job, Z):
            tok0, S, d, sbi = job
            B, dB = Z.B, Z.dB
            KRf, dKR = Z.KRf, Z.dKR
            g0 = tok0 + sbi * SB
            sl = slice(g0, g0 + SB)
            for bi, nm in ((0, "r"), (1, "kap"), (2, "v"), (3, "kd%d" % d), (4, "sg%d" % d), (7, "b%d" % d)):
                fw.dma(fw.sp, B[bi][:], hkt(ws[nm][:, sl]), writes=[dB[bi]], q="x")
            yield
            for _ in range(20):
                yield
            sg, dsg = B[4], dB[4]
            Ls, dLs = B[5], dB[5]
            flat = lambda t: t[:].rearrange("k h t -> k (h t)")
            rvf = (lambda ap: ap) if d == 0 else (lambda ap: ap[:, ::-1])
            self.fw.op(fw.dve, lambda: nc.vector.tensor_tensor_scan(out=rvf(flat(Ls)), data0=rvf(flat(smask[d])), data1=rvf(flat(sg)), initial=0.0, op0=ALU.mult, op1=ALU.add), [dsm, dsg], [dLs])
            yield
            self.tt(sg[:], Ls[:], sg[:], ALU.subtract, [dLs, dsg], [dsg], E=fw.pool)
            self.act(sg[:], sg[:], AF.Exp, [dsg], [dsg], scale=C0)
            yield
            Ep, dEp = B[6], dB[6]
            self.act(Ep[:], Ls[:], AF.Exp, [dLs], [dEp], scale=C0)
            self.act(Ls[:], Ls[:], AF.Exp, [dLs], [dLs], scale=-C0)
            yield
            v4 = lambda t: t[:].rearrange("k h (p t) -> k h p t", p=NP)
            self.tt(KRf[:, :, :, 0, :], v4(B[1]), v4(sg), ALU.mult, [dB[1], dsg], [dKR], E=fw.pool)
            yield
            self.tt(KRf[:, :, :, 1, :], v4(B[0]), v4(Ep), ALU.mult, [dB[0], dEp], [dKR], E=fw.pool)
            yield
            self.tt(Z.Bfb[:], B[7][:], Ls[:], ALU.mult, [dB[7], dLs], [Z.dBfb], E=fw.pool)
            yield
            self.tt(Z.Kfb[:], B[3][:], Ls[:], ALU.mult, [dB[3], dLs], [Z.dKfb], E=fw.pool)
            self.cp(fw.pool, Z.Vb[:], B[2][:], [dB[2]], [Z.dVb])
            yield

        def matrices(job, Z, pull, first_of_dir, diag=False):
            tok0, S, d, sbi = job
            nsb = S // SB
            B, dB = Z.B, Z.dB
            KRf, dKR = Z.KRf, Z.dKR
            Bf, dBf = Z.Bfb, Z.dBfb
            Kf, dKf = Z.Kfb, Z.dKfb
            Vf, dVf = Z.Vb, Z.dVb
            Ep, dEp = B[6], dB[6]
            yb, dyb = B[8], dB[8]
            t0 = sbi * SB
            gt0 = tok0 + t0
            if first_of_dir:
                for h in range(4):
                    self.memset(fw.pool, Tst[:, h, :], 0.0, [dT[h]])
                    self.memset(fw.pool, Tb[:, h, :], 0.0, [dTb[h]])
            if d == 1:
                fw.dma(fw.sp, B[5][:], hkt(self.ys[:, gt0:gt0 + SB]), reads=[dys[gt0]], writes=[dB[5]], q="x")
            scope = (lambda nm: nc.named_scope(nm)) if diag else (lambda nm: contextlib.nullcontext())
            sc_ = scope("rwA_chunk"); sc_.__enter__()
            for h in range(4):
                for c in range(NCH):
                    ci = h * NCH + c
                    cs_ = slice(c * 64, (c + 1) * 64)
                    pt, dp = self.psr.nxt()
                    for j, (src, dsrc) in enumerate(((Bf, dBf), (Kf, dKf), (Vf, dVf))):
                        self.fw.op(fw.pe, lambda j=j, src=src: nc.tensor.transpose(pt[0:64, 0:96].bitcast(BF16)[:, j * 64:(j + 1) * 64], src[:, h, cs_], identb[0:64, 0:64]), [dsrc, dsm], [dp])
                    self.cp(fw.act, tok3[ci][:], pt[0:64, 0:96].bitcast(BF16), [dp], [dch[ci]])
                    pull(1)
            for h in range(4):
                for c in range(NCH):
                    ci = h * NCH + c
                    p, e = c // 2, c % 2
                    cs_ = slice(c * 64, (c + 1) * 64)
                    rhat = KRf[:, h, p, 1, e * 64:(e + 1) * 64]
                    pc, dpc = self.psr.nxt()
                    self.mm(pc[0:64, 0:64], Bf[:, h, cs_], rhat, True, True, [dBf, dKR], [dpc])
                    self.mm(pc[0:64, 64:192].rearrange("p (a t) -> p a t", a=2), Kf[:, h, cs_], KRf[:, h, p, :, e * 64:(e + 1) * 64], True, True, [dKf, dKR], [dpc])
                    self.tt(Ach[ci][:], pc[0:64, 0:192], mD[d], ALU.mult, [dpc, self.dcst], [dAch[ci]])
            sc_.__exit__(None, None, None); sc_ = scope("rwB_dbl"); sc_.__enter__()
            cur = [0] * NPI
            for h in range(4):
                for p in range(NP):
                    pi = h * NP + p
                    ps_ = slice(p * 128, (p + 1) * 128)
                    p1, dp1 = self.psr.nxt()
                    self.mm(p1[:, 0:128], Bf[:, h, ps_], KRf[:, h, p, 0, :], True, True, [dBf, dKR], [dp1])
                    self.tt(Xb[pi][0][:], p1[:, 0:128], mX[d], ALU.mult, [dp1, self.dcst], [dXb[pi][0]])
                    p3, dp3 = self.psr.nxt()
                    self.mm(p3[:, 0:128], KRf[:, h, p, 0, :], Bf[:, h, ps_], True, True, [dBf, dKR], [dp3])
                    self.tt(Xtb[pi][0][:], p3[:, 0:128], mXt[d], ALU.mult, [dp3, self.dcst], [dXtb[pi][0]])
                    self.tt(Accb[pi][0][:], Xb[pi][0][:], ident, ALU.add, [dXb[pi][0], self.dcst], [dAcc[pi][0]])
                    pull(1)
            acur = [0] * NPI
            for lev in range(1, 6):
                for pi in range(NPI):
                    k = cur[pi]
                    X, dX, Xt, dXt = Xb[pi][k], dXb[pi][k], Xtb[pi][k], dXtb[pi][k]
                    pxt, dpxt = self.psr.nxt()
                    self.mm(pxt[:, 0:128], X[:], Xt[:], True, True, [dX, dXt], [dpxt])
                    self.cp(fw.act if pi % 2 else fw.dve, Xtb[pi][1 - k][:], pxt[:, 0:128], [dpxt], [dXtb[pi][1 - k]])
                    if lev < 5:
                        px, dpx = self.psr.nxt()
                        self.mm(px[:, 0:128], Xt[:], X[:], True, True, [dX, dXt], [dpx])
                        self.cp(fw.dve if pi % 2 else fw.act, Xb[pi][1 - k][:], px[:, 0:128], [dpx], [dXb[pi][1 - k]])
                    cur[pi] = 1 - k
                    pull(1)
                for pi in range(NPI):
                    k = cur[pi]
                    ak = acur[pi]
                    pa, dpa = self.psr.nxt()
                    self.mm(pa[:, 0:128], identb[:], Accb[pi][ak][:], True, False, [dsm, dAcc[pi][ak]], [dpa])
                    self.mm(pa[:, 0:128], Xtb[pi][k][:], Accb[pi][ak][:], False, True, [dXtb[pi][k], dAcc[pi][ak]], [dpa])
                    if lev < 5:
                        self.cp(fw.act if (pi + lev) % 2 else fw.dve, Accb[pi][1 - ak][:], pa[:, 0:128], [dpa], [dAcc[pi][1 - ak]])
                        acur[pi] = 1 - ak
                    else:
                        self.cp(fw.act if (pi + lev) % 2 else fw.dve, MT[pi][:], pa[:, 0:128], [dpa], [dsl[pi]])
                    pull(1)
            for pi in range(NPI):
                psh, dpsh = self.psr.nxt()
                self.mm(psh[0:64, 0:64], shb[:], MT[pi][:, 64:128], True, True, [dsm, dsl[pi]], [dpsh])
                self.cp(fw.act, MT1[pi][:], psh[0:64, 0:64], [dpsh], [dsl[pi]])
            pull(2)
            sc_.__exit__(None, None, None); sc_ = scope("rwC_chain"); sc_.__enter__()
            for c in (range(NCH) if d == 0 else range(NCH - 1, -1, -1)):
                p, e = c // 2, c % 2
                cs = slice(c * 64, (c + 1) * 64)
                wcol = c * 64 + 63 if d == 0 else c * 64
                Gs, Us = [], []
                for h in range(4):
                    ci = h * NCH + c
                    khat = KRf[:, h, p, 0, e * 64:(e + 1) * 64]
                    pgc, dpg = self.psr.nxt()
                    self.mm(pgc[0:64, 0:64], khat, Tb[:, h, :], True, False, [dKR, dTb[h]], [dpg])
                    self.mm(pgc[0:64, 0:64], Ach[ci][:, 64:128], tok3[ci][:, 128:192], False, True, [dch[ci], dAch[ci]], [dpg])
                    G, dG = gsr.nxt()
                    self.aff(G[:], pgc[0:64, 0:64], -1.0, None, [dpg], [dG])
                    Gs.append((G, dG))
                pull(2)
                for h in range(4):
                    pi = h * NP + p
                    G, dG = Gs[h]
                    MTc = MT[pi][0:64, 0:64] if e == 0 else MT1[pi][:]
                    pu, dpu = self.psr.nxt()
                    self.mm(pu[0:64, 0:64], MTc, G[:], True, True, [dsl[pi], dG], [dpu])
                    U, dU = usr.nxt()
                    self.cp(fw.act, U[:], pu[0:64, 0:64], [dpu], [dU])
                    Us.append((U, dU))
                pull(2)
                for h in range(4):
                    ci = h * NCH + c
                    U, dU = Us[h]
                    rhat = KRf[:, h, p, 1, e * 64:(e + 1) * 64]
                    Btok, Ktok, Vtok = tok3[ci][:, 0:64], tok3[ci][:, 64:128], tok3[ci][:, 128:192]
                    ArbT, ArkT = Ach[ci][:, 0:64], Ach[ci][:, 128:192]
                    py, dpy = self.psr.nxt()
                    self.mm(py[0:64, 0:64], Tb[:, h, :], rhat, True, False, [dTb[h], dKR], [dpy])
                    self.mm(py[0:64, 0:64], U[:], ArbT, False, False, [dU, dAch[ci]], [dpy])
                    self.mm(py[0:64, 0:64], Vtok, ArkT, False, True, [dch[ci], dAch[ci]], [dpy])
                    self.cp(fw.dve, yb[:, h, cs], py[0:64, 0:64], [dpy], [dyb])
                    ptn, dptn = self.psr.nxt()
                    self.mm(ptn[0:64, 0:64], Btok, U[:], True, False, [dch[ci], dU], [dptn])
                    self.mm(ptn[0:64, 0:64], Ktok, Vtok, False, True, [dch[ci]], [dptn])
                    TW, dTW = twr.nxt()
                    self.aff(TW[:], Tst[:, h, :], Ep[:, h, wcol:wcol + 1], None, [dT[h], dEp], [dTW])
                    self.stt(Tb[:, h, :], ptn[0:64, 0:64], Ep[:, h, wcol:wcol + 1], TW[:], ALU.mult, ALU.add, [dptn, dEp, dTW], [dTb[h]])
                    self.stt(Tst[:, h, :], ptn[0:64, 0:64], Ep[:, h, wcol:wcol + 1], TW[:], ALU.mult, ALU.add, [dptn, dEp, dTW], [dT[h]])
                pull(2)
            sc_.__exit__(None, None, None)
            if S == 2 * SEG and ((d == 0 and sbi == nsb // 2 - 1) or (d == 1 and sbi == nsb // 2)):
                for h in range(4):
                    self.ts(Tst[:, h, :], Tst[:, h, :], self.link[0:64, 0:1], None, ALU.mult, None, [dT[h], self.dlink], [dT[h]])
                    self.ts(Tb[:, h, :], Tb[:, h, :], self.link[0:64, 0:1], None, ALU.mult, None, [dTb[h], self.dlink], [dTb[h]])
            if d == 0:
                dys[gt0] = Dep()
                fw.dma(fw.sp, hkt(self.ys[:, gt0:gt0 + SB]), yb[:], reads=[dyb], writes=[dys[gt0]], q="xo")
                return
            yf, dyf = B[5], dB[5]
            bon, dbon = B[0], dB[0]
            gg, dgg = B[1], dB[1]
            fw.dma(fw.sp, bon[:], hkt(ws["bon"][:, gt0:gt0 + SB]), writes=[dbon], q="x")
            fw.dma(fw.sp, gg[:], hkt(ws["g"][:, gt0:gt0 + SB]), writes=[dgg], q="x")
            self.tt(yf[:], yf[:], yb[:], ALU.add, [dyf, dyb], [dyf])
            for h in range(4):
                pm, dpm = self.psr.nxt()
                self.mm(pm[0:64, 0:SB], c64, yf[:, h, :], True, True, [dyf, self.dcst], [dpm])
                yc, dyc = tsm.nxt()
                self.tt(yc[:], yf[:, h, :], pm[0:64, 0:SB], ALU.subtract, [dyf, dpm], [dyc])
                sq, dsq = tsm.nxt()
                self.act(sq[:], yc[:], AF.Square, [dyc], [dsq])
                pv_, dpv_ = self.psr.nxt()
                self.mm(pv_[0:64, 0:SB], c64, sq[:], True, True, [dsq, self.dcst], [dpv_])
                self.act(sq[:], pv_[0:64, 0:SB], AF.Ln, [dpv_], [dsq], bias=GN_EPS)
                self.act(sq[:], sq[:], AF.Exp, [dsq], [dsq], scale=-0.5)
                self.stt(yc[:], yc[:], pr[:, h:h + 1], sq[:], ALU.mult, ALU.mult, [dyc, dsq, dpr], [dyc])
                self.stt(yc[:], yc[:], pr[:, 4 + h:5 + h], bon[:, h, :], ALU.add, ALU.add, [dyc, dbon, dpr], [dyc])
                self.tt(obuf[:, h, :], yc[:], gg[:, h, :], ALU.mult, [dyc, dgg], [dob])
                pull(1)
            fw.dma(fw.sp, hkt(self.os[384:640, gt0:gt0 + SB]), obuf[:], reads=[dob], q="xo")

        jobs = []
        for (tok0, nseg) in self.UNITS:
            S = nseg * SEG
            nsb = S // SB
            for d in range(2):
                order = range(nsb) if d == 0 else range(nsb - 1, -1, -1)
                for i, sbi in enumerate(order):
                    jobs.append(((tok0, S, d, sbi), i == 0))
        g0_ = elementwise(jobs[0][0], sets[0])
        for _ in g0_:
            pass
        for n, (job, first) in enumerate(jobs):
            nxt = elementwise(jobs[n + 1][0], sets[(n + 1) % 2]) if n + 1 < len(jobs) else iter(())

            def pull(k, g=nxt):
                for _ in range(k):
                    try:
                        next(g)
                    except StopIteration:
                        return
            matrices(job, sets[n % 2], pull, first, diag=(n == 5 and (self.debug or {}).get("diag")))
            for _ in nxt:
                pass


WSHAPES = {
    "w_ada": (2, 1024, 9216), "b_ada": (2, 9216), "norm_g": (2, 3, 1024),
    "ffn_w_in": (2, 2, 1024, 5632), "ffn_w_out": (2, 2, 2816, 1024),
    "w_mix_in": (2, 1024, 2432), "w_mix_out": (2, 1024, 1024),
    "lru_conv_w": (2, 4, 384), "lru_conv_b": (2, 384),
    "lru_w_gate_a": (2, 2, 6, 64, 64), "lru_b_gate_a": (2, 2, 384),
    "lru_w_gate_x": (2, 2, 6, 64, 64), "lru_b_gate_x": (2, 2, 384), "lru_lambda": (2, 2, 384),
    "rwkv_mu": (2, 1024), "rwkv_w_up": (2, 2, 64, 256), "rwkv_w0": (2, 2, 256),
    "rwkv_a_up": (2, 64, 256), "rwkv_a0": (2, 2, 256), "rwkv_g_up": (2, 128, 256),
    "rwkv_k_k": (2, 256), "rwkv_k_a": (2, 256), "rwkv_r_k": (2, 4, 64),
    "rwkv_ln_g": (2, 256), "rwkv_ln_b": (2, 256), "attn_q_norm": (2, 64), "attn_k_norm": (2, 64),
}


def rope_tables(positions):
    inv = (np.float32(10000.0) ** (-(np.arange(16, dtype=np.float32)) / np.float32(16))).astype(np.float32)
    row = (positions // 64).astype(np.float32)
    col = (positions % 64).astype(np.float32)
    cos = np.zeros((128, positions.shape[0]), np.float32)
    sin = np.zeros((128, positions.shape[0]), np.float32)
    for p in range(128):
        d = p % 64
        base = row if d < 32 else col
        ang = (base * inv[d % 16]).astype(np.float32)
        cos[p] = np.cos(ang)
        sin[p] = np.sin(ang)
    return cos, sin


def make_consts():
    c = np.zeros((14, 128, 128), np.float32)
    ii = np.arange(128)
    same = (ii[:, None] // 64) == (ii[None, :] // 64)
    su = (same & (ii[:, None] < ii[None, :])).astype(np.float32)
    iu = (same & (ii[:, None] <= ii[None, :])).astype(np.float32)
    c[4] = -su
    c[5] = iu
    c[6] = su
    c[7] = iu
    c[8] = -su.T
    c[9, :64, 0:64] = iu[:64, :64]
    c[9, :64, 64:128] = su[:64, :64]
    c[10, :64, 0:64] = iu[:64, :64]
    for i in range(64):
        c[11, 64 + i, i] = 1.0
    c[12, :64, 0:64] = iu[:64, :64].T
    c[12, :64, 64:128] = su[:64, :64].T
    c[13, :64, 0:64] = iu[:64, :64].T
    c[0] = np.eye(128, dtype=np.float32)
    R = np.zeros((128, 128), np.float32)
    for d in range(128):
        if d % 32 < 16:
            R[d, d + 16] = -1.0
        else:
            R[d, d - 16] = 1.0
    c[1] = R.T
    c[2, :64, :64] = 1.0 / 64
    c[2, 64:, 64:] = 1.0 / 64
    c[3] = 1.0
    return c


def core_layout(x_prompt, x_sample, c_prompt, c_sample):
    maps = []
    for core in range(NCORES):
        if core < 4:
            xs = [x_prompt[core, :SEG], x_prompt[core, SEG:], x_sample[core]]
            cs = [c_prompt[core], c_prompt[core], c_sample[core]]
            pos = np.concatenate([np.arange(2 * SEG), np.arange(SEG)])
            link = 1.0
        else:
            b = 4 + 3 * (core - 4)
            xs = [x_sample[b], x_sample[b + 1], x_sample[b + 2]]
            cs = [c_sample[b], c_sample[b + 1], c_sample[b + 2]]
            pos = np.concatenate([np.arange(SEG)] * 3)
            link = 0.0
        cos, sin = rope_tables(pos)
        maps.append({
            "x_in": np.ascontiguousarray(np.concatenate(xs, 0)),
            "c_in": np.ascontiguousarray(np.stack(cs, 0)),
            "link_in": np.full((128, 1), link, np.float32),
            "cos_in": cos, "sin_in": sin, "cst_in": make_consts(),
        })
    return maps


_NC_CACHE = {}


def kernel(**inputs):
    inputs = {k: np.asarray(v) for k, v in inputs.items()}
    if "nc" not in _NC_CACHE:
        _NC_CACHE["nc"] = K().build()
    nc = _NC_CACHE["nc"]
    maps = core_layout(inputs["x_prompt"], inputs["x_sample"], inputs["c_prompt"], inputs["c_sample"])
    for m in maps:
        for name in WSHAPES:
            m[name] = np.ascontiguousarray(inputs[name], dtype=np.float32)
    res = run_bass_kernel_spmd(nc, maps, core_ids=list(range(NCORES)))
    y_prompt = np.zeros((4, 2 * SEG, D), np.float32)
    y_sample = np.zeros((16, SEG, D), np.float32)
    for core in range(NCORES):
        y = res.results[core]["y_out"]
        if core < 4:
            y_prompt[core, :SEG] = y[:SEG]
            y_prompt[core, SEG:] = y[SEG:2 * SEG]
            y_sample[core] = y[2 * SEG:]
        else:
            b = 4 + 3 * (core - 4)
            for j in range(3):
                y_sample[b + j] = y[j * SEG:(j + 1) * SEG]
    return (y_prompt, y_sample)
```

```python
import contextlib
import numpy as np
import concourse.bass as bass
import concourse.mybir as mybir
from concourse.bass_utils import run_bass_kernel_spmd

F32 = mybir.dt.float32
BF16 = mybir.dt.bfloat16
AF = mybir.ActivationFunctionType
ALU = mybir.AluOpType

NCORES = 8
D = 1024
DFF = 2816
DIN = 2432
SEG = 2048
NSEG = 3
NT = NSEG * SEG
TT = 1024
NTILE = NT // TT
KC = D // 128
FC = DFF // 128
NZC = 20
DEPTH = 2
EPS = 1e-6


class Dep:
    __slots__ = ("w", "r")

    def __init__(self):
        self.w = None
        self.r = {}


class Eng:
    def __init__(self, fw, name, eng, is_pe=False):
        self.name = name
        self.e = eng
        self.is_pe = is_pe
        self.sem = fw.new_sem("s_" + name)
        self.cnt = 0
        self.known = {}


class FW:
    DMA_RING = 8

    def __init__(self, nc, stack):
        self.nc = nc
        self.stack = stack
        self.nsem = 0
        self.semobj = {}
        self.pe = Eng(self, "pe", nc.tensor, True)
        self.dve = Eng(self, "dve", nc.vector)
        self.act = Eng(self, "act", nc.scalar)
        self.pool = Eng(self, "pool", nc.gpsimd)
        self.sp = Eng(self, "sp", nc.sync)
        self.engs = [self.pe, self.dve, self.act, self.pool, self.sp]
        self.dmaq = {}
        self.ntens = 0
        self.ninst = 0

    def new_sem(self, name):
        s = self.stack.enter_context(self.nc.semaphore(name))
        self.nsem += 1
        self.semobj[id(s)] = s
        return s

    def sb(self, shape, dt=F32, stack=None):
        self.ntens += 1
        return (stack or self.stack).enter_context(self.nc.sbuf_tensor(f"t{self.ntens}", list(shape), dt))

    def ps(self, shape, dt=F32, stack=None):
        self.ntens += 1
        return (stack or self.stack).enter_context(self.nc.psum_tensor(f"p{self.ntens}", list(shape), dt))

    def _need(self, E, ev):
        if ev is None:
            return
        key, val = ev
        if key is E.sem and E.is_pe:
            return
        if E.known.get(id(key), 0) >= val:
            return
        E.e.wait_ge(key, val)
        self.ninst += 1
        E.known[id(key)] = val

    def _pre(self, E, reads, writes):
        for d in reads:
            self._need(E, d.w)
        for d in writes:
            self._need(E, d.w)
            for k, v in list(d.r.items()):
                self._need(E, (self.semobj[k], v))

    def _post(self, ev, reads, writes):
        key, val = ev
        for d in reads:
            if d.r.get(id(key), 0) < val:
                d.r[id(key)] = val
        for d in writes:
            d.w = ev
            d.r = {}

    def op(self, E, fn, reads=(), writes=()):
        self._pre(E, reads, writes)
        ins = fn()
        E.cnt += 1
        self.ninst += 1
        ins.then_inc(E.sem, 1)
        ev = (E.sem, E.cnt)
        self._post(ev, reads, writes)
        return ev

    def dma(self, E, out, in_, reads=(), writes=(), q="q", **kw):
        key = (E.name, q)
        if key not in self.dmaq:
            self.dmaq[key] = {"sems": [self.new_sem(f"d_{E.name}_{q}_{i}") for i in range(self.DMA_RING)], "n": 0}
        Q = self.dmaq[key]
        i = Q["n"]
        Q["n"] += 1
        sem = Q["sems"][i % self.DMA_RING]
        gen = i // self.DMA_RING
        if gen > 0:
            self._need(E, (sem, 16 * gen))
        self._pre(E, reads, writes)
        ins = E.e.dma_start(out=out, in_=in_, **kw)
        ins.then_inc(sem, 16)
        self.ninst += 1
        ev = (sem, 16 * (gen + 1))
        self._post(ev, reads, writes)
        return ev

    def all_events(self):
        evs = []
        for Q in self.dmaq.values():
            n = Q["n"]
            for s_i, sem in enumerate(Q["sems"]):
                cnt = (n - s_i + self.DMA_RING - 1) // self.DMA_RING if n > s_i else 0
                if cnt > 0:
                    evs.append((sem, 16 * cnt))
        for X in self.engs:
            if X.cnt > 0:
                evs.append((X.sem, X.cnt))
        return evs

    def barrier(self):
        evs = self.all_events()
        for E in self.engs:
            for ev in evs:
                self._need(E, ev)

    def finish(self):
        for ev in self.all_events():
            self._need(self.sp, ev)


class Ring:
    def __init__(self, items):
        self.items = [(t, Dep()) for t in items]
        self.i = 0

    def nxt(self):
        r = self.items[self.i % len(self.items)]
        self.i += 1
        return r


def zc_cols():
    ch = []
    for c in range(14):
        ch.append([(c * 128, 128)])
    for c in range(3):
        ch.append([(1792 + c * 128, 128)])
    ch.append([(2176, 64), (2176, 64)])
    ch.append([(2240, 64), (2240, 64)])
    ch.append([(2304, 128)])
    return ch


class K:
    def __init__(self, debug=False):
        self.debug = debug
        nc = self.nc = bass.Bass("TRN2", target_bir_lowering=False)
        dt = nc.dram_tensor
        self.x_in = dt("x_in", [NT, D], F32, kind="ExternalInput").ap()
        self.c_in = dt("c_in", [NSEG, D], F32, kind="ExternalInput").ap()
        self.link_in = dt("link_in", [128, 1], F32, kind="ExternalInput").ap()
        self.cos_in = dt("cos_in", [128, NT], F32, kind="ExternalInput").ap()
        self.sin_in = dt("sin_in", [128, NT], F32, kind="ExternalInput").ap()
        self.cst_in = dt("cst_in", [14, 128, 128], F32, kind="ExternalInput").ap()
        self.W = {}
        for name, shape in WSHAPES.items():
            self.W[name] = dt(name, list(shape), F32, kind="ExternalInput").ap()
        self.y_out = dt("y_out", [NT, D], F32, kind="ExternalOutput").ap()
        sk = "ExternalOutput" if debug else "Internal"
        self.xs = dt("xs", [D, NT], F32, kind=sk).ap()
        self.zs = dt("zs", [NZC * 128, NT], F32, kind="ExternalInput" if (debug and debug.get("mixer_test")) else sk).ap()
        self.os = dt("os", [D, NT], BF16, kind=sk).ap()
        self.ys = dt("ys", [256, NT], F32, kind="Internal").ap()
        self.ws = {nm: dt("ws_" + nm, [256, NT], F32, kind="Internal").ap() for nm in ("r", "kap", "v", "bon", "g", "sg0", "sg1", "kd0", "kd1", "b0", "b1")}

    def mm(self, out, lhsT, rhs, start, stop, reads, writes):
        nc = self.nc
        return self.fw.op(self.fw.pe, lambda: nc.tensor.matmul(out, lhsT, rhs, start=start, stop=stop), reads, writes)

    def act(self, out, in_, func, reads, writes, bias=None, scale=None):
        nc = self.nc
        kw = {}
        if bias is not None:
            kw["bias"] = bias
        if scale is not None:
            kw["scale"] = scale
        return self.fw.op(self.fw.act, lambda: nc.scalar.activation(out=out, in_=in_, func=func, **kw), reads, writes)

    def tt(self, out, in0, in1, op, reads, writes, E=None):
        E = E or self.fw.dve
        return self.fw.op(E, lambda: E.e.tensor_tensor(out=out, in0=in0, in1=in1, op=op), reads, writes)

    def ts(self, out, in0, s1, s2, op0, op1, reads, writes, E=None):
        E = E or self.fw.dve
        if op1 is None:
            return self.fw.op(E, lambda: E.e.tensor_scalar(out=out, in0=in0, scalar1=s1, scalar2=None, op0=op0), reads, writes)
        return self.fw.op(E, lambda: E.e.tensor_scalar(out=out, in0=in0, scalar1=s1, scalar2=s2, op0=op0, op1=op1), reads, writes)

    def stt(self, out, in0, scalar, in1, op0, op1, reads, writes):
        nc = self.nc
        return self.fw.op(self.fw.dve, lambda: nc.vector.scalar_tensor_tensor(out=out, in0=in0, scalar=scalar, in1=in1, op0=op0, op1=op1), reads, writes)

    def aff(self, out, in_, scale, bias, reads, writes):
        kw = {}
        if bias is not None:
            kw["bias"] = bias
        return self.fw.op(self.fw.act, lambda: self.nc.scalar.activation(out=out, in_=in_, func=AF.Identity, scale=scale, **kw), reads, writes)

    def cp(self, E, out, in_, reads, writes):
        if E is self.fw.act:
            return self.fw.op(E, lambda: self.nc.scalar.copy(out=out, in_=in_), reads, writes)
        return self.fw.op(E, lambda: E.e.tensor_copy(out=out, in_=in_), reads, writes)

    def memset(self, E, ap, val, writes):
        return self.fw.op(E, lambda: E.e.memset(ap, val), (), writes)

    def vec_load(self, dst, src_ap, dep):
        return self.fw.dma(self.fw.sp, dst, src_ap, writes=[dep], q="v", allow_slow_non_contiguous=True)

    def build(self):
        nc = self.nc
        with contextlib.ExitStack() as st:
            fw = self.fw = FW(nc, st)
            ps_all = fw.ps([128, 4096])
            self.psr = Ring([ps_all[:, i * 512:(i + 1) * 512] for i in range(6)])
            self.pso = Ring([ps_all[:, i * 512:(i + 1) * 512] for i in range(6, 8)])
            self.psw_views = [ps_all[:, j * 1024:(j + 1) * 1024] for j in range(3)]
            self.psq_views = [ps_all[:, j * 128:(j + 1) * 128] for j in range(16)]
            self.psh_views = [ps_all[:, 2048 + j * 256:2048 + (j + 1) * 256] for j in range(8)]
            self.setup_consts()
            if self.debug and self.debug.get("mixer_test"):
                with contextlib.ExitStack() as pst:
                    self.mixer_phase(pst, 0)
                fw.finish()
                return nc
            with nc.named_scope("phase0"):
                self.phase0()
            fw.barrier()
            for l in range(DEPTH):
                with contextlib.ExitStack() as pst, nc.named_scope(f"tok{l}"):
                    self.token_phase(pst, l)
                fw.barrier()
                if self.debug and self.debug.get("stop_after_A"):
                    break
                with contextlib.ExitStack() as pst:
                    self.mixer_phase(pst, l)
                fw.barrier()
            if not (self.debug and self.debug.get("stop_after_A")):
                with contextlib.ExitStack() as pst, nc.named_scope(f"tok{DEPTH}"):
                    self.token_phase(pst, DEPTH)
            fw.finish()
        return nc

    def setup_consts(self):
        fw = self.fw
        nc = self.nc
        self.cst = fw.sb([128, 14, 128])
        self.dcst = Dep()
        fw.dma(fw.sp, self.cst[:], self.cst_in.rearrange("c p n -> p c n"), writes=[self.dcst], q="v")
        self.ident = self.cst[:, 0, :]
        self.onesD = fw.sb([128, 128], BF16)
        self.blk64 = fw.sb([128, 128], BF16)
        self.dconst2 = Dep()
        self.ts(self.onesD[:], self.cst[:, 3, :], 1.0 / D, None, ALU.mult, None, [self.dcst], [self.dconst2])
        self.cp(fw.dve, self.blk64[:], self.cst[:, 2, :], [self.dcst], [self.dconst2])
        self.link = fw.sb([128, 1])
        self.dlink = Dep()
        fw.dma(fw.sp, self.link[:], self.link_in, writes=[self.dlink], q="v")

    def phase0(self):
        fw = self.fw
        nc = self.nc
        W = self.W
        self.mod = [fw.sb([128, 72, NSEG]) for _ in range(DEPTH)]
        self.dmod = [Dep() for _ in range(DEPTH)]
        self.gsc = [fw.sb([128, 3, KC, NSEG]) for _ in range(DEPTH)]
        self.gate = [fw.sb([128, 3, KC, NSEG]) for _ in range(DEPTH)]
        with contextlib.ExitStack() as pst:
            cT = fw.sb([128, KC, NSEG], stack=pst)
            dc = Dep()
            for s_ in range(NSEG):
                self.vec_load(cT[:, :, s_], self.c_in[s_].rearrange("(kc p) -> p kc", p=128), dc)
            sc = fw.sb([128, KC, NSEG], stack=pst)
            self.act(sc[:], cT[:], AF.Silu, [dc], [dc])
            wr = Ring([fw.sb([128, KC, 512], stack=pst) for _ in range(4)])
            for l in range(DEPTH):
                bT = fw.sb([128, 72], stack=pst)
                db = Dep()
                self.vec_load(bT[:], W["b_ada"][l].rearrange("(c p) -> p c", p=128), db)
                gT = fw.sb([128, 3, KC], stack=pst)
                dg = Dep()
                for i_ in range(3):
                    self.vec_load(gT[:, i_, :], W["norm_g"][l, i_].rearrange("(c p) -> p c", p=128), dg)
                wv = W["w_ada"][l].rearrange("(kc p) n -> p kc n", p=128)
                for blk in range(18):
                    wt, dw = wr.nxt()
                    fw.dma(fw.sp if blk % 2 == 0 else fw.act, wt[:], wv[:, :, blk * 512:(blk + 1) * 512], writes=[dw], q="w")
                    for j in range(4):
                        fcn = blk * 4 + j
                        pt, dp = self.psr.nxt()
                        for k in range(KC):
                            self.mm(pt[:, 0:NSEG], wt[:, k, j * 128:(j + 1) * 128], sc[:, k, :], k == 0, k == KC - 1, [dw, dc], [dp])
                        self.ts(self.mod[l][:, fcn, :], pt[:, 0:NSEG], bT[:, fcn:fcn + 1], None, ALU.add, None, [dp, db], [self.dmod[l]])
                m = self.mod[l]
                for i in range(3):
                    for s in range(NSEG):
                        self.stt(self.gsc[l][:, i, :, s], m[:, (3 * i + 1) * 8:(3 * i + 2) * 8, s], 1.0, gT[:, i, :], ALU.add, ALU.mult, [self.dmod[l], dg], [self.dmod[l]])
                    self.ts(self.gate[l][:, i, :, :], m[:, (3 * i + 2) * 8:(3 * i + 3) * 8, :], 1.0 if i == 1 else 0.5, None, ALU.mult, None, [self.dmod[l]], [self.dmod[l]])
            fw.barrier()

    def shift_ap(self, l, i, c, seg):
        return self.mod[l][:, 3 * i * 8 + c, seg:seg + 1]

    def token_phase(self, pst, l):
        fw = self.fw
        nc = self.nc
        self.xT = fw.sb([128, KC, TT], stack=pst)
        self.dx = Dep()
        self.hT = fw.sb([128, KC, TT], BF16, stack=pst)
        self.dh = Dep()
        self.oT = fw.sb([128, KC, TT], BF16, stack=pst)
        self.do_ = Dep()
        self.aT = fw.sb([128, FC, TT], BF16, stack=pst)
        self.da = Dep()
        self.sqr = Ring([fw.sb([128, 512], BF16, stack=pst) for _ in range(3)])
        self.f32r = Ring([fw.sb([128, 512], F32, stack=pst) for _ in range(4)])
        self.rstd = fw.sb([128, 512], stack=pst)
        self.drstd = Dep()
        self.wr = Ring([fw.sb([128, KC, 512], BF16, stack=pst) for _ in range(3)])
        self.wor = Ring([fw.sb([128, FC, 256], BF16, stack=pst) for _ in range(2)])
        self.iot = Ring([fw.sb([128, 512], F32, stack=pst) for _ in range(3)])
        for ti in range(NTILE):
            seg = ti // (SEG // TT)
            t0 = ti * TT
            if l == 0:
                self.load_x_input(t0)
            else:
                fw.dma(fw.sp, self.xT[:], self.xs.rearrange("(c p) t -> p c t", p=128)[:, :, t0:t0 + TT], writes=[self.dx], q="x")
                fw.dma(fw.sp, self.oT[:], self.os.rearrange("(c p) t -> p c t", p=128)[:, :, t0:t0 + TT], writes=[self.do_], q="x")
                self.out_proj(self.oT, self.do_, KC, self.W["w_mix_out"][l - 1].rearrange("(c p) n -> p c n", p=128), l - 1, 1, seg)
                self.norm_mod(l - 1, 2, seg)
                self.ffn(l - 1, 1, 2, seg)
            if l < DEPTH:
                self.norm_mod(l, 0, seg)
                self.ffn(l, 0, 0, seg)
                fw.dma(fw.act, self.xs.rearrange("(c p) t -> p c t", p=128)[:, :, t0:t0 + TT], self.xT[:], reads=[self.dx], q="xo")
                self.norm_mod(l, 1, seg)
                self.mix_in(l, t0)
            else:
                self.store_y(t0)

    def load_x_input(self, t0):
        fw = self.fw
        nc = self.nc
        for b in range(TT // 128):
            it, di = self.iot.nxt()
            it2, di2 = self.iot.nxt()
            for hf, (tt_, dd) in enumerate(((it, di), (it2, di2))):
                fw.dma(fw.sp, tt_[:], self.x_in[t0 + b * 128:t0 + (b + 1) * 128, hf * 512:(hf + 1) * 512], writes=[dd], q="x")
            for hf, (tt_, dd) in enumerate(((it, di), (it2, di2))):
                pt, dp = self.psr.nxt()
                for j in range(4):
                    self.fw.op(fw.pe, lambda j=j: nc.tensor.transpose(pt[:, j * 128:(j + 1) * 128], tt_[:, j * 128:(j + 1) * 128], self.ident), [dd, self.dcst], [dp])
                E = fw.act if hf == 0 else fw.dve
                self.cp(E, self.xT[:, hf * 4:(hf + 1) * 4, b * 128:(b + 1) * 128], pt[:].rearrange("p (j t) -> p j t", j=4), [dp], [self.dx])

    def store_y(self, t0):
        fw = self.fw
        nc = self.nc
        for b in range(TT // 128):
            for hf in range(2):
                pt, dp = self.psr.nxt()
                for j in range(4):
                    c = hf * 4 + j
                    self.fw.op(fw.pe, lambda j=j, c=c: nc.tensor.transpose(pt[:, j * 128:(j + 1) * 128], self.xT[:, c, b * 128:(b + 1) * 128], self.ident), [self.dx, self.dcst], [dp])
                ot, do = self.iot.nxt()
                E = fw.act if hf == 0 else fw.dve
                self.cp(E, ot[:], pt[:], [dp], [do])
                fw.dma(fw.act, self.y_out[t0 + b * 128:t0 + (b + 1) * 128, hf * 512:(hf + 1) * 512], ot[:], reads=[do], q="xo")

    def norm_mod(self, l, i, seg):
        fw = self.fw
        for hf in range(TT // 512):
            sl = slice(hf * 512, (hf + 1) * 512)
            pt, dp = self.psr.nxt()
            for c in range(KC):
                sq, dsq = self.sqr.nxt()
                self.act(sq[:], self.xT[:, c, sl], AF.Square, [self.dx], [dsq])
                self.mm(pt[:], self.onesD[:], sq[:], c == 0, c == KC - 1, [dsq, self.dconst2], [dp])
            t1, d1 = self.f32r.nxt()
            self.act(t1[:], pt[:], AF.Ln, [dp], [d1], bias=EPS)
            self.act(self.rstd[:], t1[:], AF.Exp, [d1], [self.drstd], scale=-0.5)
            for c in range(KC):
                t2, d2 = self.f32r.nxt()
                self.stt(t2[:], self.xT[:, c, sl], self.gsc[l][:, i, c, seg:seg + 1], self.rstd[:], ALU.mult, ALU.mult, [self.dx, self.drstd, self.dmod[l]], [d2])
                self.act(self.hT[:, c, sl], t2[:], AF.Identity, [d2, self.dmod[l]], [self.dh], bias=self.shift_ap(l, i, c, seg))

    def ffn(self, l, which, i, seg):
        fw = self.fw
        nc = self.nc
        wv = self.W["ffn_w_in"][l, which].rearrange("(kc p) n -> p kc n", p=128)
        for j in range(FC // 2):
            wt, dw = self.wr.nxt()
            fw.dma(fw.pool, wt[:, :, 0:256], wv[:, :, j * 256:(j + 1) * 256], writes=[dw], q="w")
            fw.dma(fw.pool, wt[:, :, 256:512], wv[:, :, DFF + j * 256:DFF + (j + 1) * 256], writes=[dw], q="w")
            for fc in range(2):
                for hf in range(TT // 512):
                    sl = slice(hf * 512, (hf + 1) * 512)
                    pg, dpg = self.psr.nxt()
                    pu, dpu = self.psr.nxt()
                    for k in range(KC):
                        self.mm(pg[:], wt[:, k, fc * 128:(fc + 1) * 128], self.hT[:, k, sl], k == 0, k == KC - 1, [dw, self.dh], [dpg])
                    for k in range(KC):
                        self.mm(pu[:], wt[:, k, 256 + fc * 128:256 + (fc + 1) * 128], self.hT[:, k, sl], k == 0, k == KC - 1, [dw, self.dh], [dpu])
                    sg, dsg = self.f32r.nxt()
                    self.act(sg[:], pg[:], AF.Silu, [dpg], [dsg])
                    self.tt(self.aT[:, 2 * j + fc, sl], sg[:], pu[:], ALU.mult, [dsg, dpu], [self.da])
        self.out_proj(self.aT, self.da, FC, self.W["ffn_w_out"][l, which].rearrange("(c p) n -> p c n", p=128), l, i, seg)

    def out_proj(self, src, dsrc, nk, wv, l, i, seg):
        fw = self.fw
        for dp2 in range(KC // 2):
            wt, dw = self.wor.nxt()
            fw.dma(fw.pool, wt[:, 0:nk, :], wv[:, :, dp2 * 256:(dp2 + 1) * 256], writes=[dw], q="w")
            for dc in range(2):
                c = dp2 * 2 + dc
                for hf in range(TT // 512):
                    sl = slice(hf * 512, (hf + 1) * 512)
                    pt, dp = self.psr.nxt()
                    for k in range(nk):
                        self.mm(pt[:], wt[:, k, dc * 128:(dc + 1) * 128], src[:, k, sl], k == 0, k == nk - 1, [dw, dsrc], [dp])
                    self.stt(self.xT[:, c, sl], pt[:], self.gate[l][:, i, c, seg:seg + 1], self.xT[:, c, sl], ALU.mult, ALU.add, [dp, self.dx, self.dmod[l]], [self.dx])

    def mix_in(self, l, t0):
        fw = self.fw
        wv = self.W["w_mix_in"][l].rearrange("(kc p) n -> p kc n", p=128)
        for zc, cols in enumerate(zc_cols()):
            wt, dw = self.wr.nxt()
            o = 0
            for (c0, n) in cols:
                fw.dma(fw.pool, wt[:, :, o:o + n], wv[:, :, c0:c0 + n], writes=[dw], q="w")
                o += n
            for hf in range(TT // 512):
                sl = slice(hf * 512, (hf + 1) * 512)
                pt, dp = self.psr.nxt()
                for k in range(KC):
                    self.mm(pt[:], wt[:, k, 0:128], self.hT[:, k, sl], k == 0, k == KC - 1, [dw, self.dh], [dp])
                ot, do = self.iot.nxt()
                self.cp(fw.act if hf == 0 else fw.dve, ot[:], pt[:], [dp], [do])
                fw.dma(fw.act, self.zs[zc * 128:(zc + 1) * 128, t0 + hf * 512:t0 + (hf + 1) * 512], ot[:], reads=[do], q="xo")

    def mixer_phase(self, pst, l):
        fw = self.fw
        which = (self.debug or {}).get("mixers", ("lru", "att", "rwkv"))
        if "lru" in which:
            with contextlib.ExitStack() as st2, self.nc.named_scope(f"lru{l}"):
                self.lru(st2, l)
            fw.barrier()
        if "att" in which:
            with contextlib.ExitStack() as st2, self.nc.named_scope(f"att{l}"):
                self.attention(st2, l)
            fw.barrier()
        if "rwkv" in which:
            with contextlib.ExitStack() as st2, self.nc.named_scope(f"rwkv{l}"):
                self.rwkv(st2, l)
            fw.barrier()

    UNITS = ((0, 2), (2 * SEG, 1))

    def lru(self, st, l):
        fw = self.fw
        nc = self.nc
        W = self.W
        SA = 2 * SEG
        pv = fw.sb([128, 3, 16], stack=st)
        dpv = Dep()
        col1 = lambda ap: ap.rearrange("(p o) -> p o", o=1)
        for c in range(3):
            cs = slice(c * 128, (c + 1) * 128)
            self.vec_load(pv[:, c, 0:4], W["lru_conv_w"][l][:, cs].rearrange("j p -> p j"), dpv)
            self.vec_load(pv[:, c, 4:5], col1(W["lru_conv_b"][l, cs]), dpv)
            for d in range(2):
                self.vec_load(pv[:, c, 5 + d:6 + d], col1(W["lru_b_gate_a"][l, d, cs]), dpv)
                self.vec_load(pv[:, c, 7 + d:8 + d], col1(W["lru_b_gate_x"][l, d, cs]), dpv)
                self.vec_load(pv[:, c, 9 + d:10 + d], col1(W["lru_lambda"][l, d, cs]), dpv)
        self.act(pv[:, :, 11:13], pv[:, :, 9:11], AF.Exp, [dpv], [dpv], scale=-1.0)
        self.act(pv[:, :, 11:13], pv[:, :, 11:13], AF.Ln, [dpv], [dpv], bias=1.0)
        self.ts(pv[:, :, 13:15], pv[:, :, 11:13], -16.0, None, ALU.mult, None, [dpv], [dpv])
        self.ts(pv[:, :, 11:13], pv[:, :, 11:13], -8.0, None, ALU.mult, None, [dpv], [dpv])
        w32 = fw.sb([128, 4, 128], stack=st)
        dw32 = Dep()
        wbd = fw.sb([128, 4, 128], BF16, stack=st)
        dwbd = Dep()
        xpr = Ring([fw.sb([128, 2, SEG + 3], stack=st) for _ in range(2)])
        ybr = Ring([fw.sb([128, SA], stack=st) for _ in range(2)])
        xc = fw.sb([128, SA], stack=st)
        dxc = Dep()
        xcb = fw.sb([128, SA], BF16, stack=st)
        dxcb = Dep()
        bufs = [fw.sb([128, SA], stack=st) for _ in range(5)]
        dbs = [Dep() for _ in range(5)]
        h0, h1 = bufs[3], bufs[4]
        dh0, dh1 = dbs[3], dbs[4]
        pre = [None]

        def prefetch(c, ui):
            tok0, nseg = self.UNITS[ui]
            S = nseg * SEG
            cs = slice(c * 128, (c + 1) * 128)
            xpad, dxp = xpr.nxt()
            ybt, dyb = ybr.nxt()
            xp = xpad[:, 0:nseg, :]
            self.memset(fw.pool, xp[:, :, 0:2], 0.0, [dxp])
            self.memset(fw.pool, xp[:, :, SEG + 2:SEG + 3], 0.0, [dxp])
            fw.dma(fw.sp, xp[:, :, 2:SEG + 2], self.zs[cs, tok0:tok0 + S].rearrange("p (s t) -> p s t", s=nseg), writes=[dxp], q="x")
            fw.dma(fw.sp, ybt[:, 0:S], self.zs[384 + c * 128:384 + (c + 1) * 128, tok0:tok0 + S], writes=[dyb], q="x")
            return xpad, dxp, ybt, dyb

        for c in range(3):
            cs = slice(c * 128, (c + 1) * 128)
            self.memset(fw.pool, w32[:], 0.0, [dw32])
            for d in range(2):
                for gi, nm in enumerate(("lru_w_gate_a", "lru_w_gate_x")):
                    for n in range(2):
                        fw.dma(fw.sp, w32[n * 64:(n + 1) * 64, d * 2 + gi, n * 64:(n + 1) * 64], W[nm][l, d, 2 * c + n], writes=[dw32], q="v")
            self.cp(fw.pool, wbd[:], w32[:], [dw32], [dwbd])
            for ui, (tok0, nseg) in enumerate(self.UNITS):
                S = nseg * SEG
                if pre[0] is None:
                    pre[0] = prefetch(c, ui)
                xpad, dxp, ybt, dyb = pre[0]
                nc_, nu_ = (c, ui + 1) if ui + 1 < len(self.UNITS) else (c + 1, 0)
                pre[0] = prefetch(nc_, nu_) if nc_ < 3 else None
                xp = xpad[:, 0:nseg, :]
                if nseg == 2:
                    self.ts(xpad[:, 1, 0:2], xpad[:, 0, SEG:SEG + 2], self.link[:, 0:1], None, ALU.mult, None, [dxp, self.dlink], [dxp])
                    self.ts(xpad[:, 0, SEG + 2:SEG + 3], xpad[:, 1, 2:3], self.link[:, 0:1], None, ALU.mult, None, [dxp, self.dlink], [dxp])
                xc3 = xc[:, 0:S].rearrange("p (s t) -> p s t", s=nseg)
                self.ts(xc3, xp[:, :, 0:SEG], pv[:, c, 0:1], pv[:, c, 4:5], ALU.mult, ALU.add, [dxp, dpv], [dxc])
                for j in range(1, 4):
                    self.stt(xc3, xp[:, :, j:j + SEG], pv[:, c, j:j + 1], xc3, ALU.mult, ALU.add, [dxp, dpv, dxc], [dxc])
                self.cp(fw.act, xcb[:, 0:S], xc[:, 0:S], [dxc], [dxcb])
                for d in range(2):
                    hb, dhb = (h0, dh0) if d == 0 else (h1, dh1)
                    b1, b2, b3 = bufs[0:3]
                    d1, d2, d3 = dbs[0:3]
                    for gi, (bt, dbt) in enumerate(((b1, d1), (b2, d2))):
                        for blk in range(S // 512):
                            sl = slice(blk * 512, (blk + 1) * 512)
                            pt, dp = self.psr.nxt()
                            self.mm(pt[:], wbd[:, d * 2 + gi, :], xcb[:, sl], True, True, [dwbd, dxcb], [dp])
                            self.act(bt[:, sl], pt[:], AF.Sigmoid, [dp, dpv], [dbt], bias=pv[:, c, 5 + 2 * gi + d:6 + 2 * gi + d])
                    self.act(b3[:, 0:S], b1[:, 0:S], AF.Exp, [d1, dpv], [d3], scale=pv[:, c, 11 + d:12 + d])
                    self.act(b1[:, 0:S], b1[:, 0:S], AF.Exp, [d1, dpv], [d1], scale=pv[:, c, 13 + d:14 + d])
                    self.act(b1[:, 0:S], b1[:, 0:S], AF.Sqrt, [d1], [d1], scale=-1.0, bias=1.0)
                    self.tt(b2[:, 0:S], b2[:, 0:S], xc[:, 0:S], ALU.mult, [d2, dxc], [d2])
                    self.tt(b2[:, 0:S], b2[:, 0:S], b1[:, 0:S], ALU.mult, [d2, d1], [d2])
                    if nseg == 2:
                        cp_ = SEG if d == 0 else SEG - 1
                        self.ts(b3[:, cp_:cp_ + 1], b3[:, cp_:cp_ + 1], self.link[:, 0:1], None, ALU.mult, None, [d3, self.dlink], [d3])
                    if d == 0:
                        self.fw.op(fw.dve, lambda: nc.vector.tensor_tensor_scan(out=hb[:, 0:S], data0=b3[:, 0:S], data1=b2[:, 0:S], initial=0.0, op0=ALU.mult, op1=ALU.add), [d3, d2], [dhb])
                    else:
                        self.fw.op(fw.dve, lambda: nc.vector.tensor_tensor_scan(out=hb[:, S - 1::-1] if False else hb[:, 0:S][:, ::-1], data0=b3[:, 0:S][:, ::-1], data1=b2[:, 0:S][:, ::-1], initial=0.0, op0=ALU.mult, op1=ALU.add), [d3, d2], [dhb])
                b1, b2 = ybt, bufs[1]
                d1, d2 = dyb, dbs[1]
                self.act(b2[:, 0:S], b1[:, 0:S], AF.Square, [d1], [d2])
                self.ts(b2[:, 0:S], b2[:, 0:S], 0.044715, 1.0, ALU.mult, ALU.add, [d2], [d2])
                self.tt(b2[:, 0:S], b2[:, 0:S], b1[:, 0:S], ALU.mult, [d2, d1], [d2])
                self.act(b2[:, 0:S], b2[:, 0:S], AF.Sigmoid, [d2], [d2], scale=1.5957691216057308)
                self.tt(b1[:, 0:S], b1[:, 0:S], b2[:, 0:S], ALU.mult, [d1, d2], [d1])
                self.tt(h0[:, 0:S], h0[:, 0:S], h1[:, 0:S], ALU.add, [dh0, dh1], [dh0])
                self.tt(xcb[:, 0:S], h0[:, 0:S], b1[:, 0:S], ALU.mult, [dh0, d1], [dxcb])
                fw.dma(fw.sp, self.os[cs, tok0:tok0 + S], xcb[:, 0:S], reads=[dxcb], q="xo")

    def attention(self, st, l):
        fw = self.fw
        nc = self.nc
        W = self.W
        SA = 2 * SEG
        onesf = self.cst[:, 3, :]
        psw = Ring(self.psw_views)
        rotT = self.cst[:, 1, :]
        gv = fw.sb([128, 8], stack=st)
        dgv = Dep()
        col1 = lambda ap: ap.rearrange("(p o) -> p o", o=1)
        for hh in range(2):
            self.vec_load(gv[hh * 64:(hh + 1) * 64, 0:1], col1(W["attn_q_norm"][l]), dgv)
            self.vec_load(gv[hh * 64:(hh + 1) * 64, 1:2], col1(W["attn_k_norm"][l]), dgv)
        self.ts(gv[:, 0:1], gv[:, 0:1], 0.125, None, ALU.mult, None, [dgv], [dgv])
        rows = fw.sb([1, 132], stack=st)
        drw = Dep()
        fw.dma(fw.sp, rows[0:1, 0:64], W["attn_q_norm"][l].rearrange("(o d) -> o d", o=1), writes=[drw], q="v")
        fw.dma(fw.sp, rows[0:1, 64:128], W["attn_k_norm"][l].rearrange("(o d) -> o d", o=1), writes=[drw], q="v")
        self.fw.op(fw.dve, lambda: nc.vector.tensor_reduce(out=rows[0:1, 128:130], in_=rows[0:1, 0:128].rearrange("o (a d) -> o a d", a=2), axis=mybir.AxisListType.X, op=ALU.max, apply_absolute_value=True), [drw], [drw])
        self.tt(rows[0:1, 130:131], rows[0:1, 128:129], rows[0:1, 129:130], ALU.mult, [drw], [drw])
        self.cp(fw.dve, rows[0:1, 131:132], rows[0:1, 130:131], [drw], [drw])
        pt, dp = psw.nxt()
        self.mm(pt[:, 0:2], onesf[0:1, :], rows[0:1, 130:132], True, True, [drw, self.dcst], [dp])
        self.ts(gv[:, 2:3], pt[:, 0:1], -8.0, None, ALU.mult, None, [dp], [dgv])
        self.ts(gv[:, 3:4], self.link[:, 0:1], 30000.0, -30000.0, ALU.mult, ALU.add, [self.dlink], [dgv])
        self.tt(gv[:, 3:4], gv[:, 3:4], gv[:, 2:3], ALU.add, [dgv], [dgv])
        cosT = fw.sb([128, SA], stack=st)
        sinT = fw.sb([128, SA], stack=st)
        dcs = Dep()
        srcr = Ring([fw.sb([128, SA], stack=st) for _ in range(2)])
        kT = [[fw.sb([128, SA], BF16, stack=st) for _ in range(2)] for _ in range(2)]
        dkT = [Dep(), Dep()]
        for kv_ in range(2):
            self.memset(fw.pool, kT[kv_][0][64:128, :], 0.0, [dkT[kv_]])
            self.memset(fw.pool, kT[kv_][1][0:64, :], 0.0, [dkT[kv_]])
        vtok = fw.sb([128, SA // 128, 2, 192], BF16, stack=st)
        dvt = Dep()
        qT = fw.sb([128, SA], BF16, stack=st)
        dqT = Dep()
        och = fw.sb([128, SA], BF16, stack=st)
        doc = Dep()
        pTr = Ring([fw.sb([128, 1024], BF16, stack=st) for _ in range(4)])
        osbr = Ring([fw.sb([128, 512], stack=st) for _ in range(2)])
        lnrr = Ring([fw.sb([128, 512], stack=st) for _ in range(2)])
        tail = [None]
        sqr = Ring([fw.sb([128, 512], BF16, stack=st) for _ in range(2)])
        f32r = Ring([fw.sb([128, 512], stack=st) for _ in range(6)])
        osb = fw.sb([128, 512], stack=st)
        dosb = Dep()
        lnr = fw.sb([128, 512], stack=st)
        dlnr = Dep()
        self.memset(fw.pool, vtok[:], 0.0, [dvt])
        self.memset(fw.pool, vtok[:, :, :, 64:65], 1.0, [dvt])

        def norm_rope(src, dsrc, dst, ddst, gcol, S, tok0):
            for blk in range(S // 512):
                sl = slice(blk * 512, (blk + 1) * 512)
                sq, dsq = sqr.nxt()
                self.act(sq[:], src[:, sl], AF.Square, [dsrc], [dsq])
                p1, dp1 = psw.nxt()
                self.mm(p1[:, 0:512], self.blk64[:], sq[:], True, True, [dsq, self.dconst2], [dp1])
                t, dt_ = f32r.nxt()
                self.act(t[:], p1[:, 0:512], AF.Ln, [dp1], [dt_], bias=EPS)
                self.act(t[:], t[:], AF.Exp, [dt_], [dt_], scale=-0.5)
                qn, dqn = f32r.nxt()
                self.stt(qn[:], src[:, sl], gv[:, gcol:gcol + 1], t[:], ALU.mult, ALU.mult, [dsrc, dgv, dt_], [dqn])
                p2, dp2 = psw.nxt()
                self.mm(p2[:, 0:512], rotT, qn[:], True, True, [dqn, self.dcst], [dp2])
                t1, dt1 = f32r.nxt()
                self.tt(t1[:], qn[:], cosT[:, sl], ALU.mult, [dqn, dcs], [dt1], )
                self.tt(t[:], p2[:, 0:512], sinT[:, sl], ALU.mult, [dp2, dcs, dt_], [dt_])
                if isinstance(dst, list):
                    self.tt(dst[0][0:64, sl], t1[0:64, :], t[0:64, :], ALU.add, [dt1, dt_], [ddst])
                    self.tt(dst[1][64:128, sl], t1[64:128, :], t[64:128, :], ALU.add, [dt1, dt_], [ddst])
                else:
                    self.tt(dst[:, sl], t1[:], t[:], ALU.add, [dt1, dt_], [ddst])

        for (tok0, nseg) in self.UNITS:
            S = nseg * SEG
            fw.dma(fw.sp, cosT[:, 0:S], self.cos_in[:, tok0:tok0 + S], writes=[dcs], q="x")
            fw.dma(fw.sp, sinT[:, 0:S], self.sin_in[:, tok0:tok0 + S], writes=[dcs], q="x")
            for kv in range(2):
                src, dsrc = srcr.nxt()
                fw.dma(fw.sp, src[:, 0:S], self.zs[(17 + kv) * 128:(18 + kv) * 128, tok0:tok0 + S], writes=[dsrc], q="x")
                norm_rope(src, dsrc, kT[kv], dkT[kv], 1, S, tok0)
            src, dsrc = srcr.nxt()
            fw.dma(fw.sp, src[:, 0:S], self.zs[19 * 128:20 * 128, tok0:tok0 + S], writes=[dsrc], q="x")
            for b0 in range(0, S // 128, 4):
                pt, dp = psw.nxt()
                for j in range(4):
                    self.fw.op(fw.pe, lambda j=j: nc.tensor.transpose(pt[:, j * 128:(j + 1) * 128], src[:, (b0 + j) * 128:(b0 + j + 1) * 128], self.ident), [dsrc, self.dcst], [dp])
                pv4 = pt[:, 0:512].rearrange("p (j k d) -> p j k d", j=4, k=2)
                self.cp(fw.act, vtok[:, b0:b0 + 4, :, 0:64], pv4, [dp], [dvt])
                self.cp(fw.dve, vtok[:, b0:b0 + 4, :, 128:192], pv4, [dp], [dvt])
            for qc in range(3):
                src, dsrc = srcr.nxt()
                fw.dma(fw.sp, src[:, 0:S], self.zs[(14 + qc) * 128:(15 + qc) * 128, tok0:tok0 + S], writes=[dsrc], q="x")
                norm_rope(src, dsrc, qT, dqT, 0, S, tok0)
                for qb in range(S // 512):
                    qs = slice(qb * 512, (qb + 1) * 512)
                    for e in range(2):
                        h = 2 * qc + e
                        kv = h // 3
                        es = slice(e * 64, (e + 1) * 64)
                        po, dpo = self.pso.nxt()
                        nk = S // 128
                        LOOK = 2
                        pend = []
                        nk2 = nk // 2
                        for k2i in range(nk2 + LOOK):
                            if k2i < nk2:
                                ps_, dps = psw.nxt()
                                for u in range(2):
                                    kc = 2 * k2i + u
                                    self.mm(ps_[:, u * 512:(u + 1) * 512], kT[kv][e][:, kc * 128:(kc + 1) * 128], qT[:, qs], True, True, [dkT[kv], dqT], [dps])
                                pT, dpT = pTr.nxt()
                                same = (nseg == 1) or ((qb // 4) == ((2 * k2i) // 16))
                                bc = 2 if same else 3
                                self.act(pT[:], ps_[:], AF.Exp, [dps, dgv], [dpT], bias=gv[:, bc:bc + 1])
                                pend.append((pT, dpT, k2i))
                            if k2i == LOOK - 1 and tail[0] is not None:
                                tail[0]()
                                tail[0] = None
                            if k2i >= LOOK:
                                pT, dpT, kk2 = pend.pop(0)
                                for u in range(2):
                                    k2 = 2 * kk2 + u
                                    if e == 0:
                                        self.mm(po[0:65, :], vtok[:, k2, kv, 0:65], pT[:, u * 512:(u + 1) * 512], k2 == 0, k2 == nk - 1, [dvt, dpT], [dpo])
                                    else:
                                        self.mm(po[:, :], vtok[:, k2, kv, 64:192], pT[:, u * 512:(u + 1) * 512], k2 == 0, k2 == nk - 1, [dvt, dpT], [dpo])

                        def mk_tail(po=po, dpo=dpo, e=e, es=es, qs=qs):
                            def f():
                                r0 = 64 if e == 0 else 0
                                lnr, dlnr = lnrr.nxt()
                                self.act(lnr[r0:r0 + 1, :], po[r0:r0 + 1, :], AF.Ln, [dpo], [dlnr])
                                self.act(lnr[r0:r0 + 1, :], lnr[r0:r0 + 1, :], AF.Exp, [dlnr], [dlnr], scale=-1.0)
                                pb, dpb = psw.nxt()
                                self.mm(pb[:, 0:512], onesf[r0:r0 + 1, :], lnr[r0:r0 + 1, :], True, True, [dlnr, self.dcst], [dpb])
                                osb, dosb = osbr.nxt()
                                self.cp(fw.act, osb[es, :], po[es, :], [dpo], [dosb])
                                self.tt(och[es, qs], osb[es, :], pb[es, 0:512], ALU.mult, [dosb, dpb], [doc])
                            return f
                        tail[0] = mk_tail()
                if tail[0] is not None:
                    tail[0]()
                    tail[0] = None
                fw.dma(fw.sp, self.os[640 + qc * 128:640 + (qc + 1) * 128, tok0:tok0 + S], och[:, 0:S], reads=[doc], q="xo")

    def rwkv_pre(self, st, l):
        fw = self.fw
        nc = self.nc
        W = self.W
        ws = self.ws
        BL = 512
        col1 = lambda ap: ap.rearrange("(p o) -> p o", o=1)
        cp2 = lambda ap: ap.rearrange("(c p) -> p c", p=128)
        blk64f = self.cst[:, 2, :]
        pp = fw.sb([128, 64], stack=st)
        dpp = Dep()
        mu = W["rwkv_mu"][l]
        self.vec_load(pp[:, 0:8], cp2(mu), dpp)
        self.ts(pp[:, 8:16], pp[:, 0:8], 0.5, None, ALU.mult, None, [dpp], [dpp])
        self.ts(pp[:, 16:24], pp[:, 0:8], -1.0, 1.0, ALU.mult, ALU.add, [dpp], [dpp])
        for d in range(2):
            self.vec_load(pp[:, 24 + 2 * d:26 + 2 * d], cp2(W["rwkv_w0"][l, d]), dpp)
            self.vec_load(pp[:, 28 + 2 * d:30 + 2 * d], cp2(W["rwkv_a0"][l, d]), dpp)
        self.vec_load(pp[:, 32:34], cp2(W["rwkv_k_k"][l]), dpp)
        self.vec_load(pp[:, 34:36], cp2(W["rwkv_k_a"][l]), dpp)
        self.ts(pp[:, 36:38], pp[:, 34:36], -1.0, 1.0, ALU.mult, ALU.add, [dpp], [dpp])
        self.vec_load(pp[:, 38:40], cp2(W["rwkv_r_k"][l].rearrange("h k -> (h k)")), dpp)
        wup = fw.sb([64, 2, 256], stack=st)
        aup = fw.sb([128, 256], stack=st)
        gup = fw.sb([128, 256], stack=st)
        dwl = Dep()
        for d in range(2):
            fw.dma(fw.sp, wup[:, d, :], W["rwkv_w_up"][l, d], writes=[dwl], q="v")
        fw.dma(fw.sp, aup[64:128, :], W["rwkv_a_up"][l], writes=[dwl], q="v")
        fw.dma(fw.sp, gup[:], W["rwkv_g_up"][l], writes=[dwl], q="v")
        padr = Ring([fw.sb([128, 8, BL + 2], stack=st) for _ in range(2)])
        sR = Ring([fw.sb([128, 8, BL], stack=st) for _ in range(2)])
        fR = Ring([fw.sb([128, 8, BL], stack=st) for _ in range(2)])
        tR = Ring([fw.sb([128, BL], stack=st) for _ in range(12)])
        oR = Ring([fw.sb([128, BL], stack=st) for _ in range(10)])
        zv = self.zs[768:1792, :].rearrange("(c p) t -> p c t", p=128)
        def load_pad(tok0, S, bi):
            t0 = bi * BL
            g0 = tok0 + t0
            lo = 1 if t0 == 0 else 0
            hi = 1 if t0 + BL == S else 0
            n = BL + 2 - lo - hi
            pad, dpad = padr.nxt()
            if lo:
                self.memset(fw.pool, pad[:, :, 0:1], 0.0, [dpad])
            if hi:
                self.memset(fw.pool, pad[:, :, BL + 1:BL + 2], 0.0, [dpad])
            fw.dma(fw.sp, pad[:, :, lo:lo + n], zv[:, :, g0 - 1 + lo:g0 - 1 + lo + n], writes=[dpad], q="x")
            if S == 2 * SEG and (t0 == SEG or t0 + BL == SEG):
                cix = 0 if t0 == SEG else BL + 1
                self.ts(pad[:, :, cix:cix + 1], pad[:, :, cix:cix + 1], self.link[:, 0:1], None, ALU.mult, None, [dpad, self.dlink], [dpad])
            return pad, dpad
        blocks = [(tok0, nseg * SEG, bi) for (tok0, nseg) in self.UNITS for bi in range(nseg * SEG // BL)]
        nxt_pad = load_pad(*blocks[0])
        for ib, (tok0, S, bi) in enumerate(blocks):
            if True:
                t0 = bi * BL
                g0 = tok0 + t0
                sl = slice(g0, g0 + BL)
                pad, dpad = nxt_pad
                if ib + 1 < len(blocks):
                    nxt_pad = load_pad(*blocks[ib + 1])
                s_, ds_ = sR.nxt()
                f, df = fR.nxt()
                for c in range(8):
                    self.tt(s_[:, c, :], pad[:, c, 0:BL], pad[:, c, 2:BL + 2], ALU.add, [dpad], [ds_], E=fw.pool if c % 2 else fw.dve)
                    self.aff(s_[:, c, :], s_[:, c, :], pp[:, 8 + c:9 + c], None, [ds_, dpp], [ds_])
                    self.stt(f[:, c, :], pad[:, c, 1:BL + 1], pp[:, 16 + c:17 + c], s_[:, c, :], ALU.mult, ALU.add, [dpad, ds_, dpp], [df])
                for fc in range(2):
                    fw.dma(fw.sp, ws["r"][fc * 128:(fc + 1) * 128, sl], f[:, fc, :], reads=[df], q="xo")
                    fw.dma(fw.sp, ws["v"][fc * 128:(fc + 1) * 128, sl], f[:, 4 + fc, :], reads=[df], q="xo")
                tw, dtw = tR.nxt()
                self.act(tw[0:64, :], f[0:64, 6, :], AF.Tanh, [df], [dtw])
                for d in range(2):
                    for fc in range(2):
                        pt, dp = self.psr.nxt()
                        self.mm(pt[:], wup[:, d, fc * 128:(fc + 1) * 128], tw[0:64, :], True, True, [dwl, dtw], [dp])
                        o, do = oR.nxt()
                        self.act(o[:], pt[:], AF.Sigmoid, [dp, dpp], [do], bias=pp[:, 24 + 2 * d + fc:25 + 2 * d + fc])
                        fw.dma(fw.sp, ws["sg%d" % d][fc * 128:(fc + 1) * 128, sl], o[:], reads=[do], q="xo")
                kaps = []
                for fc in range(2):
                    kk, dkk = tR.nxt()
                    self.aff(kk[:], f[:, 2 + fc, :], pp[:, 32 + fc:33 + fc], None, [df, dpp], [dkk])
                    sq, dsq = tR.nxt()
                    self.act(sq[:], kk[:], AF.Square, [dkk], [dsq])
                    pt, dp = self.psr.nxt()
                    self.mm(pt[:], blk64f, sq[:], True, True, [dsq, self.dcst], [dp])
                    self.act(sq[:], pt[:], AF.Ln, [dp], [dsq], scale=64.0, bias=1e-24)
                    self.act(sq[:], sq[:], AF.Exp, [dsq], [dsq], scale=-0.5)
                    kap, dkap = oR.nxt()
                    self.tt(kap[:], kk[:], sq[:], ALU.mult, [dkk, dsq], [dkap])
                    fw.dma(fw.sp, ws["kap"][fc * 128:(fc + 1) * 128, sl], kap[:], reads=[dkap], q="xo")
                    kaps.append((kap, dkap))
                for fc in range(2):
                    pt, dp = self.psr.nxt()
                    self.mm(pt[:], aup[64:128, fc * 128:(fc + 1) * 128], f[64:128, 6, :], True, True, [dwl, df], [dp])
                    for d in range(2):
                        a_, da_ = tR.nxt()
                        self.act(a_[:], pt[:], AF.Sigmoid, [dp, dpp], [da_], bias=pp[:, 28 + 2 * d + fc:29 + 2 * d + fc])
                        t_, dt_ = tR.nxt()
                        self.aff(t_[:], a_[:], pp[:, 34 + fc:35 + fc], pp[:, 36 + fc:37 + fc], [da_, dpp], [dt_])
                        kd, dkd = oR.nxt()
                        self.tt(kd[:], t_[:], f[:, 2 + fc, :], ALU.mult, [dt_, df], [dkd])
                        fw.dma(fw.sp, ws["kd%d" % d][fc * 128:(fc + 1) * 128, sl], kd[:], reads=[dkd], q="xo")
                        b_, db_ = oR.nxt()
                        self.tt(b_[:], a_[:], kaps[fc][0][:], ALU.mult, [da_, kaps[fc][1]], [db_], E=fw.pool)
                        fw.dma(fw.sp, ws["b%d" % d][fc * 128:(fc + 1) * 128, sl], b_[:], reads=[db_], q="xo")
                for fc in range(2):
                    rk, drk = tR.nxt()
                    self.stt(rk[:], f[:, fc, :], pp[:, 38 + fc:39 + fc], f[:, 2 + fc, :], ALU.mult, ALU.mult, [df, dpp], [drk])
                    pt, dp = self.psr.nxt()
                    self.mm(pt[:], blk64f, rk[:], True, True, [drk, self.dcst], [dp])
                    bo, dbo = oR.nxt()
                    self.stt(bo[:], pt[:], 64.0, f[:, 4 + fc, :], ALU.mult, ALU.mult, [dp, df], [dbo])
                    fw.dma(fw.sp, ws["bon"][fc * 128:(fc + 1) * 128, sl], bo[:], reads=[dbo], q="xo")
                sgx, dsgx = tR.nxt()
                self.act(sgx[:], f[:, 7, :], AF.Sigmoid, [df], [dsgx])
                for fc in range(2):
                    pt, dp = self.psr.nxt()
                    self.mm(pt[:], gup[:, fc * 128:(fc + 1) * 128], sgx[:], True, True, [dwl, dsgx], [dp])
                    go, dgo = oR.nxt()
                    self.cp(fw.act, go[:], pt[:], [dp], [dgo])
                    fw.dma(fw.sp, ws["g"][fc * 128:(fc + 1) * 128, sl], go[:], reads=[dgo], q="xo")

    def rwkv(self, st0, l):
        fw = self.fw
        with contextlib.ExitStack() as st1, self.nc.named_scope(f"rwpre{l}"):
            self.rwkv_pre(st1, l)
        fw.barrier()
        with contextlib.ExitStack() as st, self.nc.named_scope(f"rwscan{l}"):
            self.rwkv_scan(st, l)

    def rwkv_scan(self, st, l):
        fw = self.fw
        nc = self.nc
        W = self.W
        ws = self.ws
        SB = 256
        NP = 2
        NCH = 4
        C0 = -0.6065306597126334
        GN_EPS = 64e-5
        ident = self.ident
        id64 = self.cst[0:64, 0, 0:64]
        c64 = self.cst[0:64, 2, 0:64]
        hk = lambda ap: ap.rearrange("(h k) -> k h", k=64)
        hkt = lambda ap: ap.rearrange("(h k) t -> k h t", k=64)
        flat2 = lambda ap: ap.rearrange("p a b -> p (a b)")
        mX = [self.cst[:, 4, :], self.cst[:, 8, :]]
        mXt = [self.cst[:, 8, :], self.cst[:, 4, :]]
        mD = [flat2(self.cst[0:64, 9:11, :])[:, 0:192], flat2(self.cst[0:64, 12:14, :])[:, 0:192]]
        pr = fw.sb([64, 8], stack=st)
        dpr = Dep()
        self.vec_load(pr[:, 0:4], hk(W["rwkv_ln_g"][l]), dpr)
        self.vec_load(pr[:, 4:8], hk(W["rwkv_ln_b"][l]), dpr)
        smask = [fw.sb([64, 4, SB], stack=st) for _ in range(2)]
        dsm = Dep()
        for d in range(2):
            self.memset(fw.pool, smask[d][:], 1.0, [dsm])
            z0 = 0 if d == 0 else 63
            self.memset(fw.pool, smask[d][:].rearrange("k h (c t) -> k (h c) t", t=64)[:, :, z0:z0 + 1], 0.0, [dsm])
        tsm = Ring([fw.sb([64, SB], stack=st) for _ in range(6)])

        class Set:
            pass
        sets = []
        for _ in range(2):
            S_ = Set()
            S_.B = [fw.sb([64, 4, SB], stack=st) for _ in range(9)]
            S_.dB = [Dep() for _ in range(9)]
            S_.KRf = fw.sb([64, 4, NP, 2, 128], BF16, stack=st)
            S_.dKR = Dep()
            S_.Bfb, S_.Kfb, S_.Vb = [fw.sb([64, 4, SB], BF16, stack=st) for _ in range(3)]
            S_.dBfb, S_.dKfb, S_.dVb = Dep(), Dep(), Dep()
            sets.append(S_)
        Tst = fw.sb([64, 4, 64], stack=st)
        dT = [Dep() for _ in range(4)]
        Tb = fw.sb([64, 4, 64], BF16, stack=st)
        dTb = [Dep() for _ in range(4)]
        shb = fw.sb([128, 64], BF16, stack=st)
        twr = Ring([fw.sb([64, 64], stack=st) for _ in range(8)])
        NPI = 4 * NP
        NCI = 4 * NCH
        tok3 = [fw.sb([64, 192], BF16, stack=st) for _ in range(NCI)]
        Ach = [fw.sb([64, 192], BF16, stack=st) for _ in range(NCI)]
        dch = [Dep() for _ in range(NCI)]
        dAch = [Dep() for _ in range(NCI)]
        MT = [fw.sb([128, 128], BF16, stack=st) for _ in range(NPI)]
        MT1 = [fw.sb([64, 64], BF16, stack=st) for _ in range(NPI)]
        dsl = [Dep() for _ in range(NPI)]
        Xb = [[fw.sb([128, 128], BF16, stack=st) for _ in range(2)] for _ in range(NPI)]
        Xtb = [[fw.sb([128, 128], BF16, stack=st) for _ in range(2)] for _ in range(NPI)]
        Accb = [[fw.sb([128, 128], BF16, stack=st) for _ in range(2)] for _ in range(NPI)]
        dAcc = [[Dep(), Dep()] for _ in range(NPI)]
        identb = fw.sb([128, 128], BF16, stack=st)
        self.cp(fw.dve, identb[:], ident, [self.dcst], [dsm])
        self.cp(fw.dve, shb[:], self.cst[:, 11, 0:64], [self.dcst], [dsm])
        dXb = [[Dep(), Dep()] for _ in range(NPI)]
        dXtb = [[Dep(), Dep()] for _ in range(NPI)]
        gsr = Ring([fw.sb([64, 64], BF16, stack=st) for _ in range(8)])
        usr = Ring([fw.sb([64, 64], BF16, stack=st) for _ in range(8)])
        obuf = fw.sb([64, 4, SB], BF16, stack=st)
        dob = Dep()
        dys = {}

        def elementwise(job, Z):
            tok0, S, d, sbi = job
            B, dB = Z.B, Z.dB
            KRf, dKR = Z.KRf, Z.dKR
            g0 = tok0 + sbi * SB
            sl = slice(g0, g0 + SB)
            for bi, nm in ((0, "r"), (1, "kap"), (2, "v"), (3, "kd%d" % d), (4, "sg%d" % d), (7, "b%d" % d)):
                fw.dma(fw.sp, B[bi][:], hkt(ws[nm][:, sl]), writes=[dB[bi]], q="x")
            yield
            for _ in range(20):
                yield
            sg, dsg = B[4], dB[4]
            Ls, dLs = B[5], dB[5]
            flat = lambda t: t[:].rearrange("k h t -> k (h t)")
            rvf = (lambda ap: ap) if d == 0 else (lambda ap: ap[:, ::-1])
            self.fw.op(fw.dve, lambda: nc.vector.tensor_tensor_scan(out=rvf(flat(Ls)), data0=rvf(flat(smask[d])), data1=rvf(flat(sg)), initial=0.0, op0=ALU.mult, op1=ALU.add), [dsm, dsg], [dLs])
            yield
            self.tt(sg[:], Ls[:], sg[:], ALU.subtract, [dLs, dsg], [dsg], E=fw.pool)
            self.act(sg[:], sg[:], AF.Exp, [dsg], [dsg], scale=C0)
            yield
            Ep, dEp = B[6], dB[6]
            self.act(Ep[:], Ls[:], AF.Exp, [dLs], [dEp], scale=C0)
            self.act(Ls[:], Ls[:], AF.Exp, [dLs], [dLs], scale=-C0)
            yield
            v4 = lambda t: t[:].rearrange("k h (p t) -> k h p t", p=NP)
            self.tt(KRf[:, :, :, 0, :], v4(B[1]), v4(sg), ALU.mult, [dB[1], dsg], [dKR], E=fw.pool)
            yield
            self.tt(KRf[:, :, :, 1, :], v4(B[0]), v4(Ep), ALU.mult, [dB[0], dEp], [dKR], E=fw.pool)
            yield
            self.tt(Z.Bfb[:], B[7][:], Ls[:], ALU.mult, [dB[7], dLs], [Z.dBfb], E=fw.pool)
            yield
            self.tt(Z.Kfb[:], B[3][:], Ls[:], ALU.mult, [dB[3], dLs], [Z.dKfb], E=fw.pool)
            self.cp(fw.pool, Z.Vb[:], B[2][:], [dB[2]], [Z.dVb])
            yield

        def matrices(job, Z, pull, first_of_dir, diag=False):
            tok0, S, d, sbi = job
            nsb = S // SB
            B, dB = Z.B, Z.dB
            KRf, dKR = Z.KRf, Z.dKR
            Bf, dBf = Z.Bfb, Z.dBfb
            Kf, dKf = Z.Kfb, Z.dKfb
            Vf, dVf = Z.Vb, Z.dVb
            Ep, dEp = B[6], dB[6]
            yb, dyb = B[8], dB[8]
            t0 = sbi * SB
            gt0 = tok0 + t0
            if first_of_dir:
                for h in range(4):
                    self.memset(fw.pool, Tst[:, h, :], 0.0, [dT[h]])
                    self.memset(fw.pool, Tb[:, h, :], 0.0, [dTb[h]])
            if d == 1:
                fw.dma(fw.sp, B[5][:], hkt(self.ys[:, gt0:gt0 + SB]), reads=[dys[gt0]], writes=[dB[5]], q="x")
            scope = (lambda nm: nc.named_scope(nm)) if diag else (lambda nm: contextlib.nullcontext())
            sc_ = scope("rwA_chunk"); sc_.__enter__()
            for h in range(4):
                for c in range(NCH):
                    ci = h * NCH + c
                    cs_ = slice(c * 64, (c + 1) * 64)
                    pt, dp = self.psr.nxt()
                    for j, (src, dsrc) in enumerate(((Bf, dBf), (Kf, dKf), (Vf, dVf))):
                        self.fw.op(fw.pe, lambda j=j, src=src: nc.tensor.transpose(pt[0:64, 0:96].bitcast(BF16)[:, j * 64:(j + 1) * 64], src[:, h, cs_], identb[0:64, 0:64]), [dsrc, dsm], [dp])
                    self.cp(fw.act, tok3[ci][:], pt[0:64, 0:96].bitcast(BF16), [dp], [dch[ci]])
                    pull(1)
            for h in range(4):
                for c in range(NCH):
                    ci = h * NCH + c
                    p, e = c // 2, c % 2
                    cs_ = slice(c * 64, (c + 1) * 64)
                    rhat = KRf[:, h, p, 1, e * 64:(e + 1) * 64]
                    pc, dpc = self.psr.nxt()
                    self.mm(pc[0:64, 0:64], Bf[:, h, cs_], rhat, True, True, [dBf, dKR], [dpc])
                    self.mm(pc[0:64, 64:192].rearrange("p (a t) -> p a t", a=2), Kf[:, h, cs_], KRf[:, h, p, :, e * 64:(e + 1) * 64], True, True, [dKf, dKR], [dpc])
                    self.tt(Ach[ci][:], pc[0:64, 0:192], mD[d], ALU.mult, [dpc, self.dcst], [dAch[ci]])
            sc_.__exit__(None, None, None); sc_ = scope("rwB_dbl"); sc_.__enter__()
            cur = [0] * NPI
            for h in range(4):
                for p in range(NP):
                    pi = h * NP + p
                    ps_ = slice(p * 128, (p + 1) * 128)
                    p1, dp1 = self.psr.nxt()
                    self.mm(p1[:, 0:128], Bf[:, h, ps_], KRf[:, h, p, 0, :], True, True, [dBf, dKR], [dp1])
                    self.tt(Xb[pi][0][:], p1[:, 0:128], mX[d], ALU.mult, [dp1, self.dcst], [dXb[pi][0]])
                    p3, dp3 = self.psr.nxt()
                    self.mm(p3[:, 0:128], KRf[:, h, p, 0, :], Bf[:, h, ps_], True, True, [dBf, dKR], [dp3])
                    self.tt(Xtb[pi][0][:], p3[:, 0:128], mXt[d], ALU.mult, [dp3, self.dcst], [dXtb[pi][0]])
                    self.tt(Accb[pi][0][:], Xb[pi][0][:], ident, ALU.add, [dXb[pi][0], self.dcst], [dAcc[pi][0]])
                    pull(1)
            acur = [0] * NPI
            for lev in range(1, 6):
                for pi in range(NPI):
                    k = cur[pi]
                    X, dX, Xt, dXt = Xb[pi][k], dXb[pi][k], Xtb[pi][k], dXtb[pi][k]
                    pxt, dpxt = self.psr.nxt()
                    self.mm(pxt[:, 0:128], X[:], Xt[:], True, True, [dX, dXt], [dpxt])
                    self.cp(fw.act if pi % 2 else fw.dve, Xtb[pi][1 - k][:], pxt[:, 0:128], [dpxt], [dXtb[pi][1 - k]])
                    if lev < 5:
                        px, dpx = self.psr.nxt()
                        self.mm(px[:, 0:128], Xt[:], X[:], True, True, [dX, dXt], [dpx])
                        self.cp(fw.dve if pi % 2 else fw.act, Xb[pi][1 - k][:], px[:, 0:128], [dpx], [dXb[pi][1 - k]])
                    cur[pi] = 1 - k
                    pull(1)
                for pi in range(NPI):
                    k = cur[pi]
                    ak = acur[pi]
                    pa, dpa = self.psr.nxt()
                    self.mm(pa[:, 0:128], identb[:], Accb[pi][ak][:], True, False, [dsm, dAcc[pi][ak]], [dpa])
                    self.mm(pa[:, 0:128], Xtb[pi][k][:], Accb[pi][ak][:], False, True, [dXtb[pi][k], dAcc[pi][ak]], [dpa])
                    if lev < 5:
                        self.cp(fw.act if (pi + lev) % 2 else fw.dve, Accb[pi][1 - ak][:], pa[:, 0:128], [dpa], [dAcc[pi][1 - ak]])
                        acur[pi] = 1 - ak
                    else:
                        self.cp(fw.act if (pi + lev) % 2 else fw.dve, MT[pi][:], pa[:, 0:128], [dpa], [dsl[pi]])
                    pull(1)
            for pi in range(NPI):
                psh, dpsh = self.psr.nxt()
                self.mm(psh[0:64, 0:64], shb[:], MT[pi][:, 64:128], True, True, [dsm, dsl[pi]], [dpsh])
                self.cp(fw.act, MT1[pi][:], psh[0:64, 0:64], [dpsh], [dsl[pi]])
            pull(2)
            sc_.__exit__(None, None, None); sc_ = scope("rwC_chain"); sc_.__enter__()
            for c in (range(NCH) if d == 0 else range(NCH - 1, -1, -1)):
                p, e = c // 2, c % 2
                cs = slice(c * 64, (c + 1) * 64)
                wcol = c * 64 + 63 if d == 0 else c * 64
                Gs, Us = [], []
                for h in range(4):
                    ci = h * NCH + c
                    khat = KRf[:, h, p, 0, e * 64:(e + 1) * 64]
                    pgc, dpg = self.psr.nxt()
                    self.mm(pgc[0:64, 0:64], khat, Tb[:, h, :], True, False, [dKR, dTb[h]], [dpg])
                    self.mm(pgc[0:64, 0:64], Ach[ci][:, 64:128], tok3[ci][:, 128:192], False, True, [dch[ci], dAch[ci]], [dpg])
                    G, dG = gsr.nxt()
                    self.aff(G[:], pgc[0:64, 0:64], -1.0, None, [dpg], [dG])
                    Gs.append((G, dG))
                pull(2)
                for h in range(4):
                    pi = h * NP + p
                    G, dG = Gs[h]
                    MTc = MT[pi][0:64, 0:64] if e == 0 else MT1[pi][:]
                    pu, dpu = self.psr.nxt()
                    self.mm(pu[0:64, 0:64], MTc, G[:], True, True, [dsl[pi], dG], [dpu])
                    U, dU = usr.nxt()
                    self.cp(fw.act, U[:], pu[0:64, 0:64], [dpu], [dU])
                    Us.append((U, dU))
                pull(2)
                for h in range(4):
                    ci = h * NCH + c
                    U, dU = Us[h]
                    rhat = KRf[:, h, p, 1, e * 64:(e + 1) * 64]
                    Btok, Ktok, Vtok = tok3[ci][:, 0:64], tok3[ci][:, 64:128], tok3[ci][:, 128:192]
                    ArbT, ArkT = Ach[ci][:, 0:64], Ach[ci][:, 128:192]
                    py, dpy = self.psr.nxt()
                    self.mm(py[0:64, 0:64], Tb[:, h, :], rhat, True, False, [dTb[h], dKR], [dpy])
                    self.mm(py[0:64, 0:64], U[:], ArbT, False, False, [dU, dAch[ci]], [dpy])
                    self.mm(py[0:64, 0:64], Vtok, ArkT, False, True, [dch[ci], dAch[ci]], [dpy])
                    self.cp(fw.dve, yb[:, h, cs], py[0:64, 0:64], [dpy], [dyb])
                    ptn, dptn = self.psr.nxt()
                    self.mm(ptn[0:64, 0:64], Btok, U[:], True, False, [dch[ci], dU], [dptn])
                    self.mm(ptn[0:64, 0:64], Ktok, Vtok, False, True, [dch[ci]], [dptn])
                    TW, dTW = twr.nxt()
                    self.aff(TW[:], Tst[:, h, :], Ep[:, h, wcol:wcol + 1], None, [dT[h], dEp], [dTW])
                    self.stt(Tb[:, h, :], ptn[0:64, 0:64], Ep[:, h, wcol:wcol + 1], TW[:], ALU.mult, ALU.add, [dptn, dEp, dTW], [dTb[h]])
                    self.stt(Tst[:, h, :], ptn[0:64, 0:64], Ep[:, h, wcol:wcol + 1], TW[:], ALU.mult, ALU.add, [dptn, dEp, dTW], [dT[h]])
                pull(2)
            sc_.__exit__(None, None, None)
            if S == 2 * SEG and ((d == 0 and sbi == nsb // 2 - 1) or (d == 1 and sbi == nsb // 2)):
                for h in range(4):
                    self.ts(Tst[:, h, :], Tst[:, h, :], self.link[0:64, 0:1], None, ALU.mult, None, [dT[h], self.dlink], [dT[h]])
                    self.ts(Tb[:, h, :], Tb[:, h, :], self.link[0:64, 0:1], None, ALU.mult, None, [dTb[h], self.dlink], [dTb[h]])
            if d == 0:
                dys[gt0] = Dep()
                fw.dma(fw.sp, hkt(self.ys[:, gt0:gt0 + SB]), yb[:], reads=[dyb], writes=[dys[gt0]], q="xo")
                return
            yf, dyf = B[5], dB[5]
            bon, dbon = B[0], dB[0]
            gg, dgg = B[1], dB[1]
            fw.dma(fw.sp, bon[:], hkt(ws["bon"][:, gt0:gt0 + SB]), writes=[dbon], q="x")
            fw.dma(fw.sp, gg[:], hkt(ws["g"][:, gt0:gt0 + SB]), writes=[dgg], q="x")
            self.tt(yf[:], yf[:], yb[:], ALU.add, [dyf, dyb], [dyf])
            for h in range(4):
                pm, dpm = self.psr.nxt()
                self.mm(pm[0:64, 0:SB], c64, yf[:, h, :], True, True, [dyf, self.dcst], [dpm])
                yc, dyc = tsm.nxt()
                self.tt(yc[:], yf[:, h, :], pm[0:64, 0:SB], ALU.subtract, [dyf, dpm], [dyc])
                sq, dsq = tsm.nxt()
                self.act(sq[:], yc[:], AF.Square, [dyc], [dsq])
                pv_, dpv_ = self.psr.nxt()
                self.mm(pv_[0:64, 0:SB], c64, sq[:], True, True, [dsq, self.dcst], [dpv_])
                self.act(sq[:], pv_[0:64, 0:SB], AF.Ln, [dpv_], [dsq], bias=GN_EPS)
                self.act(sq[:], sq[:], AF.Exp, [dsq], [dsq], scale=-0.5)
                self.stt(yc[:], yc[:], pr[:, h:h + 1], sq[:], ALU.mult, ALU.mult, [dyc, dsq, dpr], [dyc])
                self.stt(yc[:], yc[:], pr[:, 4 + h:5 + h], bon[:, h, :], ALU.add, ALU.add, [dyc, dbon, dpr], [dyc])
                self.tt(obuf[:, h, :], yc[:], gg[:, h, :], ALU.mult, [dyc, dgg], [dob])
                pull(1)
            fw.dma(fw.sp, hkt(self.os[384:640, gt0:gt0 + SB]), obuf[:], reads=[dob], q="xo")

        jobs = []
        for (tok0, nseg) in self.UNITS:
            S = nseg * SEG
            nsb = S // SB
            for d in range(2):
                order = range(nsb) if d == 0 else range(nsb - 1, -1, -1)
                for i, sbi in enumerate(order):
                    jobs.append(((tok0, S, d, sbi), i == 0))
        g0_ = elementwise(jobs[0][0], sets[0])
        for _ in g0_:
            pass
        for n, (job, first) in enumerate(jobs):
            nxt = elementwise(jobs[n + 1][0], sets[(n + 1) % 2]) if n + 1 < len(jobs) else iter(())

            def pull(k, g=nxt):
                for _ in range(k):
                    try:
                        next(g)
                    except StopIteration:
                        return
            matrices(job, sets[n % 2], pull, first, diag=(n == 5 and (self.debug or {}).get("diag")))
            for _ in nxt:
                pass


WSHAPES = {
    "w_ada": (2, 1024, 9216), "b_ada": (2, 9216), "norm_g": (2, 3, 1024),
    "ffn_w_in": (2, 2, 1024, 5632), "ffn_w_out": (2, 2, 2816, 1024),
    "w_mix_in": (2, 1024, 2432), "w_mix_out": (2, 1024, 1024),
    "lru_conv_w": (2, 4, 384), "lru_conv_b": (2, 384),
    "lru_w_gate_a": (2, 2, 6, 64, 64), "lru_b_gate_a": (2, 2, 384),
    "lru_w_gate_x": (2, 2, 6, 64, 64), "lru_b_gate_x": (2, 2, 384), "lru_lambda": (2, 2, 384),
    "rwkv_mu": (2, 1024), "rwkv_w_up": (2, 2, 64, 256), "rwkv_w0": (2, 2, 256),
    "rwkv_a_up": (2, 64, 256), "rwkv_a0": (2, 2, 256), "rwkv_g_up": (2, 128, 256),
    "rwkv_k_k": (2, 256), "rwkv_k_a": (2, 256), "rwkv_r_k": (2, 4, 64),
    "rwkv_ln_g": (2, 256), "rwkv_ln_b": (2, 256), "attn_q_norm": (2, 64), "attn_k_norm": (2, 64),
}


def rope_tables(positions):
    inv = (np.float32(10000.0) ** (-(np.arange(16, dtype=np.float32)) / np.float32(16))).astype(np.float32)
    row = (positions // 64).astype(np.float32)
    col = (positions % 64).astype(np.float32)
    cos = np.zeros((128, positions.shape[0]), np.float32)
    sin = np.zeros((128, positions.shape[0]), np.float32)
    for p in range(128):
        d = p % 64
        base = row if d < 32 else col
        ang = (base * inv[d % 16]).astype(np.float32)
        cos[p] = np.cos(ang)
        sin[p] = np.sin(ang)
    return cos, sin


def make_consts():
    c = np.zeros((14, 128, 128), np.float32)
    ii = np.arange(128)
    same = (ii[:, None] // 64) == (ii[None, :] // 64)
    su = (same & (ii[:, None] < ii[None, :])).astype(np.float32)
    iu = (same & (ii[:, None] <= ii[None, :])).astype(np.float32)
    c[4] = -su
    c[5] = iu
    c[6] = su
    c[7] = iu
    c[8] = -su.T
    c[9, :64, 0:64] = iu[:64, :64]
    c[9, :64, 64:128] = su[:64, :64]
    c[10, :64, 0:64] = iu[:64, :64]
    for i in range(64):
        c[11, 64 + i, i] = 1.0
    c[12, :64, 0:64] = iu[:64, :64].T
    c[12, :64, 64:128] = su[:64, :64].T
    c[13, :64, 0:64] = iu[:64, :64].T
    c[0] = np.eye(128, dtype=np.float32)
    R = np.zeros((128, 128), np.float32)
    for d in range(128):
        if d % 32 < 16:
            R[d, d + 16] = -1.0
        else:
            R[d, d - 16] = 1.0
    c[1] = R.T
    c[2, :64, :64] = 1.0 / 64
    c[2, 64:, 64:] = 1.0 / 64
    c[3] = 1.0
    return c


def core_layout(x_prompt, x_sample, c_prompt, c_sample):
    maps = []
    for core in range(NCORES):
        if core < 4:
            xs = [x_prompt[core, :SEG], x_prompt[core, SEG:], x_sample[core]]
            cs = [c_prompt[core], c_prompt[core], c_sample[core]]
            pos = np.concatenate([np.arange(2 * SEG), np.arange(SEG)])
            link = 1.0
        else:
            b = 4 + 3 * (core - 4)
            xs = [x_sample[b], x_sample[b + 1], x_sample[b + 2]]
            cs = [c_sample[b], c_sample[b + 1], c_sample[b + 2]]
            pos = np.concatenate([np.arange(SEG)] * 3)
            link = 0.0
        cos, sin = rope_tables(pos)
        maps.append({
            "x_in": np.ascontiguousarray(np.concatenate(xs, 0)),
            "c_in": np.ascontiguousarray(np.stack(cs, 0)),
            "link_in": np.full((128, 1), link, np.float32),
            "cos_in": cos, "sin_in": sin, "cst_in": make_consts(),
        })
    return maps


_NC_CACHE = {}


def kernel(**inputs):
    inputs = {k: np.asarray(v) for k, v in inputs.items()}
    if "nc" not in _NC_CACHE:
        _NC_CACHE["nc"] = K().build()
    nc = _NC_CACHE["nc"]
    maps = core_layout(inputs["x_prompt"], inputs["x_sample"], inputs["c_prompt"], inputs["c_sample"])
    for m in maps:
        for name in WSHAPES:
            m[name] = np.ascontiguousarray(inputs[name], dtype=np.float32)
    res = run_bass_kernel_spmd(nc, maps, core_ids=list(range(NCORES)))
    y_prompt = np.zeros((4, 2 * SEG, D), np.float32)
    y_sample = np.zeros((16, SEG, D), np.float32)
    for core in range(NCORES):
        y = res.results[core]["y_out"]
        if core < 4:
            y_prompt[core, :SEG] = y[:SEG]
            y_prompt[core, SEG:] = y[SEG:2 * SEG]
            y_sample[core] = y[2 * SEG:]
        else:
            b = 4 + 3 * (core - 4)
            for j in range(3):
                y_sample[b + j] = y[j * SEG:(j + 1) * SEG]
    return (y_prompt, y_sample)
```

```python
import contextlib
import numpy as np
import concourse.bass as bass
import concourse.mybir as mybir
from concourse.bass_utils import run_bass_kernel_spmd

F32 = mybir.dt.float32
BF16 = mybir.dt.bfloat16
AF = mybir.ActivationFunctionType
ALU = mybir.AluOpType

NCORES = 8
D = 1024
DFF = 2816
DIN = 2432
SEG = 2048
NSEG = 3
NT = NSEG * SEG
TT = 1024
NTILE = NT // TT
KC = D // 128
FC = DFF // 128
NZC = 20
DEPTH = 2
EPS = 1e-6


class Dep:
    __slots__ = ("w", "r")

    def __init__(self):
        self.w = None
        self.r = {}


class Eng:
    def __init__(self, fw, name, eng, is_pe=False):
        self.name = name
        self.e = eng
        self.is_pe = is_pe
        self.sem = fw.new_sem("s_" + name)
        self.cnt = 0
        self.known = {}


class FW:
    DMA_RING = 8

    def __init__(self, nc, stack):
        self.nc = nc
        self.stack = stack
        self.nsem = 0
        self.semobj = {}
        self.pe = Eng(self, "pe", nc.tensor, True)
        self.dve = Eng(self, "dve", nc.vector)
        self.act = Eng(self, "act", nc.scalar)
        self.pool = Eng(self, "pool", nc.gpsimd)
        self.sp = Eng(self, "sp", nc.sync)
        self.engs = [self.pe, self.dve, self.act, self.pool, self.sp]
        self.dmaq = {}
        self.ntens = 0
        self.ninst = 0

    def new_sem(self, name):
        s = self.stack.enter_context(self.nc.semaphore(name))
        self.nsem += 1
        self.semobj[id(s)] = s
        return s

    def sb(self, shape, dt=F32, stack=None):
        self.ntens += 1
        return (stack or self.stack).enter_context(self.nc.sbuf_tensor(f"t{self.ntens}", list(shape), dt))

    def ps(self, shape, dt=F32, stack=None):
        self.ntens += 1
        return (stack or self.stack).enter_context(self.nc.psum_tensor(f"p{self.ntens}", list(shape), dt))

    def _need(self, E, ev):
        if ev is None:
            return
        key, val = ev
        if key is E.sem and E.is_pe:
            return
        if E.known.get(id(key), 0) >= val:
            return
        E.e.wait_ge(key, val)
        self.ninst += 1
        E.known[id(key)] = val

    def _pre(self, E, reads, writes):
        for d in reads:
            self._need(E, d.w)
        for d in writes:
            self._need(E, d.w)
            for k, v in list(d.r.items()):
                self._need(E, (self.semobj[k], v))

    def _post(self, ev, reads, writes):
        key, val = ev
        for d in reads:
            if d.r.get(id(key), 0) < val:
                d.r[id(key)] = val
        for d in writes:
            d.w = ev
            d.r = {}

    def op(self, E, fn, reads=(), writes=()):
        self._pre(E, reads, writes)
        ins = fn()
        E.cnt += 1
        self.ninst += 1
        ins.then_inc(E.sem, 1)
        ev = (E.sem, E.cnt)
        self._post(ev, reads, writes)
        return ev

    def dma(self, E, out, in_, reads=(), writes=(), q="q", **kw):
        key = (E.name, q)
        if key not in self.dmaq:
            self.dmaq[key] = {"sems": [self.new_sem(f"d_{E.name}_{q}_{i}") for i in range(self.DMA_RING)], "n": 0}
        Q = self.dmaq[key]
        i = Q["n"]
        Q["n"] += 1
        sem = Q["sems"][i % self.DMA_RING]
        gen = i // self.DMA_RING
        if gen > 0:
            self._need(E, (sem, 16 * gen))
        self._pre(E, reads, writes)
        ins = E.e.dma_start(out=out, in_=in_, **kw)
        ins.then_inc(sem, 16)
        self.ninst += 1
        ev = (sem, 16 * (gen + 1))
        self._post(ev, reads, writes)
        return ev

    def all_events(self):
        evs = []
        for Q in self.dmaq.values():
            n = Q["n"]
            for s_i, sem in enumerate(Q["sems"]):
                cnt = (n - s_i + self.DMA_RING - 1) // self.DMA_RING if n > s_i else 0
                if cnt > 0:
                    evs.append((sem, 16 * cnt))
        for X in self.engs:
            if X.cnt > 0:
                evs.append((X.sem, X.cnt))
        return evs

    def barrier(self):
        evs = self.all_events()
        for E in self.engs:
            for ev in evs:
                self._need(E, ev)

    def finish(self):
        for ev in self.all_events():
            self._need(self.sp, ev)


class Ring:
    def __init__(self, items):
        self.items = [(t, Dep()) for t in items]
        self.i = 0

    def nxt(self):
        r = self.items[self.i % len(self.items)]
        self.i += 1
        return r


def zc_cols():
    ch = []
    for c in range(14):
        ch.append([(c * 128, 128)])
    for c in range(3):
        ch.append([(1792 + c * 128, 128)])
    ch.append([(2176, 64), (2176, 64)])
    ch.append([(2240, 64), (2240, 64)])
    ch.append([(2304, 128)])
    return ch


class K:
    def __init__(self, debug=False):
        self.debug = debug
        nc = self.nc = bass.Bass("TRN2", target_bir_lowering=False)
        dt = nc.dram_tensor
        self.x_in = dt("x_in", [NT, D], F32, kind="ExternalInput").ap()
        self.c_in = dt("c_in", [NSEG, D], F32, kind="ExternalInput").ap()
        self.link_in = dt("link_in", [128, 1], F32, kind="ExternalInput").ap()
        self.cos_in = dt("cos_in", [128, NT], F32, kind="ExternalInput").ap()
        self.sin_in = dt("sin_in", [128, NT], F32, kind="ExternalInput").ap()
        self.cst_in = dt("cst_in", [14, 128, 128], F32, kind="ExternalInput").ap()
        self.W = {}
        for name, shape in WSHAPES.items():
            self.W[name] = dt(name, list(shape), F32, kind="ExternalInput").ap()
        self.y_out = dt("y_out", [NT, D], F32, kind="ExternalOutput").ap()
        sk = "ExternalOutput" if debug else "Internal"
        self.xs = dt("xs", [D, NT], F32, kind=sk).ap()
        self.zs = dt("zs", [NZC * 128, NT], F32, kind="ExternalInput" if (debug and debug.get("mixer_test")) else sk).ap()
        self.os = dt("os", [D, NT], BF16, kind=sk).ap()
        self.ys = dt("ys", [256, NT], F32, kind="Internal").ap()
        self.ws = {nm: dt("ws_" + nm, [256, NT], F32, kind="Internal").ap() for nm in ("r", "kap", "v", "bon", "g", "sg0", "sg1", "kd0", "kd1", "b0", "b1")}

    def mm(self, out, lhsT, rhs, start, stop, reads, writes):
        nc = self.nc
        return self.fw.op(self.fw.pe, lambda: nc.tensor.matmul(out, lhsT, rhs, start=start, stop=stop), reads, writes)

    def act(self, out, in_, func, reads, writes, bias=None, scale=None):
        nc = self.nc
        kw = {}
        if bias is not None:
            kw["bias"] = bias
        if scale is not None:
            kw["scale"] = scale
        return self.fw.op(self.fw.act, lambda: nc.scalar.activation(out=out, in_=in_, func=func, **kw), reads, writes)

    def tt(self, out, in0, in1, op, reads, writes, E=None):
        E = E or self.fw.dve
        return self.fw.op(E, lambda: E.e.tensor_tensor(out=out, in0=in0, in1=in1, op=op), reads, writes)

    def ts(self, out, in0, s1, s2, op0, op1, reads, writes, E=None):
        E = E or self.fw.dve
        if op1 is None:
            return self.fw.op(E, lambda: E.e.tensor_scalar(out=out, in0=in0, scalar1=s1, scalar2=None, op0=op0), reads, writes)
        return self.fw.op(E, lambda: E.e.tensor_scalar(out=out, in0=in0, scalar1=s1, scalar2=s2, op0=op0, op1=op1), reads, writes)

    def stt(self, out, in0, scalar, in1, op0, op1, reads, writes):
        nc = self.nc
        return self.fw.op(self.fw.dve, lambda: nc.vector.scalar_tensor_tensor(out=out, in0=in0, scalar=scalar, in1=in1, op0=op0, op1=op1), reads, writes)

    def aff(self, out, in_, scale, bias, reads, writes):
        kw = {}
        if bias is not None:
            kw["bias"] = bias
        return self.fw.op(self.fw.act, lambda: self.nc.scalar.activation(out=out, in_=in_, func=AF.Identity, scale=scale, **kw), reads, writes)

    def cp(self, E, out, in_, reads, writes):
        if E is self.fw.act:
            return self.fw.op(E, lambda: self.nc.scalar.copy(out=out, in_=in_), reads, writes)
        return self.fw.op(E, lambda: E.e.tensor_copy(out=out, in_=in_), reads, writes)

    def memset(self, E, ap, val, writes):
        return self.fw.op(E, lambda: E.e.memset(ap, val), (), writes)

    def vec_load(self, dst, src_ap, dep):
        return self.fw.dma(self.fw.sp, dst, src_ap, writes=[dep], q="v", allow_slow_non_contiguous=True)

    def build(self):
        nc = self.nc
        with contextlib.ExitStack() as st:
            fw = self.fw = FW(nc, st)
            ps_all = fw.ps([128, 4096])
            self.psr = Ring([ps_all[:, i * 512:(i + 1) * 512] for i in range(6)])
            self.pso = Ring([ps_all[:, i * 512:(i + 1) * 512] for i in range(6, 8)])
            self.psw_views = [ps_all[:, j * 1024:(j + 1) * 1024] for j in range(3)]
            self.psq_views = [ps_all[:, j * 128:(j + 1) * 128] for j in range(16)]
            self.psh_views = [ps_all[:, 2048 + j * 256:2048 + (j + 1) * 256] for j in range(8)]
            self.setup_consts()
            if self.debug and self.debug.get("mixer_test"):
                with contextlib.ExitStack() as pst:
                    self.mixer_phase(pst, 0)
                fw.finish()
                return nc
            with nc.named_scope("phase0"):
                self.phase0()
            fw.barrier()
            for l in range(DEPTH):
                with contextlib.ExitStack() as pst, nc.named_scope(f"tok{l}"):
                    self.token_phase(pst, l)
                fw.barrier()
                if self.debug and self.debug.get("stop_after_A"):
                    break
                with contextlib.ExitStack() as pst:
                    self.mixer_phase(pst, l)
                fw.barrier()
            if not (self.debug and self.debug.get("stop_after_A")):
                with contextlib.ExitStack() as pst, nc.named_scope(f"tok{DEPTH}"):
                    self.token_phase(pst, DEPTH)
            fw.finish()
        return nc

    def setup_consts(self):
        fw = self.fw
        nc = self.nc
        self.cst = fw.sb([128, 14, 128])
        self.dcst = Dep()
        fw.dma(fw.sp, self.cst[:], self.cst_in.rearrange("c p n -> p c n"), writes=[self.dcst], q="v")
        self.ident = self.cst[:, 0, :]
        self.onesD = fw.sb([128, 128], BF16)
        self.blk64 = fw.sb([128, 128], BF16)
        self.dconst2 = Dep()
        self.ts(self.onesD[:], self.cst[:, 3, :], 1.0 / D, None, ALU.mult, None, [self.dcst], [self.dconst2])
        self.cp(fw.dve, self.blk64[:], self.cst[:, 2, :], [self.dcst], [self.dconst2])
        self.link = fw.sb([128, 1])
        self.dlink = Dep()
        fw.dma(fw.sp, self.link[:], self.link_in, writes=[self.dlink], q="v")

    def phase0(self):
        fw = self.fw
        nc = self.nc
        W = self.W
        self.mod = [fw.sb([128, 72, NSEG]) for _ in range(DEPTH)]
        self.dmod = [Dep() for _ in range(DEPTH)]
        self.gsc = [fw.sb([128, 3, KC, NSEG]) for _ in range(DEPTH)]
        self.gate = [fw.sb([128, 3, KC, NSEG]) for _ in range(DEPTH)]
        with contextlib.ExitStack() as pst:
            cT = fw.sb([128, KC, NSEG], stack=pst)
            dc = Dep()
            for s_ in range(NSEG):
                self.vec_load(cT[:, :, s_], self.c_in[s_].rearrange("(kc p) -> p kc", p=128), dc)
            sc = fw.sb([128, KC, NSEG], stack=pst)
            self.act(sc[:], cT[:], AF.Silu, [dc], [dc])
            wr = Ring([fw.sb([128, KC, 512], stack=pst) for _ in range(4)])
            for l in range(DEPTH):
                bT = fw.sb([128, 72], stack=pst)
                db = Dep()
                self.vec_load(bT[:], W["b_ada"][l].rearrange("(c p) -> p c", p=128), db)
                gT = fw.sb([128, 3, KC], stack=pst)
                dg = Dep()
                for i_ in range(3):
                    self.vec_load(gT[:, i_, :], W["norm_g"][l, i_].rearrange("(c p) -> p c", p=128), dg)
                wv = W["w_ada"][l].rearrange("(kc p) n -> p kc n", p=128)
                for blk in range(18):
                    wt, dw = wr.nxt()
                    fw.dma(fw.sp if blk % 2 == 0 else fw.act, wt[:], wv[:, :, blk * 512:(blk + 1) * 512], writes=[dw], q="w")
                    for j in range(4):
                        fcn = blk * 4 + j
                        pt, dp = self.psr.nxt()
                        for k in range(KC):
                            self.mm(pt[:, 0:NSEG], wt[:, k, j * 128:(j + 1) * 128], sc[:, k, :], k == 0, k == KC - 1, [dw, dc], [dp])
                        self.ts(self.mod[l][:, fcn, :], pt[:, 0:NSEG], bT[:, fcn:fcn + 1], None, ALU.add, None, [dp, db], [self.dmod[l]])
                m = self.mod[l]
                for i in range(3):
                    for s in range(NSEG):
                        self.stt(self.gsc[l][:, i, :, s], m[:, (3 * i + 1) * 8:(3 * i + 2) * 8, s], 1.0, gT[:, i, :], ALU.add, ALU.mult, [self.dmod[l], dg], [self.dmod[l]])
                    self.ts(self.gate[l][:, i, :, :], m[:, (3 * i + 2) * 8:(3 * i + 3) * 8, :], 1.0 if i == 1 else 0.5, None, ALU.mult, None, [self.dmod[l]], [self.dmod[l]])
            fw.barrier()

    def shift_ap(self, l, i, c, seg):
        return self.mod[l][:, 3 * i * 8 + c, seg:seg + 1]

    def token_phase(self, pst, l):
        fw = self.fw
        nc = self.nc
        self.xT = fw.sb([128, KC, TT], stack=pst)
        self.dx = Dep()
        self.hT = fw.sb([128, KC, TT], BF16, stack=pst)
        self.dh = Dep()
        self.oT = fw.sb([128, KC, TT], BF16, stack=pst)
        self.do_ = Dep()
        self.aT = fw.sb([128, FC, TT], BF16, stack=pst)
        self.da = Dep()
        self.sqr = Ring([fw.sb([128, 512], BF16, stack=pst) for _ in range(3)])
        self.f32r = Ring([fw.sb([128, 512], F32, stack=pst) for _ in range(4)])
        self.rstd = fw.sb([128, 512], stack=pst)
        self.drstd = Dep()
        self.wr = Ring([fw.sb([128, KC, 512], BF16, stack=pst) for _ in range(3)])
        self.wor = Ring([fw.sb([128, FC, 256], BF16, stack=pst) for _ in range(2)])
        self.iot = Ring([fw.sb([128, 512], F32, stack=pst) for _ in range(3)])
        xsv = self.xs.rearrange("(c p) t -> p c t", p=128)
        osv = self.os.rearrange("(c p) t -> p c t", p=128)

        def load_x(ti):
            fw.dma(fw.sp, self.xT[:], xsv[:, :, ti * TT:(ti + 1) * TT], writes=[self.dx], q="x")

        def load_o(ti):
            fw.dma(fw.sp, self.oT[:], osv[:, :, ti * TT:(ti + 1) * TT], writes=[self.do_], q="x")

        for ti in range(NTILE):
            seg = ti // (SEG // TT)
            t0 = ti * TT
            if l == 0:
                self.load_x_input(t0)
            else:
                if ti == 0:
                    load_x(0)
                    load_o(0)
                self.out_proj(self.oT, self.do_, KC, self.W["w_mix_out"][l - 1].rearrange("(c p) n -> p c n", p=128), l - 1, 1, seg)
                if ti + 1 < NTILE:
                    load_o(ti + 1)
                self.norm_mod(l - 1, 2, seg)
                self.ffn(l - 1, 1, 2, seg)
            if l < DEPTH:
                self.norm_mod(l, 0, seg)
                self.ffn(l, 0, 0, seg)
                fw.dma(fw.sp, xsv[:, :, t0:t0 + TT], self.xT[:], reads=[self.dx], q="xo")
                self.norm_mod(l, 1, seg)
                if l >= 1 and ti + 1 < NTILE:
                    load_x(ti + 1)
                self.mix_in(l, t0)
            else:
                self.store_y(t0)
                if ti + 1 < NTILE:
                    load_x(ti + 1)

    def load_x_input(self, t0):
        fw = self.fw
        nc = self.nc
        for b in range(TT // 128):
            it, di = self.iot.nxt()
            it2, di2 = self.iot.nxt()
            for hf, (tt_, dd) in enumerate(((it, di), (it2, di2))):
                fw.dma(fw.sp, tt_[:], self.x_in[t0 + b * 128:t0 + (b + 1) * 128, hf * 512:(hf + 1) * 512], writes=[dd], q="x")
            for hf, (tt_, dd) in enumerate(((it, di), (it2, di2))):
                pt, dp = self.psr.nxt()
                for j in range(4):
                    self.fw.op(fw.pe, lambda j=j: nc.tensor.transpose(pt[:, j * 128:(j + 1) * 128], tt_[:, j * 128:(j + 1) * 128], self.ident), [dd, self.dcst], [dp])
                E = fw.act if hf == 0 else fw.dve
                self.cp(E, self.xT[:, hf * 4:(hf + 1) * 4, b * 128:(b + 1) * 128], pt[:].rearrange("p (j t) -> p j t", j=4), [dp], [self.dx])

    def store_y(self, t0):
        fw = self.fw
        nc = self.nc
        for b in range(TT // 128):
            for hf in range(2):
                pt, dp = self.psr.nxt()
                for j in range(4):
                    c = hf * 4 + j
                    self.fw.op(fw.pe, lambda j=j, c=c: nc.tensor.transpose(pt[:, j * 128:(j + 1) * 128], self.xT[:, c, b * 128:(b + 1) * 128], self.ident), [self.dx, self.dcst], [dp])
                ot, do = self.iot.nxt()
                E = fw.act if hf == 0 else fw.dve
                self.cp(E, ot[:], pt[:], [dp], [do])
                fw.dma(fw.sp, self.y_out[t0 + b * 128:t0 + (b + 1) * 128, hf * 512:(hf + 1) * 512], ot[:], reads=[do], q="xo")

    def norm_mod(self, l, i, seg):
        fw = self.fw
        for hf in range(TT // 512):
            sl = slice(hf * 512, (hf + 1) * 512)
            pt, dp = self.psr.nxt()
            for c in range(KC):
                sq, dsq = self.sqr.nxt()
                self.act(sq[:], self.xT[:, c, sl], AF.Square, [self.dx], [dsq])
                self.mm(pt[:], self.onesD[:], sq[:], c == 0, c == KC - 1, [dsq, self.dconst2], [dp])
            t1, d1 = self.f32r.nxt()
            self.act(t1[:], pt[:], AF.Ln, [dp], [d1], bias=EPS)
            self.act(self.rstd[:], t1[:], AF.Exp, [d1], [self.drstd], scale=-0.5)
            for c in range(KC):
                t2, d2 = self.f32r.nxt()
                self.stt(t2[:], self.xT[:, c, sl], self.gsc[l][:, i, c, seg:seg + 1], self.rstd[:], ALU.mult, ALU.mult, [self.dx, self.drstd, self.dmod[l]], [d2])
                self.act(self.hT[:, c, sl], t2[:], AF.Identity, [d2, self.dmod[l]], [self.dh], bias=self.shift_ap(l, i, c, seg))

    def ffn(self, l, which, i, seg):
        fw = self.fw
        nc = self.nc
        wv = self.W["ffn_w_in"][l, which].rearrange("(kc p) n -> p kc n", p=128)
        for j in range(FC // 2):
            wt, dw = self.wr.nxt()
            fw.dma(fw.pool, wt[:, :, 0:256], wv[:, :, j * 256:(j + 1) * 256], writes=[dw], q="w")
            fw.dma(fw.pool, wt[:, :, 256:512], wv[:, :, DFF + j * 256:DFF + (j + 1) * 256], writes=[dw], q="w")
            for fc in range(2):
                for hf in range(TT // 512):
                    sl = slice(hf * 512, (hf + 1) * 512)
                    pg, dpg = self.psr.nxt()
                    pu, dpu = self.psr.nxt()
                    for k in range(KC):
                        self.mm(pg[:], wt[:, k, fc * 128:(fc + 1) * 128], self.hT[:, k, sl], k == 0, k == KC - 1, [dw, self.dh], [dpg])
                    for k in range(KC):
                        self.mm(pu[:], wt[:, k, 256 + fc * 128:256 + (fc + 1) * 128], self.hT[:, k, sl], k == 0, k == KC - 1, [dw, self.dh], [dpu])
                    sg, dsg = self.f32r.nxt()
                    self.act(sg[:], pg[:], AF.Silu, [dpg], [dsg])
                    self.tt(self.aT[:, 2 * j + fc, sl], sg[:], pu[:], ALU.mult, [dsg, dpu], [self.da])
        self.out_proj(self.aT, self.da, FC, self.W["ffn_w_out"][l, which].rearrange("(c p) n -> p c n", p=128), l, i, seg)

    def out_proj(self, src, dsrc, nk, wv, l, i, seg):
        fw = self.fw
        for dp2 in range(KC // 2):
            wt, dw = self.wor.nxt()
            fw.dma(fw.pool, wt[:, 0:nk, :], wv[:, :, dp2 * 256:(dp2 + 1) * 256], writes=[dw], q="w")
            for dc in range(2):
                c = dp2 * 2 + dc
                for hf in range(TT // 512):
                    sl = slice(hf * 512, (hf + 1) * 512)
                    pt, dp = self.psr.nxt()
                    for k in range(nk):
                        self.mm(pt[:], wt[:, k, dc * 128:(dc + 1) * 128], src[:, k, sl], k == 0, k == nk - 1, [dw, dsrc], [dp])
                    self.stt(self.xT[:, c, sl], pt[:], self.gate[l][:, i, c, seg:seg + 1], self.xT[:, c, sl], ALU.mult, ALU.add, [dp, self.dx, self.dmod[l]], [self.dx])

    def mix_in(self, l, t0):
        fw = self.fw
        wv = self.W["w_mix_in"][l].rearrange("(kc p) n -> p kc n", p=128)
        for zc, cols in enumerate(zc_cols()):
            wt, dw = self.wr.nxt()
            o = 0
            for (c0, n) in cols:
                fw.dma(fw.pool, wt[:, :, o:o + n], wv[:, :, c0:c0 + n], writes=[dw], q="w")
                o += n
            for hf in range(TT // 512):
                sl = slice(hf * 512, (hf + 1) * 512)
                pt, dp = self.psr.nxt()
                for k in range(KC):
                    self.mm(pt[:], wt[:, k, 0:128], self.hT[:, k, sl], k == 0, k == KC - 1, [dw, self.dh], [dp])
                ot, do = self.iot.nxt()
                self.cp(fw.act if hf == 0 else fw.dve, ot[:], pt[:], [dp], [do])
                fw.dma(fw.sp, self.zs[zc * 128:(zc + 1) * 128, t0 + hf * 512:t0 + (hf + 1) * 512], ot[:], reads=[do], q="xo")

    def mixer_phase(self, pst, l):
        fw = self.fw
        which = (self.debug or {}).get("mixers", ("lru", "att", "rwkv"))
        if "lru" in which:
            with contextlib.ExitStack() as st2, self.nc.named_scope(f"lru{l}"):
                self.lru(st2, l)
            fw.barrier()
        if "att" in which:
            with contextlib.ExitStack() as st2, self.nc.named_scope(f"att{l}"):
                self.attention(st2, l)
            fw.barrier()
        if "rwkv" in which:
            with contextlib.ExitStack() as st2, self.nc.named_scope(f"rwkv{l}"):
                self.rwkv(st2, l)
            fw.barrier()

    UNITS = ((0, 2), (2 * SEG, 1))

    def lru(self, st, l):
        fw = self.fw
        nc = self.nc
        W = self.W
        SA = 2 * SEG
        pv = fw.sb([128, 3, 16], stack=st)
        dpv = Dep()
        col1 = lambda ap: ap.rearrange("(p o) -> p o", o=1)
        for c in range(3):
            cs = slice(c * 128, (c + 1) * 128)
            self.vec_load(pv[:, c, 0:4], W["lru_conv_w"][l][:, cs].rearrange("j p -> p j"), dpv)
            self.vec_load(pv[:, c, 4:5], col1(W["lru_conv_b"][l, cs]), dpv)
            for d in range(2):
                self.vec_load(pv[:, c, 5 + d:6 + d], col1(W["lru_b_gate_a"][l, d, cs]), dpv)
                self.vec_load(pv[:, c, 7 + d:8 + d], col1(W["lru_b_gate_x"][l, d, cs]), dpv)
                self.vec_load(pv[:, c, 9 + d:10 + d], col1(W["lru_lambda"][l, d, cs]), dpv)
        self.act(pv[:, :, 11:13], pv[:, :, 9:11], AF.Exp, [dpv], [dpv], scale=-1.0)
        self.act(pv[:, :, 11:13], pv[:, :, 11:13], AF.Ln, [dpv], [dpv], bias=1.0)
        self.ts(pv[:, :, 13:15], pv[:, :, 11:13], -16.0, None, ALU.mult, None, [dpv], [dpv])
        self.ts(pv[:, :, 11:13], pv[:, :, 11:13], -8.0, None, ALU.mult, None, [dpv], [dpv])
        w32 = fw.sb([128, 4, 128], stack=st)
        dw32 = Dep()
        wbd = fw.sb([128, 4, 128], BF16, stack=st)
        dwbd = Dep()
        xpr = Ring([fw.sb([128, 2, SEG + 3], stack=st) for _ in range(2)])
        ybr = Ring([fw.sb([128, SA], stack=st) for _ in range(2)])
        xc = fw.sb([128, SA], stack=st)
        dxc = Dep()
        xcb = fw.sb([128, SA], BF16, stack=st)
        dxcb = Dep()
        bufs = [fw.sb([128, SA], stack=st) for _ in range(5)]
        dbs = [Dep() for _ in range(5)]
        h0, h1 = bufs[3], bufs[4]
        dh0, dh1 = dbs[3], dbs[4]
        pre = [None]

        def prefetch(c, ui):
            tok0, nseg = self.UNITS[ui]
            S = nseg * SEG
            cs = slice(c * 128, (c + 1) * 128)
            xpad, dxp = xpr.nxt()
            ybt, dyb = ybr.nxt()
            xp = xpad[:, 0:nseg, :]
            self.memset(fw.pool, xp[:, :, 0:2], 0.0, [dxp])
            self.memset(fw.pool, xp[:, :, SEG + 2:SEG + 3], 0.0, [dxp])
            fw.dma(fw.sp, xp[:, :, 2:SEG + 2], self.zs[cs, tok0:tok0 + S].rearrange("p (s t) -> p s t", s=nseg), writes=[dxp], q="x")
            fw.dma(fw.sp, ybt[:, 0:S], self.zs[384 + c * 128:384 + (c + 1) * 128, tok0:tok0 + S], writes=[dyb], q="x")
            return xpad, dxp, ybt, dyb

        for c in range(3):
            cs = slice(c * 128, (c + 1) * 128)
            self.memset(fw.pool, w32[:], 0.0, [dw32])
            for d in range(2):
                for gi, nm in enumerate(("lru_w_gate_a", "lru_w_gate_x")):
                    for n in range(2):
                        fw.dma(fw.sp, w32[n * 64:(n + 1) * 64, d * 2 + gi, n * 64:(n + 1) * 64], W[nm][l, d, 2 * c + n], writes=[dw32], q="v")
            self.cp(fw.pool, wbd[:], w32[:], [dw32], [dwbd])
            for ui, (tok0, nseg) in enumerate(self.UNITS):
                S = nseg * SEG
                if pre[0] is None:
                    pre[0] = prefetch(c, ui)
                xpad, dxp, ybt, dyb = pre[0]
                nc_, nu_ = (c, ui + 1) if ui + 1 < len(self.UNITS) else (c + 1, 0)
                pre[0] = prefetch(nc_, nu_) if nc_ < 3 else None
                xp = xpad[:, 0:nseg, :]
                if nseg == 2:
                    self.ts(xpad[:, 1, 0:2], xpad[:, 0, SEG:SEG + 2], self.link[:, 0:1], None, ALU.mult, None, [dxp, self.dlink], [dxp])
                    self.ts(xpad[:, 0, SEG + 2:SEG + 3], xpad[:, 1, 2:3], self.link[:, 0:1], None, ALU.mult, None, [dxp, self.dlink], [dxp])
                xc3 = xc[:, 0:S].rearrange("p (s t) -> p s t", s=nseg)
                self.ts(xc3, xp[:, :, 0:SEG], pv[:, c, 0:1], pv[:, c, 4:5], ALU.mult, ALU.add, [dxp, dpv], [dxc])
                for j in range(1, 4):
                    self.stt(xc3, xp[:, :, j:j + SEG], pv[:, c, j:j + 1], xc3, ALU.mult, ALU.add, [dxp, dpv, dxc], [dxc])
                self.cp(fw.act, xcb[:, 0:S], xc[:, 0:S], [dxc], [dxcb])
                for d in range(2):
                    hb, dhb = (h0, dh0) if d == 0 else (h1, dh1)
                    b1, b2, b3 = bufs[0:3]
                    d1, d2, d3 = dbs[0:3]
                    for gi, (bt, dbt) in enumerate(((b1, d1), (b2, d2))):
                        for blk in range(S // 512):
                            sl = slice(blk * 512, (blk + 1) * 512)
                            pt, dp = self.psr.nxt()
                            self.mm(pt[:], wbd[:, d * 2 + gi, :], xcb[:, sl], True, True, [dwbd, dxcb], [dp])
                            self.act(bt[:, sl], pt[:], AF.Sigmoid, [dp, dpv], [dbt], bias=pv[:, c, 5 + 2 * gi + d:6 + 2 * gi + d])
                    self.act(b3[:, 0:S], b1[:, 0:S], AF.Exp, [d1, dpv], [d3], scale=pv[:, c, 11 + d:12 + d])
                    self.act(b1[:, 0:S], b1[:, 0:S], AF.Exp, [d1, dpv], [d1], scale=pv[:, c, 13 + d:14 + d])
                    self.act(b1[:, 0:S], b1[:, 0:S], AF.Sqrt, [d1], [d1], scale=-1.0, bias=1.0)
                    self.tt(b2[:, 0:S], b2[:, 0:S], xc[:, 0:S], ALU.mult, [d2, dxc], [d2])
                    self.tt(b2[:, 0:S], b2[:, 0:S], b1[:, 0:S], ALU.mult, [d2, d1], [d2])
                    if nseg == 2:
                        cp_ = SEG if d == 0 else SEG - 1
                        self.ts(b3[:, cp_:cp_ + 1], b3[:, cp_:cp_ + 1], self.link[:, 0:1], None, ALU.mult, None, [d3, self.dlink], [d3])
                    if d == 0:
                        self.fw.op(fw.dve, lambda: nc.vector.tensor_tensor_scan(out=hb[:, 0:S], data0=b3[:, 0:S], data1=b2[:, 0:S], initial=0.0, op0=ALU.mult, op1=ALU.add), [d3, d2], [dhb])
                    else:
                        self.fw.op(fw.dve, lambda: nc.vector.tensor_tensor_scan(out=hb[:, S - 1::-1] if False else hb[:, 0:S][:, ::-1], data0=b3[:, 0:S][:, ::-1], data1=b2[:, 0:S][:, ::-1], initial=0.0, op0=ALU.mult, op1=ALU.add), [d3, d2], [dhb])
                b1, b2 = ybt, bufs[1]
                d1, d2 = dyb, dbs[1]
                self.act(b2[:, 0:S], b1[:, 0:S], AF.Square, [d1], [d2])
                self.ts(b2[:, 0:S], b2[:, 0:S], 0.044715, 1.0, ALU.mult, ALU.add, [d2], [d2])
                self.tt(b2[:, 0:S], b2[:, 0:S], b1[:, 0:S], ALU.mult, [d2, d1], [d2])
                self.act(b2[:, 0:S], b2[:, 0:S], AF.Sigmoid, [d2], [d2], scale=1.5957691216057308)
                self.tt(b1[:, 0:S], b1[:, 0:S], b2[:, 0:S], ALU.mult, [d1, d2], [d1])
                self.tt(h0[:, 0:S], h0[:, 0:S], h1[:, 0:S], ALU.add, [dh0, dh1], [dh0])
                self.tt(xcb[:, 0:S], h0[:, 0:S], b1[:, 0:S], ALU.mult, [dh0, d1], [dxcb])
                fw.dma(fw.sp, self.os[cs, tok0:tok0 + S], xcb[:, 0:S], reads=[dxcb], q="xo")

    def attention(self, st, l):
        fw = self.fw
        nc = self.nc
        W = self.W
        SA = 2 * SEG
        onesf = self.cst[:, 3, :]
        psw = Ring(self.psw_views)
        rotT = self.cst[:, 1, :]
        gv = fw.sb([128, 8], stack=st)
        dgv = Dep()
        col1 = lambda ap: ap.rearrange("(p o) -> p o", o=1)
        for hh in range(2):
            self.vec_load(gv[hh * 64:(hh + 1) * 64, 0:1], col1(W["attn_q_norm"][l]), dgv)
            self.vec_load(gv[hh * 64:(hh + 1) * 64, 1:2], col1(W["attn_k_norm"][l]), dgv)
        self.ts(gv[:, 0:1], gv[:, 0:1], 0.125, None, ALU.mult, None, [dgv], [dgv])
        rows = fw.sb([1, 132], stack=st)
        drw = Dep()
        fw.dma(fw.sp, rows[0:1, 0:64], W["attn_q_norm"][l].rearrange("(o d) -> o d", o=1), writes=[drw], q="v")
        fw.dma(fw.sp, rows[0:1, 64:128], W["attn_k_norm"][l].rearrange("(o d) -> o d", o=1), writes=[drw], q="v")
        self.fw.op(fw.dve, lambda: nc.vector.tensor_reduce(out=rows[0:1, 128:130], in_=rows[0:1, 0:128].rearrange("o (a d) -> o a d", a=2), axis=mybir.AxisListType.X, op=ALU.max, apply_absolute_value=True), [drw], [drw])
        self.tt(rows[0:1, 130:131], rows[0:1, 128:129], rows[0:1, 129:130], ALU.mult, [drw], [drw])
        self.cp(fw.dve, rows[0:1, 131:132], rows[0:1, 130:131], [drw], [drw])
        pt, dp = psw.nxt()
        self.mm(pt[:, 0:2], onesf[0:1, :], rows[0:1, 130:132], True, True, [drw, self.dcst], [dp])
        self.ts(gv[:, 2:3], pt[:, 0:1], -8.0, None, ALU.mult, None, [dp], [dgv])
        self.ts(gv[:, 3:4], self.link[:, 0:1], 30000.0, -30000.0, ALU.mult, ALU.add, [self.dlink], [dgv])
        self.tt(gv[:, 3:4], gv[:, 3:4], gv[:, 2:3], ALU.add, [dgv], [dgv])
        cosT = fw.sb([128, SA], stack=st)
        sinT = fw.sb([128, SA], stack=st)
        dcs = Dep()
        srcr = Ring([fw.sb([128, SA], stack=st) for _ in range(2)])
        kT = [[fw.sb([128, SA], BF16, stack=st) for _ in range(2)] for _ in range(2)]
        dkT = [Dep(), Dep()]
        for kv_ in range(2):
            self.memset(fw.pool, kT[kv_][0][64:128, :], 0.0, [dkT[kv_]])
            self.memset(fw.pool, kT[kv_][1][0:64, :], 0.0, [dkT[kv_]])
        vtok = fw.sb([128, SA // 128, 2, 192], BF16, stack=st)
        dvt = Dep()
        qT = fw.sb([128, SA], BF16, stack=st)
        dqT = Dep()
        och = fw.sb([128, SA], BF16, stack=st)
        doc = Dep()
        pTr = Ring([fw.sb([128, 1024], BF16, stack=st) for _ in range(4)])
        osbr = Ring([fw.sb([128, 512], stack=st) for _ in range(2)])
        lnrr = Ring([fw.sb([128, 512], stack=st) for _ in range(2)])
        tail = [None]
        sqr = Ring([fw.sb([128, 512], BF16, stack=st) for _ in range(2)])
        f32r = Ring([fw.sb([128, 512], stack=st) for _ in range(6)])
        osb = fw.sb([128, 512], stack=st)
        dosb = Dep()
        lnr = fw.sb([128, 512], stack=st)
        dlnr = Dep()
        self.memset(fw.pool, vtok[:], 0.0, [dvt])
        self.memset(fw.pool, vtok[:, :, :, 64:65], 1.0, [dvt])

        def norm_rope(src, dsrc, dst, ddst, gcol, S, tok0):
            for blk in range(S // 512):
                sl = slice(blk * 512, (blk + 1) * 512)
                sq, dsq = sqr.nxt()
                self.act(sq[:], src[:, sl], AF.Square, [dsrc], [dsq])
                p1, dp1 = psw.nxt()
                self.mm(p1[:, 0:512], self.blk64[:], sq[:], True, True, [dsq, self.dconst2], [dp1])
                t, dt_ = f32r.nxt()
                self.act(t[:], p1[:, 0:512], AF.Ln, [dp1], [dt_], bias=EPS)
                self.act(t[:], t[:], AF.Exp, [dt_], [dt_], scale=-0.5)
                qn, dqn = f32r.nxt()
                self.stt(qn[:], src[:, sl], gv[:, gcol:gcol + 1], t[:], ALU.mult, ALU.mult, [dsrc, dgv, dt_], [dqn])
                p2, dp2 = psw.nxt()
                self.mm(p2[:, 0:512], rotT, qn[:], True, True, [dqn, self.dcst], [dp2])
                t1, dt1 = f32r.nxt()
                self.tt(t1[:], qn[:], cosT[:, sl], ALU.mult, [dqn, dcs], [dt1], )
                self.tt(t[:], p2[:, 0:512], sinT[:, sl], ALU.mult, [dp2, dcs, dt_], [dt_])
                if isinstance(dst, list):
                    self.tt(dst[0][0:64, sl], t1[0:64, :], t[0:64, :], ALU.add, [dt1, dt_], [ddst])
                    self.tt(dst[1][64:128, sl], t1[64:128, :], t[64:128, :], ALU.add, [dt1, dt_], [ddst])
                else:
                    self.tt(dst[:, sl], t1[:], t[:], ALU.add, [dt1, dt_], [ddst])

        for (tok0, nseg) in self.UNITS:
            S = nseg * SEG
            fw.dma(fw.sp, cosT[:, 0:S], self.cos_in[:, tok0:tok0 + S], writes=[dcs], q="x")
            fw.dma(fw.sp, sinT[:, 0:S], self.sin_in[:, tok0:tok0 + S], writes=[dcs], q="x")
            for kv in range(2):
                src, dsrc = srcr.nxt()
                fw.dma(fw.sp, src[:, 0:S], self.zs[(17 + kv) * 128:(18 + kv) * 128, tok0:tok0 + S], writes=[dsrc], q="x")
                norm_rope(src, dsrc, kT[kv], dkT[kv], 1, S, tok0)
            src, dsrc = srcr.nxt()
            fw.dma(fw.sp, src[:, 0:S], self.zs[19 * 128:20 * 128, tok0:tok0 + S], writes=[dsrc], q="x")
            for b0 in range(0, S // 128, 4):
                pt, dp = psw.nxt()
                for j in range(4):
                    self.fw.op(fw.pe, lambda j=j: nc.tensor.transpose(pt[:, j * 128:(j + 1) * 128], src[:, (b0 + j) * 128:(b0 + j + 1) * 128], self.ident), [dsrc, self.dcst], [dp])
                pv4 = pt[:, 0:512].rearrange("p (j k d) -> p j k d", j=4, k=2)
                self.cp(fw.act, vtok[:, b0:b0 + 4, :, 0:64], pv4, [dp], [dvt])
                self.cp(fw.dve, vtok[:, b0:b0 + 4, :, 128:192], pv4, [dp], [dvt])
            for qc in range(3):
                src, dsrc = srcr.nxt()
                fw.dma(fw.sp, src[:, 0:S], self.zs[(14 + qc) * 128:(15 + qc) * 128, tok0:tok0 + S], writes=[dsrc], q="x")
                norm_rope(src, dsrc, qT, dqT, 0, S, tok0)
                for qb in range(S // 512):
                    qs = slice(qb * 512, (qb + 1) * 512)
                    for e in range(2):
                        h = 2 * qc + e
                        kv = h // 3
                        es = slice(e * 64, (e + 1) * 64)
                        po, dpo = self.pso.nxt()
                        nk = S // 128
                        LOOK = 2
                        pend = []
                        nk2 = nk // 2
                        for k2i in range(nk2 + LOOK):
                            if k2i < nk2:
                                ps_, dps = psw.nxt()
                                for u in range(2):
                                    kc = 2 * k2i + u
                                    self.mm(ps_[:, u * 512:(u + 1) * 512], kT[kv][e][:, kc * 128:(kc + 1) * 128], qT[:, qs], True, True, [dkT[kv], dqT], [dps])
                                pT, dpT = pTr.nxt()
                                same = (nseg == 1) or ((qb // 4) == ((2 * k2i) // 16))
                                bc = 2 if same else 3
                                self.act(pT[:], ps_[:], AF.Exp, [dps, dgv], [dpT], bias=gv[:, bc:bc + 1])
                                pend.append((pT, dpT, k2i))
                            if k2i == LOOK - 1 and tail[0] is not None:
                                tail[0]()
                                tail[0] = None
                            if k2i >= LOOK:
                                pT, dpT, kk2 = pend.pop(0)
                                for u in range(2):
                                    k2 = 2 * kk2 + u
                                    if e == 0:
                                        self.mm(po[0:65, :], vtok[:, k2, kv, 0:65], pT[:, u * 512:(u + 1) * 512], k2 == 0, k2 == nk - 1, [dvt, dpT], [dpo])
                                    else:
                                        self.mm(po[:, :], vtok[:, k2, kv, 64:192], pT[:, u * 512:(u + 1) * 512], k2 == 0, k2 == nk - 1, [dvt, dpT], [dpo])

                        def mk_tail(po=po, dpo=dpo, e=e, es=es, qs=qs):
                            def f():
                                r0 = 64 if e == 0 else 0
                                lnr, dlnr = lnrr.nxt()
                                self.act(lnr[r0:r0 + 1, :], po[r0:r0 + 1, :], AF.Ln, [dpo], [dlnr])
                                self.act(lnr[r0:r0 + 1, :], lnr[r0:r0 + 1, :], AF.Exp, [dlnr], [dlnr], scale=-1.0)
                                pb, dpb = psw.nxt()
                                self.mm(pb[:, 0:512], onesf[r0:r0 + 1, :], lnr[r0:r0 + 1, :], True, True, [dlnr, self.dcst], [dpb])
                                osb, dosb = osbr.nxt()
                                self.cp(fw.act, osb[es, :], po[es, :], [dpo], [dosb])
                                self.tt(och[es, qs], osb[es, :], pb[es, 0:512], ALU.mult, [dosb, dpb], [doc])
                            return f
                        tail[0] = mk_tail()
                if tail[0] is not None:
                    tail[0]()
                    tail[0] = None
                fw.dma(fw.sp, self.os[640 + qc * 128:640 + (qc + 1) * 128, tok0:tok0 + S], och[:, 0:S], reads=[doc], q="xo")

    def rwkv_pre(self, st, l):
        fw = self.fw
        nc = self.nc
        W = self.W
        ws = self.ws
        BL = 512
        col1 = lambda ap: ap.rearrange("(p o) -> p o", o=1)
        cp2 = lambda ap: ap.rearrange("(c p) -> p c", p=128)
        blk64f = self.cst[:, 2, :]
        pp = fw.sb([128, 64], stack=st)
        dpp = Dep()
        mu = W["rwkv_mu"][l]
        self.vec_load(pp[:, 0:8], cp2(mu), dpp)
        self.ts(pp[:, 8:16], pp[:, 0:8], 0.5, None, ALU.mult, None, [dpp], [dpp])
        self.ts(pp[:, 16:24], pp[:, 0:8], -1.0, 1.0, ALU.mult, ALU.add, [dpp], [dpp])
        for d in range(2):
            self.vec_load(pp[:, 24 + 2 * d:26 + 2 * d], cp2(W["rwkv_w0"][l, d]), dpp)
            self.vec_load(pp[:, 28 + 2 * d:30 + 2 * d], cp2(W["rwkv_a0"][l, d]), dpp)
        self.vec_load(pp[:, 32:34], cp2(W["rwkv_k_k"][l]), dpp)
        self.vec_load(pp[:, 34:36], cp2(W["rwkv_k_a"][l]), dpp)
        self.ts(pp[:, 36:38], pp[:, 34:36], -1.0, 1.0, ALU.mult, ALU.add, [dpp], [dpp])
        self.vec_load(pp[:, 38:40], cp2(W["rwkv_r_k"][l].rearrange("h k -> (h k)")), dpp)
        wup = fw.sb([64, 2, 256], stack=st)
        aup = fw.sb([128, 256], stack=st)
        gup = fw.sb([128, 256], stack=st)
        dwl = Dep()
        for d in range(2):
            fw.dma(fw.sp, wup[:, d, :], W["rwkv_w_up"][l, d], writes=[dwl], q="v")
        fw.dma(fw.sp, aup[64:128, :], W["rwkv_a_up"][l], writes=[dwl], q="v")
        fw.dma(fw.sp, gup[:], W["rwkv_g_up"][l], writes=[dwl], q="v")
        padr = Ring([fw.sb([128, 8, BL + 2], stack=st) for _ in range(2)])
        sR = Ring([fw.sb([128, 8, BL], stack=st) for _ in range(2)])
        fR = Ring([fw.sb([128, 8, BL], stack=st) for _ in range(2)])
        tR = Ring([fw.sb([128, BL], stack=st) for _ in range(12)])
        oR = Ring([fw.sb([128, BL], stack=st) for _ in range(10)])
        zv = self.zs[768:1792, :].rearrange("(c p) t -> p c t", p=128)
        def load_pad(tok0, S, bi):
            t0 = bi * BL
            g0 = tok0 + t0
            lo = 1 if t0 == 0 else 0
            hi = 1 if t0 + BL == S else 0
            n = BL + 2 - lo - hi
            pad, dpad = padr.nxt()
            if lo:
                self.memset(fw.pool, pad[:, :, 0:1], 0.0, [dpad])
            if hi:
                self.memset(fw.pool, pad[:, :, BL + 1:BL + 2], 0.0, [dpad])
            fw.dma(fw.sp, pad[:, :, lo:lo + n], zv[:, :, g0 - 1 + lo:g0 - 1 + lo + n], writes=[dpad], q="x")
            if S == 2 * SEG and (t0 == SEG or t0 + BL == SEG):
                cix = 0 if t0 == SEG else BL + 1
                self.ts(pad[:, :, cix:cix + 1], pad[:, :, cix:cix + 1], self.link[:, 0:1], None, ALU.mult, None, [dpad, self.dlink], [dpad])
            return pad, dpad
        blocks = [(tok0, nseg * SEG, bi) for (tok0, nseg) in self.UNITS for bi in range(nseg * SEG // BL)]
        nxt_pad = load_pad(*blocks[0])
        for ib, (tok0, S, bi) in enumerate(blocks):
            if True:
                t0 = bi * BL
                g0 = tok0 + t0
                sl = slice(g0, g0 + BL)
                pad, dpad = nxt_pad
                if ib + 1 < len(blocks):
                    nxt_pad = load_pad(*blocks[ib + 1])
                s_, ds_ = sR.nxt()
                f, df = fR.nxt()
                for c in range(8):
                    self.tt(s_[:, c, :], pad[:, c, 0:BL], pad[:, c, 2:BL + 2], ALU.add, [dpad], [ds_], E=fw.pool if c % 2 else fw.dve)
                    self.aff(s_[:, c, :], s_[:, c, :], pp[:, 8 + c:9 + c], None, [ds_, dpp], [ds_])
                    self.stt(f[:, c, :], pad[:, c, 1:BL + 1], pp[:, 16 + c:17 + c], s_[:, c, :], ALU.mult, ALU.add, [dpad, ds_, dpp], [df])
                for fc in range(2):
                    fw.dma(fw.sp, ws["r"][fc * 128:(fc + 1) * 128, sl], f[:, fc, :], reads=[df], q="xo")
                    fw.dma(fw.sp, ws["v"][fc * 128:(fc + 1) * 128, sl], f[:, 4 + fc, :], reads=[df], q="xo")
                tw, dtw = tR.nxt()
                self.act(tw[0:64, :], f[0:64, 6, :], AF.Tanh, [df], [dtw])
                for d in range(2):
                    for fc in range(2):
                        pt, dp = self.psr.nxt()
                        self.mm(pt[:], wup[:, d, fc * 128:(fc + 1) * 128], tw[0:64, :], True, True, [dwl, dtw], [dp])
                        o, do = oR.nxt()
                        self.act(o[:], pt[:], AF.Sigmoid, [dp, dpp], [do], bias=pp[:, 24 + 2 * d + fc:25 + 2 * d + fc])
                        fw.dma(fw.sp, ws["sg%d" % d][fc * 128:(fc + 1) * 128, sl], o[:], reads=[do], q="xo")
                kaps = []
                for fc in range(2):
                    kk, dkk = tR.nxt()
                    self.aff(kk[:], f[:, 2 + fc, :], pp[:, 32 + fc:33 + fc], None, [df, dpp], [dkk])
                    sq, dsq = tR.nxt()
                    self.act(sq[:], kk[:], AF.Square, [dkk], [dsq])
                    pt, dp = self.psr.nxt()
                    self.mm(pt[:], blk64f, sq[:], True, True, [dsq, self.dcst], [dp])
                    self.act(sq[:], pt[:], AF.Ln, [dp], [dsq], scale=64.0, bias=1e-24)
                    self.act(sq[:], sq[:], AF.Exp, [dsq], [dsq], scale=-0.5)
                    kap, dkap = oR.nxt()
                    self.tt(kap[:], kk[:], sq[:], ALU.mult, [dkk, dsq], [dkap])
                    fw.dma(fw.sp, ws["kap"][fc * 128:(fc + 1) * 128, sl], kap[:], reads=[dkap], q="xo")
                    kaps.append((kap, dkap))
                for fc in range(2):
                    pt, dp = self.psr.nxt()
                    self.mm(pt[:], aup[64:128, fc * 128:(fc + 1) * 128], f[64:128, 6, :], True, True, [dwl, df], [dp])
                    for d in range(2):
                        a_, da_ = tR.nxt()
                        self.act(a_[:], pt[:], AF.Sigmoid, [dp, dpp], [da_], bias=pp[:, 28 + 2 * d + fc:29 + 2 * d + fc])
                        t_, dt_ = tR.nxt()
                        self.aff(t_[:], a_[:], pp[:, 34 + fc:35 + fc], pp[:, 36 + fc:37 + fc], [da_, dpp], [dt_])
                        kd, dkd = oR.nxt()
                        self.tt(kd[:], t_[:], f[:, 2 + fc, :], ALU.mult, [dt_, df], [dkd])
                        fw.dma(fw.sp, ws["kd%d" % d][fc * 128:(fc + 1) * 128, sl], kd[:], reads=[dkd], q="xo")
                        b_, db_ = oR.nxt()
                        self.tt(b_[:], a_[:], kaps[fc][0][:], ALU.mult, [da_, kaps[fc][1]], [db_], E=fw.pool)
                        fw.dma(fw.sp, ws["b%d" % d][fc * 128:(fc + 1) * 128, sl], b_[:], reads=[db_], q="xo")
                for fc in range(2):
                    rk, drk = tR.nxt()
                    self.stt(rk[:], f[:, fc, :], pp[:, 38 + fc:39 + fc], f[:, 2 + fc, :], ALU.mult, ALU.mult, [df, dpp], [drk])
                    pt, dp = self.psr.nxt()
                    self.mm(pt[:], blk64f, rk[:], True, True, [drk, self.dcst], [dp])
                    bo, dbo = oR.nxt()
                    self.stt(bo[:], pt[:], 64.0, f[:, 4 + fc, :], ALU.mult, ALU.mult, [dp, df], [dbo])
                    fw.dma(fw.sp, ws["bon"][fc * 128:(fc + 1) * 128, sl], bo[:], reads=[dbo], q="xo")
                sgx, dsgx = tR.nxt()
                self.act(sgx[:], f[:, 7, :], AF.Sigmoid, [df], [dsgx])
                for fc in range(2):
                    pt, dp = self.psr.nxt()
                    self.mm(pt[:], gup[:, fc * 128:(fc + 1) * 128], sgx[:], True, True, [dwl, dsgx], [dp])
                    go, dgo = oR.nxt()
                    self.cp(fw.act, go[:], pt[:], [dp], [dgo])
                    fw.dma(fw.sp, ws["g"][fc * 128:(fc + 1) * 128, sl], go[:], reads=[dgo], q="xo")

    def rwkv(self, st0, l):
        fw = self.fw
        with contextlib.ExitStack() as st1, self.nc.named_scope(f"rwpre{l}"):
            self.rwkv_pre(st1, l)
        fw.barrier()
        with contextlib.ExitStack() as st, self.nc.named_scope(f"rwscan{l}"):
            self.rwkv_scan(st, l)

    def rwkv_scan(self, st, l):
        fw = self.fw
        nc = self.nc
        W = self.W
        ws = self.ws
        SB = 256
        NP = 2
        NCH = 4
        C0 = -0.6065306597126334
        GN_EPS = 64e-5
        ident = self.ident
        id64 = self.cst[0:64, 0, 0:64]
        c64 = self.cst[0:64, 2, 0:64]
        hk = lambda ap: ap.rearrange("(h k) -> k h", k=64)
        hkt = lambda ap: ap.rearrange("(h k) t -> k h t", k=64)
        flat2 = lambda ap: ap.rearrange("p a b -> p (a b)")
        mX = [self.cst[:, 4, :], self.cst[:, 8, :]]
        mXt = [self.cst[:, 8, :], self.cst[:, 4, :]]
        mD = [flat2(self.cst[0:64, 9:11, :])[:, 0:192], flat2(self.cst[0:64, 12:14, :])[:, 0:192]]
        pr = fw.sb([64, 8], stack=st)
        dpr = Dep()
        self.vec_load(pr[:, 0:4], hk(W["rwkv_ln_g"][l]), dpr)
        self.vec_load(pr[:, 4:8], hk(W["rwkv_ln_b"][l]), dpr)
        smask = [fw.sb([64, 4, SB], stack=st) for _ in range(2)]
        dsm = Dep()
        for d in range(2):
            self.memset(fw.pool, smask[d][:], 1.0, [dsm])
            z0 = 0 if d == 0 else 63
            self.memset(fw.pool, smask[d][:].rearrange("k h (c t) -> k (h c) t", t=64)[:, :, z0:z0 + 1], 0.0, [dsm])
        tsm = Ring([fw.sb([64, SB], stack=st) for _ in range(6)])

        class Set:
            pass
        sets = []
        for _ in range(2):
            S_ = Set()
            S_.B = [fw.sb([64, 4, SB], stack=st) for _ in range(9)]
            S_.dB = [Dep() for _ in range(9)]
            S_.KRf = fw.sb([64, 4, NP, 2, 128], BF16, stack=st)
            S_.dKR = Dep()
            S_.Bfb, S_.Kfb, S_.Vb = [fw.sb([64, 4, SB], BF16, stack=st) for _ in range(3)]
            S_.dBfb, S_.dKfb, S_.dVb = Dep(), Dep(), Dep()
            sets.append(S_)
        Tst = fw.sb([64, 4, 64], stack=st)
        dT = [Dep() for _ in range(4)]
        Tb = fw.sb([64, 4, 64], BF16, stack=st)
        dTb = [Dep() for _ in range(4)]
        shb = fw.sb([128, 64], BF16, stack=st)
        twr = Ring([fw.sb([64, 64], stack=st) for _ in range(8)])
        NPI = 4 * NP
        NCI = 4 * NCH
        tok3 = [fw.sb([64, 192], BF16, stack=st) for _ in range(NCI)]
        Ach = [fw.sb([64, 192], BF16, stack=st) for _ in range(NCI)]
        dch = [Dep() for _ in range(NCI)]
        dAch = [Dep() for _ in range(NCI)]
        MT = [fw.sb([128, 128], BF16, stack=st) for _ in range(NPI)]
        MT1 = [fw.sb([64, 64], BF16, stack=st) for _ in range(NPI)]
        dsl = [Dep() for _ in range(NPI)]
        Xb = [[fw.sb([128, 128], BF16, stack=st) for _ in range(2)] for _ in range(NPI)]
        Xtb = [[fw.sb([128, 128], BF16, stack=st) for _ in range(2)] for _ in range(NPI)]
        Accb = [[fw.sb([128, 128], BF16, stack=st) for _ in range(2)] for _ in range(NPI)]
        dAcc = [[Dep(), Dep()] for _ in range(NPI)]
        identb = fw.sb([128, 128], BF16, stack=st)
        self.cp(fw.dve, identb[:], ident, [self.dcst], [dsm])
        self.cp(fw.dve, shb[:], self.cst[:, 11, 0:64], [self.dcst], [dsm])
        dXb = [[Dep(), Dep()] for _ in range(NPI)]
        dXtb = [[Dep(), Dep()] for _ in range(NPI)]
        gsr = Ring([fw.sb([64, 64], BF16, stack=st) for _ in range(8)])
        usr = Ring([fw.sb([64, 64], BF16, stack=st) for _ in range(8)])
        obuf = fw.sb([64, 4, SB], BF16, stack=st)
        dob = Dep()
        dys = {}

        def elementwise(job, Z):
            tok0, S, d, sbi = job
            B, dB = Z.B, Z.dB
            KRf, dKR = Z.KRf, Z.dKR
            g0 = tok0 + sbi * SB
            sl = slice(g0, g0 + SB)
            for bi, nm in ((0, "r"), (1, "kap"), (2, "v"), (3, "kd%d" % d), (4, "sg%d" % d), (7, "b%d" % d)):
                fw.dma(fw.sp, B[bi][:], hkt(ws[nm][:, sl]), writes=[dB[bi]], q="x")
            yield
            for _ in range(20):
                yield
            sg, dsg = B[4], dB[4]
            Ls, dLs = B[5], dB[5]
            flat = lambda t: t[:].rearrange("k h t -> k (h t)")
            rvf = (lambda ap: ap) if d == 0 else (lambda ap: ap[:, ::-1])
            self.fw.op(fw.dve, lambda: nc.vector.tensor_tensor_scan(out=rvf(flat(Ls)), data0=rvf(flat(smask[d])), data1=rvf(flat(sg)), initial=0.0, op0=ALU.mult, op1=ALU.add), [dsm, dsg], [dLs])
            yield
            self.tt(sg[:], Ls[:], sg[:], ALU.subtract, [dLs, dsg], [dsg], E=fw.pool)
            self.act(sg[:], sg[:], AF.Exp, [dsg], [dsg], scale=C0)
            yield
            Ep, dEp = B[6], dB[6]
            self.act(Ep[:], Ls[:], AF.Exp, [dLs], [dEp], scale=C0)
            self.act(Ls[:], Ls[:], AF.Exp, [dLs], [dLs], scale=-C0)
            yield
            v4 = lambda t: t[:].rearrange("k h (p t) -> k h p t", p=NP)
            self.tt(KRf[:, :, :, 0, :], v4(B[1]), v4(sg), ALU.mult, [dB[1], dsg], [dKR], E=fw.pool)
            yield
            self.tt(KRf[:, :, :, 1, :], v4(B[0]), v4(Ep), ALU.mult, [dB[0], dEp], [dKR], E=fw.pool)
            yield
            self.tt(Z.Bfb[:], B[7][:], Ls[:], ALU.mult, [dB[7], dLs], [Z.dBfb], E=fw.pool)
            yield
            self.tt(Z.Kfb[:], B[3][:], Ls[:], ALU.mult, [dB[3], dLs], [Z.dKfb], E=fw.pool)
            self.cp(fw.pool, Z.Vb[:], B[2][:], [dB[2]], [Z.dVb])
            yield

        def matrices(job, Z, pull, first_of_dir, diag=False):
            tok0, S, d, sbi = job
            nsb = S // SB
            B, dB = Z.B, Z.dB
            KRf, dKR = Z.KRf, Z.dKR
            Bf, dBf = Z.Bfb, Z.dBfb
            Kf, dKf = Z.Kfb, Z.dKfb
            Vf, dVf = Z.Vb, Z.dVb
            Ep, dEp = B[6], dB[6]
            yb, dyb = B[8], dB[8]
            t0 = sbi * SB
            gt0 = tok0 + t0
            if first_of_dir:
                for h in range(4):
                    self.memset(fw.pool, Tst[:, h, :], 0.0, [dT[h]])
                    self.memset(fw.pool, Tb[:, h, :], 0.0, [dTb[h]])
            if d == 1:
                fw.dma(fw.sp, B[5][:], hkt(self.ys[:, gt0:gt0 + SB]), reads=[dys[gt0]], writes=[dB[5]], q="x")
            scope = (lambda nm: nc.named_scope(nm)) if diag else (lambda nm: contextlib.nullcontext())
            sc_ = scope("rwA_chunk"); sc_.__enter__()
            for h in range(4):
                for c in range(NCH):
                    ci = h * NCH + c
                    cs_ = slice(c * 64, (c + 1) * 64)
                    pt, dp = self.psr.nxt()
                    for j, (src, dsrc) in enumerate(((Bf, dBf), (Kf, dKf), (Vf, dVf))):
                        self.fw.op(fw.pe, lambda j=j, src=src: nc.tensor.transpose(pt[0:64, 0:96].bitcast(BF16)[:, j * 64:(j + 1) * 64], src[:, h, cs_], identb[0:64, 0:64]), [dsrc, dsm], [dp])
                    self.cp(fw.act, tok3[ci][:], pt[0:64, 0:96].bitcast(BF16), [dp], [dch[ci]])
                    pull(1)
            for h in range(4):
                for c in range(NCH):
                    ci = h * NCH + c
                    p, e = c // 2, c % 2
                    cs_ = slice(c * 64, (c + 1) * 64)
                    rhat = KRf[:, h, p, 1, e * 64:(e + 1) * 64]
                    pc, dpc = self.psr.nxt()
                    self.mm(pc[0:64, 0:64], Bf[:, h, cs_], rhat, True, True, [dBf, dKR], [dpc])
                    self.mm(pc[0:64, 64:192].rearrange("p (a t) -> p a t", a=2), Kf[:, h, cs_], KRf[:, h, p, :, e * 64:(e + 1) * 64], True, True, [dKf, dKR], [dpc])
                    self.tt(Ach[ci][:], pc[0:64, 0:192], mD[d], ALU.mult, [dpc, self.dcst], [dAch[ci]])
            sc_.__exit__(None, None, None); sc_ = scope("rwB_dbl"); sc_.__enter__()
            cur = [0] * NPI
            for h in range(4):
                for p in range(NP):
                    pi = h * NP + p
                    ps_ = slice(p * 128, (p + 1) * 128)
                    p1, dp1 = self.psr.nxt()
                    self.mm(p1[:, 0:128], Bf[:, h, ps_], KRf[:, h, p, 0, :], True, True, [dBf, dKR], [dp1])
                    self.tt(Xb[pi][0][:], p1[:, 0:128], mX[d], ALU.mult, [dp1, self.dcst], [dXb[pi][0]])
                    p3, dp3 = self.psr.nxt()
                    self.mm(p3[:, 0:128], KRf[:, h, p, 0, :], Bf[:, h, ps_], True, True, [dBf, dKR], [dp3])
                    self.tt(Xtb[pi][0][:], p3[:, 0:128], mXt[d], ALU.mult, [dp3, self.dcst], [dXtb[pi][0]])
                    self.tt(Accb[pi][0][:], Xb[pi][0][:], ident, ALU.add, [dXb[pi][0], self.dcst], [dAcc[pi][0]])
                    pull(1)
            acur = [0] * NPI
            for lev in range(1, 6):
                for pi in range(NPI):
                    k = cur[pi]
                    X, dX, Xt, dXt = Xb[pi][k], dXb[pi][k], Xtb[pi][k], dXtb[pi][k]
                    pxt, dpxt = self.psr.nxt()
                    self.mm(pxt[:, 0:128], X[:], Xt[:], True, True, [dX, dXt], [dpxt])
                    self.cp(fw.act if pi % 2 else fw.dve, Xtb[pi][1 - k][:], pxt[:, 0:128], [dpxt], [dXtb[pi][1 - k]])
                    if lev < 5:
                        px, dpx = self.psr.nxt()
                        self.mm(px[:, 0:128], Xt[:], X[:], True, True, [dX, dXt], [dpx])
                        self.cp(fw.dve if pi % 2 else fw.act, Xb[pi][1 - k][:], px[:, 0:128], [dpx], [dXb[pi][1 - k]])
                    cur[pi] = 1 - k
                    pull(1)
                for pi in range(NPI):
                    k = cur[pi]
                    ak = acur[pi]
                    pa, dpa = self.psr.nxt()
                    self.mm(pa[:, 0:128], identb[:], Accb[pi][ak][:], True, False, [dsm, dAcc[pi][ak]], [dpa])
                    self.mm(pa[:, 0:128], Xtb[pi][k][:], Accb[pi][ak][:], False, True, [dXtb[pi][k], dAcc[pi][ak]], [dpa])
                    if lev < 5:
                        self.cp(fw.act if (pi + lev) % 2 else fw.dve, Accb[pi][1 - ak][:], pa[:, 0:128], [dpa], [dAcc[pi][1 - ak]])
                        acur[pi] = 1 - ak
                    else:
                        self.cp(fw.act if (pi + lev) % 2 else fw.dve, MT[pi][:], pa[:, 0:128], [dpa], [dsl[pi]])
                    pull(1)
            for pi in range(NPI):
                psh, dpsh = self.psr.nxt()
                self.mm(psh[0:64, 0:64], shb[:], MT[pi][:, 64:128], True, True, [dsm, dsl[pi]], [dpsh])
                self.cp(fw.act, MT1[pi][:], psh[0:64, 0:64], [dpsh], [dsl[pi]])
            pull(2)
            sc_.__exit__(None, None, None); sc_ = scope("rwC_chain"); sc_.__enter__()
            for c in (range(NCH) if d == 0 else range(NCH - 1, -1, -1)):
                p, e = c // 2, c % 2
                cs = slice(c * 64, (c + 1) * 64)
                wcol = c * 64 + 63 if d == 0 else c * 64
                Gs, Us = [], []
                for h in range(4):
                    ci = h * NCH + c
                    khat = KRf[:, h, p, 0, e * 64:(e + 1) * 64]
                    pgc, dpg = self.psr.nxt()
                    self.mm(pgc[0:64, 0:64], khat, Tb[:, h, :], True, False, [dKR, dTb[h]], [dpg])
                    self.mm(pgc[0:64, 0:64], Ach[ci][:, 64:128], tok3[ci][:, 128:192], False, True, [dch[ci], dAch[ci]], [dpg])
                    G, dG = gsr.nxt()
                    self.aff(G[:], pgc[0:64, 0:64], -1.0, None, [dpg], [dG])
                    Gs.append((G, dG))
                pull(2)
                for h in range(4):
                    pi = h * NP + p
                    G, dG = Gs[h]
                    MTc = MT[pi][0:64, 0:64] if e == 0 else MT1[pi][:]
                    pu, dpu = self.psr.nxt()
                    self.mm(pu[0:64, 0:64], MTc, G[:], True, True, [dsl[pi], dG], [dpu])
                    U, dU = usr.nxt()
                    self.cp(fw.act, U[:], pu[0:64, 0:64], [dpu], [dU])
                    Us.append((U, dU))
                pull(2)
                for h in range(4):
                    ci = h * NCH + c
                    U, dU = Us[h]
                    rhat = KRf[:, h, p, 1, e * 64:(e + 1) * 64]
                    Btok, Ktok, Vtok = tok3[ci][:, 0:64], tok3[ci][:, 64:128], tok3[ci][:, 128:192]
                    ArbT, ArkT = Ach[ci][:, 0:64], Ach[ci][:, 128:192]
                    py, dpy = self.psr.nxt()
                    self.mm(py[0:64, 0:64], Tb[:, h, :], rhat, True, False, [dTb[h], dKR], [dpy])
                    self.mm(py[0:64, 0:64], U[:], ArbT, False, False, [dU, dAch[ci]], [dpy])
                    self.mm(py[0:64, 0:64], Vtok, ArkT, False, True, [dch[ci], dAch[ci]], [dpy])
                    self.cp(fw.dve, yb[:, h, cs], py[0:64, 0:64], [dpy], [dyb])
                    ptn, dptn = self.psr.nxt()
                    self.mm(ptn[0:64, 0:64], Btok, U[:], True, False, [dch[ci], dU], [dptn])
                    self.mm(ptn[0:64, 0:64], Ktok, Vtok, False, True, [dch[ci]], [dptn])
                    TW, dTW = twr.nxt()
                    self.aff(TW[:], Tst[:, h, :], Ep[:, h, wcol:wcol + 1], None, [dT[h], dEp], [dTW])
                    self.stt(Tb[:, h, :], ptn[0:64, 0:64], Ep[:, h, wcol:wcol + 1], TW[:], ALU.mult, ALU.add, [dptn, dEp, dTW], [dTb[h]])
                    self.stt(Tst[:, h, :], ptn[0:64, 0:64], Ep[:, h, wcol:wcol + 1], TW[:], ALU.mult, ALU.add, [dptn, dEp, dTW], [dT[h]])
                pull(2)
            sc_.__exit__(None, None, None)
            if S == 2 * SEG and ((d == 0 and sbi == nsb // 2 - 1) or (d == 1 and sbi == nsb // 2)):
                for h in range(4):
                    self.ts(Tst[:, h, :], Tst[:, h, :], self.link[0:64, 0:1], None, ALU.mult, None, [dT[h], self.dlink], [dT[h]])
                    self.ts(Tb[:, h, :], Tb[:, h, :], self.link[0:64, 0:1], None, ALU.mult, None, [dTb[h], self.dlink], [dTb[h]])
            if d == 0:
                dys[gt0] = Dep()
                fw.dma(fw.sp, hkt(self.ys[:, gt0:gt0 + SB]), yb[:], reads=[dyb], writes=[dys[gt0]], q="xo")
                return
            yf, dyf = B[5], dB[5]
            bon, dbon = B[0], dB[0]
            gg, dgg = B[1], dB[1]
            fw.dma(fw.sp, bon[:], hkt(ws["bon"][:, gt0:gt0 + SB]), writes=[dbon], q="x")
            fw.dma(fw.sp, gg[:], hkt(ws["g"][:, gt0:gt0 + SB]), writes=[dgg], q="x")
            self.tt(yf[:], yf[:], yb[:], ALU.add, [dyf, dyb], [dyf])
            for h in range(4):
                pm, dpm = self.psr.nxt()
                self.mm(pm[0:64, 0:SB], c64, yf[:, h, :], True, True, [dyf, self.dcst], [dpm])
                yc, dyc = tsm.nxt()
                self.tt(yc[:], yf[:, h, :], pm[0:64, 0:SB], ALU.subtract, [dyf, dpm], [dyc])
                sq, dsq = tsm.nxt()
                self.act(sq[:], yc[:], AF.Square, [dyc], [dsq])
                pv_, dpv_ = self.psr.nxt()
                self.mm(pv_[0:64, 0:SB], c64, sq[:], True, True, [dsq, self.dcst], [dpv_])
                self.act(sq[:], pv_[0:64, 0:SB], AF.Ln, [dpv_], [dsq], bias=GN_EPS)
                self.act(sq[:], sq[:], AF.Exp, [dsq], [dsq], scale=-0.5)
                self.stt(yc[:], yc[:], pr[:, h:h + 1], sq[:], ALU.mult, ALU.mult, [dyc, dsq, dpr], [dyc])
                self.stt(yc[:], yc[:], pr[:, 4 + h:5 + h], bon[:, h, :], ALU.add, ALU.add, [dyc, dbon, dpr], [dyc])
                self.tt(obuf[:, h, :], yc[:], gg[:, h, :], ALU.mult, [dyc, dgg], [dob])
                pull(1)
            fw.dma(fw.sp, hkt(self.os[384:640, gt0:gt0 + SB]), obuf[:], reads=[dob], q="xo")

        jobs = []
        for (tok0, nseg) in self.UNITS:
            S = nseg * SEG
            nsb = S // SB
            for d in range(2):
                order = range(nsb) if d == 0 else range(nsb - 1, -1, -1)
                for i, sbi in enumerate(order):
                    jobs.append(((tok0, S, d, sbi), i == 0))
        g0_ = elementwise(jobs[0][0], sets[0])
        for _ in g0_:
            pass
        for n, (job, first) in enumerate(jobs):
            nxt = elementwise(jobs[n + 1][0], sets[(n + 1) % 2]) if n + 1 < len(jobs) else iter(())

            def pull(k, g=nxt):
                for _ in range(k):
                    try:
                        next(g)
                    except StopIteration:
                        return
            matrices(job, sets[n % 2], pull, first, diag=(n == 5 and (self.debug or {}).get("diag")))
            for _ in nxt:
                pass


WSHAPES = {
    "w_ada": (2, 1024, 9216), "b_ada": (2, 9216), "norm_g": (2, 3, 1024),
    "ffn_w_in": (2, 2, 1024, 5632), "ffn_w_out": (2, 2, 2816, 1024),
    "w_mix_in": (2, 1024, 2432), "w_mix_out": (2, 1024, 1024),
    "lru_conv_w": (2, 4, 384), "lru_conv_b": (2, 384),
    "lru_w_gate_a": (2, 2, 6, 64, 64), "lru_b_gate_a": (2, 2, 384),
    "lru_w_gate_x": (2, 2, 6, 64, 64), "lru_b_gate_x": (2, 2, 384), "lru_lambda": (2, 2, 384),
    "rwkv_mu": (2, 1024), "rwkv_w_up": (2, 2, 64, 256), "rwkv_w0": (2, 2, 256),
    "rwkv_a_up": (2, 64, 256), "rwkv_a0": (2, 2, 256), "rwkv_g_up": (2, 128, 256),
    "rwkv_k_k": (2, 256), "rwkv_k_a": (2, 256), "rwkv_r_k": (2, 4, 64),
    "rwkv_ln_g": (2, 256), "rwkv_ln_b": (2, 256), "attn_q_norm": (2, 64), "attn_k_norm": (2, 64),
}


def rope_tables(positions):
    inv = (np.float32(10000.0) ** (-(np.arange(16, dtype=np.float32)) / np.float32(16))).astype(np.float32)
    row = (positions // 64).astype(np.float32)
    col = (positions % 64).astype(np.float32)
    cos = np.zeros((128, positions.shape[0]), np.float32)
    sin = np.zeros((128, positions.shape[0]), np.float32)
    for p in range(128):
        d = p % 64
        base = row if d < 32 else col
        ang = (base * inv[d % 16]).astype(np.float32)
        cos[p] = np.cos(ang)
        sin[p] = np.sin(ang)
    return cos, sin


def make_consts():
    c = np.zeros((14, 128, 128), np.float32)
    ii = np.arange(128)
    same = (ii[:, None] // 64) == (ii[None, :] // 64)
    su = (same & (ii[:, None] < ii[None, :])).astype(np.float32)
    iu = (same & (ii[:, None] <= ii[None, :])).astype(np.float32)
    c[4] = -su
    c[5] = iu
    c[6] = su
    c[7] = iu
    c[8] = -su.T
    c[9, :64, 0:64] = iu[:64, :64]
    c[9, :64, 64:128] = su[:64, :64]
    c[10, :64, 0:64] = iu[:64, :64]
    for i in range(64):
        c[11, 64 + i, i] = 1.0
    c[12, :64, 0:64] = iu[:64, :64].T
    c[12, :64, 64:128] = su[:64, :64].T
    c[13, :64, 0:64] = iu[:64, :64].T
    c[0] = np.eye(128, dtype=np.float32)
    R = np.zeros((128, 128), np.float32)
    for d in range(128):
        if d % 32 < 16:
            R[d, d + 16] = -1.0
        else:
            R[d, d - 16] = 1.0
    c[1] = R.T
    c[2, :64, :64] = 1.0 / 64
    c[2, 64:, 64:] = 1.0 / 64
    c[3] = 1.0
    return c


def core_layout(x_prompt, x_sample, c_prompt, c_sample):
    maps = []
    for core in range(NCORES):
        if core < 4:
            xs = [x_prompt[core, :SEG], x_prompt[core, SEG:], x_sample[core]]
            cs = [c_prompt[core], c_prompt[core], c_sample[core]]
            pos = np.concatenate([np.arange(2 * SEG), np.arange(SEG)])
            link = 1.0
        else:
            b = 4 + 3 * (core - 4)
            xs = [x_sample[b], x_sample[b + 1], x_sample[b + 2]]
            cs = [c_sample[b], c_sample[b + 1], c_sample[b + 2]]
            pos = np.concatenate([np.arange(SEG)] * 3)
            link = 0.0
        cos, sin = rope_tables(pos)
        maps.append({
            "x_in": np.ascontiguousarray(np.concatenate(xs, 0)),
            "c_in": np.ascontiguousarray(np.stack(cs, 0)),
            "link_in": np.full((128, 1), link, np.float32),
            "cos_in": cos, "sin_in": sin, "cst_in": make_consts(),
        })
    return maps


_NC_CACHE = {}


def kernel(**inputs):
    inputs = {k: np.asarray(v) for k, v in inputs.items()}
    if "nc" not in _NC_CACHE:
        _NC_CACHE["nc"] = K().build()
    nc = _NC_CACHE["nc"]
    maps = core_layout(inputs["x_prompt"], inputs["x_sample"], inputs["c_prompt"], inputs["c_sample"])
    for m in maps:
        for name in WSHAPES:
            m[name] = np.ascontiguousarray(inputs[name], dtype=np.float32)
    res = run_bass_kernel_spmd(nc, maps, core_ids=list(range(NCORES)))
    y_prompt = np.zeros((4, 2 * SEG, D), np.float32)
    y_sample = np.zeros((16, SEG, D), np.float32)
    for core in range(NCORES):
        y = res.results[core]["y_out"]
        if core < 4:
            y_prompt[core, :SEG] = y[:SEG]
            y_prompt[core, SEG:] = y[SEG:2 * SEG]
            y_sample[core] = y[2 * SEG:]
        else:
            b = 4 + 3 * (core - 4)
            for j in range(3):
                y_sample[b + j] = y[j * SEG:(j + 1) * SEG]
    return (y_prompt, y_sample)
```
